# Optimizing a Trainium2 kernel written in Bass

```python
import jax, jax.numpy as jnp
from jax import lax
import numpy as np

D_MODEL = 1024
BATCH = 2
SEQ = 16384
DEPTH = 1

N_META = 16
N_HEADS = 8
HEAD_DIM = 128
ATTN_WIDTH = N_HEADS * HEAD_DIM
N_IDX_HEADS = 8
IDX_DIM = 64
TOPK_MAX = 256
CONV_WIDTH = D_MODEL
CONV_K = 3
D_FF = ((8 * D_MODEL // 3 + 255) // 256) * 256
Q_BLOCK = 128
EPS = 1e-6
IDX_SCALE = (N_IDX_HEADS ** -0.5) * (IDX_DIM ** -0.5)

_PROJ_SIZES = [ATTN_WIDTH, ATTN_WIDTH, ATTN_WIDTH,
               N_IDX_HEADS * IDX_DIM, IDX_DIM, N_IDX_HEADS,
               CONV_WIDTH, CONV_WIDTH, CONV_WIDTH,
               D_MODEL, D_MODEL]
PROJ_WIDTH = int(sum(_PROJ_SIZES))
PROJ_SPLITS = [int(s) for s in np.cumsum(_PROJ_SIZES)[:-1]]

kernel_name = "hybrid_dsa_shortconv_gated_block"


def rmsnorm(x, g):
    xf = x.astype(jnp.float32)
    y = xf * lax.rsqrt(jnp.mean(xf * xf, axis=-1, keepdims=True) + EPS)
    return (y * g.astype(jnp.float32)).astype(x.dtype)


def dsa_attention(q, k, v, q_idx, k_idx, w_idx):
    b, l = q.shape[0], q.shape[1]
    topk = min(TOPK_MAX, l // 4)
    n_blk = -(-l // Q_BLOCK)
    lp = n_blk * Q_BLOCK

    def pad(a):
        return jnp.pad(a, [(0, 0), (0, lp - l)] + [(0, 0)] * (a.ndim - 2))

    q, k, v, q_idx, k_idx, w_idx = (pad(a) for a in (q, k, v, q_idx, k_idx, w_idx))
    key_pos = jnp.arange(lp)

    def to_blocks(a):
        return jnp.moveaxis(a.reshape((b, n_blk, Q_BLOCK) + a.shape[2:]), 1, 0)

    def one_block(args):
        q_b, qi_b, w_b, start = args
        q_pos = start + jnp.arange(Q_BLOCK)
        visible = key_pos[None, :] <= q_pos[:, None]
        rel = jax.nn.relu(jnp.einsum('bthd,bsd->bths', qi_b, k_idx).astype(jnp.float32))
        score = jnp.einsum('bths,bth->bts', rel, w_b.astype(jnp.float32) * IDX_SCALE)
        score = jnp.where(visible[None], score, -jnp.inf)
        _, sel = lax.top_k(score, topk)
        k_sel = jax.vmap(lambda kb, ib: kb[ib])(k, sel)
        v_sel = jax.vmap(lambda vb, ib: vb[ib])(v, sel)
        valid = sel <= q_pos[None, :, None]
        logits = jnp.einsum('bthd,btkhd->bthk', q_b, k_sel).astype(jnp.float32) * (HEAD_DIM ** -0.5)
        logits = jnp.where(valid[:, :, None, :], logits, -jnp.inf)
        p = jax.nn.softmax(logits, axis=-1).astype(v_sel.dtype)
        return jnp.einsum('bthk,btkhd->bthd', p, v_sel)

    starts = jnp.arange(n_blk) * Q_BLOCK
    out = lax.map(one_block, (to_blocks(q), to_blocks(q_idx), to_blocks(w_idx), starts))
    out = jnp.moveaxis(out, 0, 1).reshape(b, lp, N_HEADS, HEAD_DIM)
    return out[:, :l]


def short_conv(u, w):
    return lax.conv_general_dilated(
        u, w[:, None, :].astype(u.dtype), window_strides=(1,), padding=[(CONV_K - 1, 0)],
        dimension_numbers=('NWC', 'WIO', 'NWC'), feature_group_count=u.shape[-1])


def setup_inputs(seed: int = 0) -> dict:
    key = jax.random.key(seed)
    ks = jax.random.split(key, 16)
    f32 = jnp.float32
    nrm = lambda k, shape, scale: (jax.random.normal(k, shape, f32) * scale)
    gain = lambda k: 1.0 + 0.02 * jax.random.normal(k, (DEPTH, D_MODEL), f32)
    return {
        "x": nrm(ks[0], (BATCH, SEQ, D_MODEL), 1.0),
        "meta_tokens": nrm(ks[1], (N_META, D_MODEL), 1.0),
        "norm_mix_g": gain(ks[2]),
        "w_in": nrm(ks[3], (DEPTH, D_MODEL, PROJ_WIDTH), D_MODEL ** -0.5),
        "w_attn_out": nrm(ks[4], (DEPTH, ATTN_WIDTH, D_MODEL), ATTN_WIDTH ** -0.5),
        "conv_w": nrm(ks[5], (DEPTH, CONV_K, CONV_WIDTH), CONV_K ** -0.5),
        "w_conv_out": nrm(ks[6], (DEPTH, CONV_WIDTH, D_MODEL), CONV_WIDTH ** -0.5),
        "w_out": nrm(ks[7], (DEPTH, D_MODEL, D_MODEL), D_MODEL ** -0.5),
        "norm_ffn_g": gain(ks[8]),
        "w_gate": nrm(ks[9], (DEPTH, D_MODEL, D_FF), D_MODEL ** -0.5),
        "w_up": nrm(ks[10], (DEPTH, D_MODEL, D_FF), D_MODEL ** -0.5),
        "w_down": nrm(ks[11], (DEPTH, D_FF, D_MODEL), D_FF ** -0.5),
        "norm_final_g": 1.0 + 0.02 * jax.random.normal(ks[12], (D_MODEL,), f32),
    }


def reference(x, meta_tokens, norm_mix_g, w_in, w_attn_out, conv_w, w_conv_out, w_out,
              norm_ffn_g, w_gate, w_up, w_down, norm_final_g):
    b = x.shape[0]
    meta = jnp.broadcast_to(meta_tokens[None].astype(x.dtype), (b, N_META, x.shape[-1]))
    h = jnp.concatenate([meta, x], axis=1)
    l = h.shape[1]
    for i in range(DEPTH):
        a = rmsnorm(h, norm_mix_g[i])
        proj = a @ w_in[i]
        q, k, v, qi, ki, wi, cu, cb, cc, ga, gb = jnp.split(proj, PROJ_SPLITS, axis=-1)
        y_attn = dsa_attention(q.reshape(b, l, N_HEADS, HEAD_DIM),
                               k.reshape(b, l, N_HEADS, HEAD_DIM),
                               v.reshape(b, l, N_HEADS, HEAD_DIM),
                               qi.reshape(b, l, N_IDX_HEADS, IDX_DIM), ki, wi)
        y_attn = y_attn.reshape(b, l, ATTN_WIDTH) @ w_attn_out[i]
        y_conv = (cb * short_conv(cc * cu, conv_w[i])) @ w_conv_out[i]
        mixed = jax.nn.sigmoid(ga) * y_attn + jax.nn.sigmoid(gb) * y_conv
        h = h + mixed @ w_out[i]
        f = rmsnorm(h, norm_ffn_g[i])
        h = h + (jax.nn.silu(f @ w_gate[i]) * (f @ w_up[i])) @ w_down[i]
    out = rmsnorm(h, norm_final_g)
    return out[:, N_META:]
```

```python
import numpy as np
from contextlib import ExitStack
import concourse.bass as bass
import concourse.mybir as mybir
from concourse.bass_utils import run_bass_kernel_spmd

F32 = mybir.dt.float32
BF16 = mybir.dt.bfloat16
U8 = mybir.dt.uint8
AF = mybir.ActivationFunctionType
ALU = mybir.AluOpType
AX = mybir.AxisListType

D = 1024
KC = 8
NH = 8
DFF = 2816
FC = DFF // 128
PROJ = 8776
C_Q, C_K, C_V, C_QI, C_KI, C_WI, C_CU, C_CB, C_CC, C_GA, C_GB = 0, 1024, 2048, 3072, 3584, 3648, 3656, 4680, 5704, 6728, 7752
EPS = 1e-6
IDX_SCALE = (8 ** -0.5) * (64 ** -0.5)
SM_SCALE = 128 ** -0.5
NEG = -30000.0
NBIS = 18
TOPK = 256.0
ACT_COLS = 0
B_HOLD = 0
A_HOLD = 18


class Buf:
    def __init__(self, name, t, const=False):
        self.name, self.t, self.const = name, t, const
        self.lw = None
        self.rd = []

    def __getitem__(self, k):
        return self.t[k]


class Op:
    __slots__ = ("eng", "fn", "deps", "dma", "key", "signal", "tok", "idx")

    def __init__(self, eng, fn, deps, dma, key):
        self.eng, self.fn, self.deps, self.dma, self.key = eng, fn, deps, dma, key
        self.signal = False
        self.tok = None


class Prog:
    ENGS = ["pe", "act", "dve", "pool", "sp"]

    def __init__(self, nc):
        self.nc = nc
        self.ops = []
        self.last = {e: None for e in self.ENGS}
        self.dma_since = {}

    def op(self, eng, fn, r=(), w=(), dma=False, key=None):
        deps = []
        for b in r:
            if b.lw is not None:
                deps.append(b.lw)
        for b in w:
            if b.lw is not None:
                deps.append(b.lw)
            lastr = {}
            for d in b.rd:
                if d.dma:
                    deps.append(d)
                else:
                    lastr[d.eng] = d
            deps.extend(lastr.values())
        o = Op(eng, fn, None, dma, key)
        dd = []
        seen = set()
        for d in deps:
            if id(d) in seen:
                continue
            seen.add(id(d))
            if d.eng == "pe" and eng == "pe" and not d.dma and not dma:
                continue
            dd.append(d)
        o.deps = dd
        for b in r:
            if not b.const:
                b.rd.append(o)
        for b in w:
            b.lw = o
            b.rd = []
        self.ops.append(o)
        self.last[eng] = o
        if dma:
            self.dma_since[key] = o
        return o

    def dma(self, eng, fn, r=(), w=(), key=None):
        if key is None:
            key = w[0].name
        return self.op(eng, fn, r, w, dma=True, key=key)

    def barrier(self):
        lasts = [o for o in self.last.values() if o is not None] + list(self.dma_since.values())
        self.dma_since = {}
        new = []
        for e in self.ENGS:
            o = Op(e, (lambda en: en.nop()), [d for d in lasts], False, None)
            new.append(o)
        for o in new:
            self.ops.append(o)
            self.last[o.eng] = o

    def lower(self, es):
        nc = self.nc
        for o in self.ops:
            for d in o.deps:
                d.signal = True
        EPOCH = 12000
        eng_sem = {}
        eng_cnt = {}
        dma_sem = {}
        dma_cnt = {}

        def new_sem(nm):
            return es.enter_context(nc.semaphore(nm))

        nsem = [0]
        for o in self.ops:
            if o.dma:
                if o.key not in dma_sem or dma_cnt[o.key] > 28000:
                    nsem[0] += 1
                    dma_sem[o.key] = new_sem("d%d" % nsem[0])
                    dma_cnt[o.key] = 0
                o.tok = [dma_sem[o.key], dma_cnt[o.key]]
            elif o.signal:
                if o.eng not in eng_sem or eng_cnt[o.eng] >= EPOCH:
                    nsem[0] += 1
                    eng_sem[o.eng] = new_sem("e%d" % nsem[0])
                    eng_cnt[o.eng] = 0
                eng_cnt[o.eng] += 1
                o.tok = [eng_sem[o.eng], eng_cnt[o.eng]]
            if o.dma:
                n = getattr(o.fn, "ndma", 1)
                dma_cnt[o.key] += 16 * n
                o.tok = [dma_sem[o.key], dma_cnt[o.key]]
        per = {e: [o for o in self.ops if o.eng == e] for e in self.ENGS}
        block = es.enter_context(nc.Block())

        def emit(e, lst):
            seen = {}
            for o in lst:
                for d in o.deps:
                    s, v = d.tok
                    k = id(s)
                    if seen.get(k, 0) < v:
                        e.wait_ge(s, v)
                        seen[k] = v
                ins = o.fn(e)
                if o.dma:
                    if not isinstance(ins, (list, tuple)):
                        ins = [ins]
                    assert len(ins) == getattr(o.fn, "ndma", 1)
                    for i in ins:
                        i.then_inc(o.tok[0], 16)
                elif o.signal:
                    if isinstance(ins, (list, tuple)):
                        ins = ins[-1]
                    ins.then_inc(o.tok[0], 1)

        @block.tensor
        def _(e):
            emit(e, per["pe"])

        @block.scalar
        def _(e):
            emit(e, per["act"])

        @block.vector
        def _(e):
            emit(e, per["dve"])

        @block.gpsimd
        def _(e):
            emit(e, per["pool"])

        @block.sync
        def _(e):
            emit(e, per["sp"])


def multi(n, f):
    f.ndma = n
    return f


def build(SEQ, dbg=False):
    NXC = SEQ // 128
    NCH = NXC + 1
    NSLOT = NXC // 4
    TT = 4
    NSUP = NSLOT // TT
    SMAX = NCH * 128

    nc = bass.Bass("TRN2", target_bir_lowering=False)
    P = Prog(nc)

    def din(name, shape, dt=F32):
        return nc.dram_tensor(name, list(shape), dt, kind="ExternalInput").ap()

    x_all = din("x_all", [NCH * 128, D])
    x_own = din("x_own", [NSLOT, 128, D])
    x_halo = din("x_halo", [NSLOT * 2, D])
    cbx_in = din("cbx", [128, 512])
    cbm_in = din("cbm", [128, 128])
    w_in = din("w_in", [D, PROJ])
    w_ao = din("w_attn_out", [D, D])
    w_co = din("w_conv_out", [D, D])
    w_o = din("w_out", [D, D])
    w_g = din("w_gate", [D, DFF])
    w_u = din("w_up", [D, DFF])
    w_d = din("w_down", [DFF, D])
    gmix_in = din("g_mix", [128, KC])
    gffn_in = din("g_ffn", [128, KC])
    gfin_in = din("g_fin", [128, D])
    cw_in = din("conv_w", [128, KC * 3])
    ident_in = din("ident", [128, 128])
    out = nc.dram_tensor("out", [NSLOT, 128, D], F32, kind="ExternalOutput").ap()

    skind = "ExternalOutput" if dbg else "Internal"

    def dscr(name, shape, dt=BF16, kind=None):
        t = nc.dram_tensor(name, list(shape), dt, kind=kind or skind).ap()
        return Buf("dr_" + name, t)

    Wb_in = dscr("Wb_in", [128, KC, PROJ], kind="Internal")
    Wb_ao = dscr("Wb_ao", [128, KC, D], kind="Internal")
    Wb_co = dscr("Wb_co", [128, KC, D], kind="Internal")
    Wb_o = dscr("Wb_o", [128, KC, D], kind="Internal")
    Wb_g = dscr("Wb_g", [128, KC, DFF], kind="Internal")
    Wb_u = dscr("Wb_u", [128, KC, DFF], kind="Internal")
    Wb_d = dscr("Wb_d", [128, 2, FC, 512], kind="Internal")
    KT_d = dscr("KT_d", [NCH, 128, NH * 128])
    V_d = dscr("V_d", [NCH, 128, NH * 129])
    ya_d = dscr("ya_d", [NSLOT, 128, KC * 128])

    top = ExitStack()

    def sb(es, name, shape, dt, const=False):
        t = es.enter_context(nc.sbuf_tensor("s_" + name, list(shape), dt))
        return Buf(name, t, const)

    def ps(es, name, shape, dt=F32):
        t = es.enter_context(nc.psum_tensor("p_" + name, list(shape), dt))
        return Buf(name, t)

    ident_f = sb(top, "ident_f", [128, 128], F32)
    ident = sb(top, "ident", [128, 128], BF16)
    ident4 = sb(top, "ident4", [128, 4, 128], BF16)
    gmix = sb(top, "gmix", [128, KC], F32)
    gffn = sb(top, "gffn", [128, KC], F32)
    P.dma("sp", lambda e: e.dma_start(out=ident_f[:], in_=ident_in[:, :]), w=[ident_f])
    P.dma("sp", lambda e: e.dma_start(out=gmix[:], in_=gmix_in[:, :]), w=[gmix])
    P.dma("sp", lambda e: e.dma_start(out=gffn[:], in_=gffn_in[:, :]), w=[gffn])
    P.op("dve", lambda e: e.tensor_copy(out=ident[:], in_=ident_f[:]), r=[ident_f], w=[ident])
    for i in range(4):
        P.op("dve", lambda e, i=i: e.tensor_copy(out=ident4[:, i, :], in_=ident_f[:]), r=[ident_f], w=[ident4])

    Wb_in2 = Buf("dr_Wb_in2", Wb_in.t)
    epsb = sb(top, "epsb", [128, 1], F32)
    P.op("dve", lambda e: e.memset(epsb[:], EPS), w=[epsb])
    es_ki = ExitStack()
    kiT = sb(es_ki, "kiT", [128, SMAX], BF16)
    es_p0 = ExitStack()
    NST = 3
    CBLK = 2816
    stg = [sb(es_p0, "p0s%d" % i, [128, CBLK], F32) for i in range(NST)]
    stb = [sb(es_p0, "p0b%d" % i, [128, CBLK], BF16) for i in range(NST)]
    p0cnt = [0]

    def prep(src, K, c_lo, c_hi, dst, g, dst_t=None):
        for kc in range(K // 128):
            for c0 in range(c_lo, c_hi, CBLK):
                cw = min(CBLK, c_hi - c0)
                i = p0cnt[0] % NST
                p0cnt[0] += 1
                s_, b_ = stg[i], stb[i]
                P.dma("sp", lambda e, s_=s_, kc=kc, c0=c0, cw=cw, src=src: e.dma_start(
                    out=s_[:, 0:cw], in_=src[kc * 128:(kc + 1) * 128, c0:c0 + cw]), w=[s_])
                if g is None:
                    if p0cnt[0] % 2 == 0:
                        P.op("act", lambda e, s_=s_, b_=b_, cw=cw: e.activation(out=b_[:, 0:cw], in_=s_[:, 0:cw], func=AF.Copy),
                             r=[s_], w=[b_])
                    else:
                        P.op("dve", lambda e, s_=s_, b_=b_, cw=cw: e.tensor_copy(out=b_[:, 0:cw], in_=s_[:, 0:cw]),
                             r=[s_], w=[b_])
                else:
                    if p0cnt[0] % 2 == 0:
                        P.op("act", lambda e, s_=s_, b_=b_, cw=cw, g=g, kc=kc: e.activation(
                            out=b_[:, 0:cw], in_=s_[:, 0:cw], func=AF.Copy, scale=g[:, kc:kc + 1]), r=[s_, g], w=[b_])
                    else:
                        P.op("dve", lambda e, s_=s_, b_=b_, cw=cw, g=g, kc=kc: e.tensor_scalar(
                            out=b_[:, 0:cw], in0=s_[:, 0:cw], scalar1=g[:, kc:kc + 1], scalar2=None, op0=ALU.mult),
                            r=[s_, g], w=[b_])
                if dst is Wb_d:
                    P.dma("pool", lambda e, b_=b_, kc=kc, dst=dst: e.dma_start(
                        out=dst[:, :, kc, :], in_=b_[:, 0:D].rearrange("p (g c) -> p g c", c=512)), r=[b_], w=[dst], key=dst.name)
                else:
                    P.dma("pool", lambda e, b_=b_, kc=kc, c0=c0, cw=cw, dst=dst: e.dma_start(
                        out=dst[:, kc, c0:c0 + cw], in_=b_[:, 0:cw]), r=[b_], w=[dst], key=dst.name)
                yield

    def prep_rest():
        yield from prep(w_in, D, C_CU, PROJ, Wb_in2, gmix)
        yield from prep(w_ao, D, 0, D, Wb_ao, None)
        yield from prep(w_co, D, 0, D, Wb_co, None)
        yield from prep(w_o, D, 0, D, Wb_o, None)
        yield from prep(w_g, D, 0, DFF, Wb_g, gffn)
        yield from prep(w_u, D, 0, DFF, Wb_u, gffn)
        yield from prep(w_d, DFF, 0, D, Wb_d, None)

    for _ in prep(w_in, D, 0, C_CU, Wb_in, gmix):
        pass
    g_rest = prep_rest()

    def rms_transpose(es_bufs, x_src_ap, nrows, aT, col0, keep_x=None):
        rms_pre(es_bufs, x_src_ap, nrows)
        rms_tr(es_bufs, nrows, aT, col0)

    def rms_tr(es_bufs, nrows, aT, col0):
        xt, sq_junk, ss, rstd, a_bf, tp = es_bufs
        dfn = lambda hf: aT[:, hf * 4:(hf + 1) * 4, col0:col0 + nrows]
        dfn.buf = aT
        pe_transpose(a_bf, nrows, tp, dfn)

    def rms_pre(es_bufs, x_src_ap, nrows):
        xt, sq_junk, ss, rstd, a_bf, tp = es_bufs
        P.dma("sp", lambda e: e.dma_start(out=xt[0:nrows, :], in_=x_src_ap), w=[xt])
        P.op("act", lambda e: e.activation(out=sq_junk[0:nrows, :], in_=xt[0:nrows, :], func=AF.Square,
                                           accum_out=ss[0:nrows, 0:1]), r=[xt], w=[sq_junk, ss])
        P.op("act", lambda e: e.activation(out=rstd[0:nrows, 0:1], in_=ss[0:nrows, 0:1], func=AF.Sqrt,
                                           scale=1.0 / D, bias=epsb[0:nrows, 0:1]), r=[ss, epsb], w=[rstd])
        P.op("dve", lambda e: e.reciprocal(out=rstd[0:nrows, 0:1], in_=rstd[0:nrows, 0:1]), r=[rstd], w=[rstd])
        P.op("dve", lambda e: e.tensor_scalar(out=a_bf[0:nrows, :], in0=xt[0:nrows, :], scalar1=rstd[0:nrows, 0:1],
                                              scalar2=None, op0=ALU.mult), r=[xt, rstd], w=[a_bf])

    def pe_transpose(src, nrows, tp, dst_fn):
        for hf in range(2):
            t_ = tp[hf]
            for c4 in range(4):
                kc = hf * 4 + c4
                P.op("pe", lambda e, t_=t_, c4=c4, kc=kc: e.matmul(
                    t_[:, c4 * 128:c4 * 128 + nrows], lhsT=src[0:nrows, kc * 128:(kc + 1) * 128], rhs=ident[0:nrows, 0:nrows],
                    start=True, stop=True), r=[src, ident], w=[t_])
            tv = t_[:].rearrange("p (c t) -> p c t", t=128)
            if hf == 0:
                P.op("act", lambda e, tv=tv, hf=hf: e.activation(out=dst_fn(hf), in_=tv[:, :, 0:nrows], func=AF.Copy), r=[t_], w=[dst_fn.buf])
            else:
                P.op("dve", lambda e, tv=tv, hf=hf: e.tensor_copy(out=dst_fn(hf), in_=tv[:, :, 0:nrows]), r=[t_], w=[dst_fn.buf])


    def rms_bufs(es, tag, tp):
        return (sb(es, "xt" + tag, [128, D], F32), sb(es, "sqj" + tag, [128, D], BF16), sb(es, "ss" + tag, [128, 1], F32),
                sb(es, "rstd" + tag, [128, 1], F32), sb(es, "abf" + tag, [128, D], BF16), tp)


    with ExitStack() as es:
        Wkv = sb(es, "Wkv", [128, KC, 2176], BF16)
        P.dma("sp", multi(3, lambda e: [
            e.dma_start(out=Wkv[:, :, 0:2048], in_=Wb_in[:, :, C_K:C_K + 2048]),
            e.dma_start(out=Wkv[:, :, 2048:2112], in_=Wb_in[:, :, C_KI:C_KI + 64]),
            e.dma_start(out=Wkv[:, :, 2112:2176], in_=Wb_in[:, :, C_KI:C_KI + 64])]), r=[Wb_in], w=[Wkv])
        tp1 = [ps(es, "tp1_%d" % i, [128, 512]) for i in range(2)]
        xts1 = [sb(es, "xt1_%d" % i, [128, D], F32) for i in range(4)]
        sqj1 = [sb(es, "sqj1_%d" % i, [128, D], BF16) for i in range(2)]
        rb = [(xts1[i % 4], sqj1[i % 2], sb(es, "ss1_%d" % i, [128, 1], F32), sb(es, "rstd1_%d" % i, [128, 1], F32),
               sb(es, "abf1_%d" % i, [128, D], BF16), tp1) for i in range(8)]
        aTs = [sb(es, "aTs%d" % i, [128, KC, 512], BF16) for i in range(2)]
        KTst = [sb(es, "KTst%d" % i, [128, 4, NH, 128], BF16) for i in range(2)]
        Vst = [sb(es, "Vst%d" % i, [128, NH, 129], BF16) for i in range(2)]
        for v in Vst:
            P.op("dve", lambda e, v=v: e.memset(v[:, :, 128:129], 1.0), w=[v])
        pk = [ps(es, "pk%d" % i, [128, 512]) for i in range(2)]
        pv = [ps(es, "pv%d" % i, [128, 512]) for i in range(2)]
        pki = ps(es, "pki", [128, 512])
        pkc = 0

        def pre_tile(j):
            if j < NCH:
                rms_pre(rb[j % 8], x_all[j * 128:(j + 1) * 128, :], 128)
                next(g_rest, None)

        for j in range(min(4, NCH)):
            pre_tile(j)
        for g0 in range(0, NCH, 4):
            nt = min(4, NCH - g0)
            N = nt * 128
            aT = aTs[(g0 // 4) % 2]
            for jj in range(nt):
                rms_tr(rb[(g0 + jj) % 8], 128, aT, jj * 128)
            KS = KTst[(g0 // 4) % 2]
            for h in range(NH):
                if h % 2 == 1:
                    pre_tile(g0 + 4 + h // 2)
                pb = pk[pkc % 2]
                pkc += 1
                for kc in range(KC):
                    P.op("pe", lambda e, pb=pb, kc=kc, h=h, aT=aT, N=N: e.matmul(
                        pb[:, 0:N], lhsT=Wkv[:, kc, h * 128:(h + 1) * 128], rhs=aT[:, kc, 0:N],
                        start=(kc == 0), stop=(kc == KC - 1)), r=[Wkv, aT], w=[pb])
                pbv = pb[:].rearrange("p (j s) -> p j s", s=128)
                if h % 2 == 0:
                    P.op("act", lambda e, pbv=pbv, KS=KS, h=h, nt=nt: e.activation(out=KS[:, 0:nt, h, :], in_=pbv[:, 0:nt, :],
                                                                                func=AF.Copy), r=[pb], w=[KS])
                else:
                    P.op("dve", lambda e, pbv=pbv, KS=KS, h=h, nt=nt: e.tensor_copy(out=KS[:, 0:nt, h, :], in_=pbv[:, 0:nt, :]),
                         r=[pb], w=[KS])
            P.dma("pool", lambda e, KS=KS, g0=g0, nt=nt: e.dma_start(
                out=KT_d[g0:g0 + nt].rearrange("j d f -> d j f"),
                in_=KS[:, 0:nt].rearrange("p j h s -> p j (h s)")), r=[KS], w=[KT_d], key="dr_KT_d")
            for kc in range(KC):
                P.op("pe", lambda e, kc=kc, aT=aT, N=N: e.matmul(pki[:, 0:N], lhsT=Wkv[:, kc, 2048:2176], rhs=aT[:, kc, 0:N],
                                                              start=(kc == 0), stop=(kc == KC - 1)), r=[Wkv, aT], w=[pki])
            P.op("act", lambda e, g0=g0, N=N: e.activation(out=kiT[:, g0 * 128:g0 * 128 + N], in_=pki[:, 0:N], func=AF.Copy),
                 r=[pki], w=[kiT])
            for jj in range(nt):
                VS = Vst[(g0 + jj) % 2]
                for half in range(2):
                    pb = pv[half]
                    for kc in range(KC):
                        P.op("pe", lambda e, pb=pb, kc=kc, half=half, aT=aT, jj=jj: e.matmul(
                            pb[:, :], lhsT=aT[:, kc, jj * 128:(jj + 1) * 128],
                            rhs=Wkv[:, kc, 1024 + half * 512:1024 + (half + 1) * 512],
                            start=(kc == 0), stop=(kc == KC - 1)), r=[Wkv, aT], w=[pb])
                    pbv = pb[:].rearrange("p (h d) -> p h d", d=128)
                    if half == 0:
                        P.op("act", lambda e, pbv=pbv, VS=VS: e.activation(out=VS[:, 0:4, 0:128], in_=pbv, func=AF.Copy),
                             r=[pb], w=[VS])
                    else:
                        P.op("dve", lambda e, pbv=pbv, VS=VS: e.tensor_copy(out=VS[:, 4:8, 0:128], in_=pbv), r=[pb], w=[VS])
                P.dma("pool", lambda e, VS=VS, j=g0 + jj: e.dma_start(out=V_d[j], in_=VS[:].rearrange("p h d -> p (h d)")),
                      r=[VS], w=[V_d], key="dr_V_d")
        for _ in g_rest:
            pass
        P.barrier()
    es_p0.close()

    q_d = dscr("q_d", [NSLOT, 128, NH * 128], kind="Internal")
    qi_d = dscr("qi_d", [NSLOT, 128, NH * 128], kind="Internal")
    dg_d = dscr("dg_d", [NSLOT, 128, NH * 128], kind="Internal")
    with ExitStack() as es:
        Wq = sb(es, "Wq", [128, KC, 1024 + 512], BF16)
        P.dma("sp", multi(2, lambda e: [
            e.dma_start(out=Wq[:, :, 0:1024], in_=Wb_in[:, :, C_Q:C_Q + 1024]),
            e.dma_start(out=Wq[:, :, 1024:1536], in_=Wb_in[:, :, C_QI:C_QI + 512])]), r=[Wb_in], w=[Wq])
        Ww = sb(es, "Ww", [128, KC, 8], BF16)
        P.dma("sp", lambda e: e.dma_start(out=Ww[:], in_=Wb_in[:, :, C_WI:C_WI + 8]), r=[Wb_in], w=[Ww])
        tpa = [ps(es, "tpa%d" % i, [128, 512]) for i in range(2)]
        rba = [rms_bufs(es, "qa", tpa), rms_bufs(es, "qb", tpa)]
        G4 = 4
        NG = NSLOT // G4
        aT4 = [sb(es, "aT4_%d" % i, [128, KC, G4 * 128], BF16) for i in range(2)]
        qT4 = [sb(es, "qT4_%d" % i, [128, G4, NH, 128], BF16) for i in range(2)]
        qiT4 = [sb(es, "qiT4_%d" % i, [128, G4, 4, 2, 128], BF16) for i in range(2)]
        for q_ in qiT4:
            P.op("dve", lambda e, q_=q_: e.memset(q_[:], 0.0), w=[q_])
        Dgs = [sb(es, "Dga%d" % i, [128, NH, 128], BF16) for i in range(2)]
        wsb4 = [sb(es, "wsb4_%d" % i, [128, G4 * 8], F32) for i in range(2)]
        pqa = [ps(es, "pqa%d" % i, [128, 512]) for i in range(2)]
        pw = ps(es, "pw", [128, 512])
        rms_pre(rba[0], x_own[0], 128)
        pcnt = 0
        for g in range(NG):
            aT, qTg, qiTg, wsbg = aT4[g % 2], qT4[g % 2], qiT4[g % 2], wsb4[g % 2]
            for j in range(G4):
                m = g * G4 + j
                rms_tr(rba[m % 2], 128, aT, j * 128)
                if m + 1 < NSLOT:
                    rms_pre(rba[(m + 1) % 2], x_own[m + 1], 128)
            for h in range(NH):
                pb = pqa[pcnt % 2]
                pcnt += 1
                for kc in range(KC):
                    P.op("pe", lambda e, pb=pb, h=h, kc=kc, aT=aT: e.matmul(
                        pb[:, :], lhsT=Wq[:, kc, h * 128:(h + 1) * 128], rhs=aT[:, kc, :],
                        start=(kc == 0), stop=(kc == KC - 1)), r=[Wq, aT], w=[pb])
                pbv = pb[:].rearrange("p (b t) -> p b t", t=128)
                if h % 2 == 0:
                    P.op("act", lambda e, pbv=pbv, h=h, qTg=qTg: e.activation(out=qTg[:, :, h, :], in_=pbv, func=AF.Copy),
                         r=[pb], w=[qTg])
                else:
                    P.op("dve", lambda e, pbv=pbv, h=h, qTg=qTg: e.tensor_copy(out=qTg[:, :, h, :], in_=pbv), r=[pb], w=[qTg])
            for c in range(4):
                pb = pqa[pcnt % 2]
                pcnt += 1
                for kc in range(KC):
                    P.op("pe", lambda e, pb=pb, c=c, kc=kc, aT=aT: e.matmul(
                        pb[:, :], lhsT=Wq[:, kc, 1024 + c * 128:1024 + (c + 1) * 128], rhs=aT[:, kc, :],
                        start=(kc == 0), stop=(kc == KC - 1)), r=[Wq, aT], w=[pb])
                pbv = pb[:].rearrange("p (b t) -> p b t", t=128)
                P.op("act", lambda e, pbv=pbv, c=c, qiTg=qiTg: e.activation(out=qiTg[0:64, :, c, 0, :], in_=pbv[0:64, :, :],
                                                                          func=AF.Copy), r=[pb], w=[qiTg])
                P.op("dve", lambda e, pbv=pbv, c=c, qiTg=qiTg: e.tensor_copy(out=qiTg[64:128, :, c, 1, :], in_=pbv[64:128, :, :]),
                     r=[pb], w=[qiTg])
            for j in range(G4):
                for kc in range(KC):
                    P.op("pe", lambda e, j=j, kc=kc, aT=aT: e.matmul(pw[:, j * 8:(j + 1) * 8], lhsT=aT[:, kc, j * 128:(j + 1) * 128],
                                                                    rhs=Ww[:, kc, :], start=(kc == 0), stop=(kc == KC - 1)),
                         r=[Ww, aT], w=[pw])
            P.op("dve", lambda e, wsbg=wsbg: e.tensor_scalar(out=wsbg[:], in0=pw[:, 0:G4 * 8], scalar1=IDX_SCALE, scalar2=None,
                                                            op0=ALU.mult), r=[pw], w=[wsbg])
            for j in range(G4):
                m = g * G4 + j
                Dg = Dgs[m % 2]
                for h in range(NH):
                    col = j * 8 + h
                    if h % 2 == 0:
                        P.op("dve", lambda e, h=h, Dg=Dg, wsbg=wsbg, col=col: e.tensor_scalar(
                            out=Dg[:, h, :], in0=ident_f[:], scalar1=wsbg[:, col:col + 1], scalar2=None, op0=ALU.mult),
                            r=[ident_f, wsbg], w=[Dg])
                    else:
                        P.op("act", lambda e, h=h, Dg=Dg, wsbg=wsbg, col=col: e.activation(
                            out=Dg[:, h, :], in_=ident_f[:], func=AF.Copy, scale=wsbg[:, col:col + 1]), r=[ident_f, wsbg], w=[Dg])
                P.dma("pool", lambda e, m=m, j=j, qTg=qTg: e.dma_start(out=q_d[m], in_=qTg[:, j].rearrange("p h t -> p (h t)")),
                      r=[qTg], w=[q_d], key="dr_q_d")
                P.dma("pool", lambda e, m=m, j=j, qiTg=qiTg: e.dma_start(out=qi_d[m], in_=qiTg[:, j].rearrange("p c two t -> p (c two t)")),
                      r=[qiTg], w=[qi_d], key="dr_qi_d")
                P.dma("pool", lambda e, m=m, Dg=Dg: e.dma_start(out=dg_d[m], in_=Dg[:].rearrange("p h t -> p (h t)")),
                      r=[Dg], w=[dg_d], key="dr_dg_d")
        P.barrier()

    with ExitStack() as es:
        cbx = sb(es, "cbx", [128, 512], F32)
        cbm = sb(es, "cbm", [128, 128], F32)
        P.dma("sp", lambda e: e.dma_start(out=cbx[:], in_=cbx_in[:, :]), w=[cbx])
        P.dma("sp", lambda e: e.dma_start(out=cbm[:], in_=cbm_in[:, :]), w=[cbm])
        qTs = [sb(es, "qT%d" % i, [128, NH, 128], BF16) for i in range(2)]
        qiTs = [sb(es, "qiT%d" % i, [128, NH, 128], BF16) for i in range(2)]
        Dgs = [sb(es, "Dg%d" % i, [128, NH, 128], BF16) for i in range(2)]
        score = sb(es, "score", [128, SMAX], F32)
        MBs = [sb(es, "MB%d" % i, [128, SMAX], BF16) for i in range(2)]
        MBas = [Buf("MBact%d" % i, MBs[i].t) for i in range(2)]
        sacc = sb(es, "sacc", [128, 1], F32)
        cnte = sb(es, "cnte", [128, 1], F32)
        NR = 3
        Rsb = [sb(es, "Rsb%d" % i, [128, 512], BF16) for i in range(NR)]
        stile = [sb(es, "stile%d" % i, [128, 512], F32) for i in range(2)]
        score_d = [dscr("score_d%d" % i, [128, SMAX], F32, kind="Internal") for i in range(2)]
        PT = [sb(es, "PT%d" % i, [128, 512], BF16) for i in range(4)]
        NKV = 3
        KTs = [sb(es, "KTs%d" % i, [128, NH * 128], BF16) for i in range(NKV)]
        Vs = [sb(es, "Vs%d" % i, [128, NH * 129], BF16) for i in range(NKV)]
        amax = sb(es, "amax", [128, 1], F32)
        hk = sb(es, "hk", [128, NBIS + 2], F32)
        p2 = sb(es, "p2", [128, NBIS + 2], F32)
        mid = sb(es, "mid", [128, 1], F32)
        cntb = sb(es, "cntb", [128, 1], F32)
        ub = sb(es, "ub", [128, 1], F32)
        tau = sb(es, "tau", [128, 1], F32)
        rden = sb(es, "rden", [128, NH], F32)
        yat = sb(es, "yat", [128, D], BF16)
        yT = sb(es, "yT", [128, D], BF16)
        for k in range(NBIS + 2):
            P.op("dve", lambda e, k=k: e.memset(p2[:, k:k + 1], 2.0 ** (-k)), w=[p2])
        Rps = [ps(es, "Rps%d" % i, [128, 512]) for i in range(2)]
        SCps = ps(es, "SCps", [128, 512])
        Lps = [ps(es, "Lps%d" % i, [128, 512]) for i in range(2)]
        Ops = [ps(es, "Ops%d" % i, [128, 512]) for i in range(3)]
        cnts = {"r": 0, "kv": 0, "l": 0, "st": 0}

        def chunks_of(m):
            return list(range(4 * m + 4)) + [NXC]

        def load_qT(m):
            P.dma("sp", lambda e: e.dma_start(out=qTs[m % 2][:].rearrange("p h t -> p (h t)"), in_=q_d[m]), r=[q_d], w=[qTs[m % 2]])

        def load_qi(m):
            P.dma("sp", lambda e: e.dma_start(out=qiTs[m % 2][:].rearrange("p h t -> p (h t)"), in_=qi_d[m]), r=[qi_d], w=[qiTs[m % 2]])
            P.dma("sp", lambda e: e.dma_start(out=Dgs[m % 2][:].rearrange("p h t -> p (h t)"), in_=dg_d[m]), r=[dg_d], w=[Dgs[m % 2]])

        def stage_I(m):
            qiT, Dg = qiTs[m % 2], Dgs[m % 2]
            par = m % 2
            nxg = m + 1
            groups = [(g * 512, 512, g * 512) for g in range(nxg)] + [(NXC * 128, 128, nxg * 512)]
            for gi, (k0, N, s0) in enumerate(groups):
                pend = None
                for h in range(NH):
                    rp = Rps[cnts["r"] % 2]
                    rs = Rsb[cnts["r"] % NR]
                    cnts["r"] += 1
                    P.op("pe", lambda e, rp=rp, h=h, k0=k0, N=N: e.matmul(
                        rp[:, 0:N], lhsT=qiT[:, h, :], rhs=kiT[:, k0:k0 + N], start=True, stop=True),
                        r=[qiT, kiT], w=[rp])
                    P.op("act", lambda e, rp=rp, rs=rs, N=N: e.activation(out=rs[:, 0:N], in_=rp[:, 0:N], func=AF.Relu),
                         r=[rp], w=[rs])
                    if pend is not None:
                        ph_, prs = pend
                        P.op("pe", lambda e, prs=prs, ph_=ph_, N=N: e.matmul(SCps[:, 0:N], lhsT=Dg[:, ph_, :], rhs=prs[:, 0:N],
                                                                           start=(ph_ == 0), stop=False), r=[Dg, prs], w=[SCps])
                    pend = (h, rs)
                    yield
                ph_, prs = pend
                P.op("pe", lambda e, prs=prs, ph_=ph_, N=N: e.matmul(SCps[:, 0:N], lhsT=Dg[:, ph_, :], rhs=prs[:, 0:N],
                                                                   start=False, stop=True), r=[Dg, prs], w=[SCps])
                st = stile[cnts["st"] % 2]
                cnts["st"] += 1
                P.op("act", lambda e, st=st, N=N: e.activation(out=st[:, 0:N], in_=SCps[:, 0:N], func=AF.Copy), r=[SCps], w=[st])
                P.dma("pool", lambda e, st=st, s0=s0, N=N, par=par: e.dma_start(out=score_d[par][:, s0:s0 + N], in_=st[:, 0:N]),
                      r=[st], w=[score_d[par]], key=score_d[par].name)
                yield

        scq = [Buf("scq%d" % i, score.t) for i in range(4)]
        amax4 = sb(es, "amax4", [128, 4], F32)
        hk2 = sb(es, "hk2", [128, NBIS + 2], F32)
        thr0 = sb(es, "thr0", [128, 1], F32)

        def quarters(m):
            S = len(chunks_of(m)) * 128
            return [(i * S // 4 // 128) * 128 for i in range(4)] + [S]

        def reload(m):
            par = m % 2
            q4 = quarters(m)
            for i in range(4):
                P.dma("sp", lambda e, i=i: e.dma_start(out=score[:, q4[i]:q4[i + 1]], in_=score_d[par][:, q4[i]:q4[i + 1]]),
                      r=[score_d[par]], w=[scq[i]], key="scq%d" % i)

        def stage_B(m):
            S = len(chunks_of(m)) * 128
            MB, MBa = MBs[m % 2], MBas[m % 2]
            n_act = min(ACT_COLS, (S // 3 // 128) * 128)
            c0 = S - n_act
            nxg = m + 1
            q4 = quarters(m)
            for i in range(4):
                P.op("dve", lambda e, i=i: e.tensor_reduce(out=amax4[:, i:i + 1], in_=score[:, q4[i]:q4[i + 1]], axis=AX.X, op=ALU.max,
                                                           apply_absolute_value=True), r=[scq[i]], w=[amax4])
            P.op("dve", lambda e: e.tensor_reduce(out=amax[:], in_=amax4[:], axis=AX.X, op=ALU.max), r=[amax4], w=[amax])
            sx = (nxg - 1) * 512
            P.op("dve", lambda e, sx=sx: e.tensor_tensor(out=score[:, sx:sx + 512], in0=score[:, sx:sx + 512], in1=cbx[:],
                                                         op=ALU.add), r=scq + [cbx], w=scq)
            sm = nxg * 512
            P.op("dve", lambda e, sm=sm: e.tensor_tensor(out=score[:, sm:sm + 128], in0=score[:, sm:sm + 128], in1=cbm[:],
                                                         op=ALU.add), r=scq + [cbm], w=scq)
            P.op("dve", lambda e: e.tensor_scalar(out=amax[:], in0=amax[:], scalar1=1.001, scalar2=1e-3, op0=ALU.mult,
                                                  op1=ALU.add), r=[amax], w=[amax])
            P.op("dve", lambda e: e.tensor_scalar(out=hk[:], in0=p2[:], scalar1=amax[:, 0:1], scalar2=None, op0=ALU.mult),
                 r=[p2, amax], w=[hk])
            P.op("dve", lambda e: e.tensor_copy(out=mid[:], in_=hk[:, NBIS + 1:NBIS + 2]), r=[hk], w=[mid])
            P.op("dve", lambda e: e.tensor_scalar(out=hk2[:], in0=hk[:], scalar1=2.0, scalar2=None, op0=ALU.mult), r=[hk], w=[hk2])
            P.op("dve", lambda e, n_act=n_act: e.memset(thr0[:], TOPK - 0.5 - 0.5 * n_act), w=[thr0])
            if n_act == 0:
                P.op("dve", lambda e: e.memset(cnte[:], TOPK - 0.5), w=[cnte])
            yield
            for k in range(NBIS):
                if n_act > 0:
                    P.op("act", lambda e, S=S, c0=c0: e.activation(out=MBa[:, c0:S], in_=score[:, c0:S], func=AF.Sign, scale=-1.0,
                                                                   bias=mid[:, 0:1], accum_out=sacc[:, 0:1]),
                         r=scq + [mid], w=[MBa, sacc])
                    P.op("act", lambda e, n_act=n_act: e.activation(out=cnte[:], in_=sacc[:], func=AF.Identity, scale=0.5,
                                                                    bias=thr0[:, 0:1]), r=[sacc, thr0], w=[cnte])
                hcol = k + 1 if k < NBIS - 1 else k
                P.op("dve", lambda e, hcol=hcol: e.tensor_scalar(out=tau[:], in0=mid[:], scalar1=hk[:, hcol:hcol + 1], scalar2=None,
                                                                 op0=ALU.subtract), r=[mid, hk], w=[tau])
                P.op("dve", lambda e, c0=c0: e.tensor_scalar(out=MB[:, 0:c0], in0=score[:, 0:c0], scalar1=mid[:, 0:1], scalar2=None,
                                                             op0=ALU.is_ge, op1=ALU.add, accum_out=cntb[:, 0:1]),
                     r=scq + [mid], w=[MB, cntb])
                if k < NBIS - 1:
                    P.op("dve", lambda e, k=k: e.tensor_scalar(out=ub[:], in0=cntb[:], scalar1=cnte[:, 0:1], scalar2=hk2[:, k + 1:k + 2],
                                                               op0=ALU.is_ge, op1=ALU.mult), r=[cntb, cnte, hk2], w=[ub])
                    P.op("dve", lambda e: e.tensor_tensor(out=mid[:], in0=ub[:], in1=tau[:], op=ALU.add), r=[ub, tau], w=[mid])
                else:
                    P.op("dve", lambda e, k=k: e.tensor_scalar(out=ub[:], in0=cntb[:], scalar1=cnte[:, 0:1], scalar2=hk[:, k:k + 1],
                                                               op0=ALU.is_ge, op1=ALU.mult), r=[cntb, cnte, hk], w=[ub])
                    P.op("dve", lambda e: e.tensor_tensor(out=tau[:], in0=ub[:], in1=tau[:], op=ALU.add), r=[ub, tau], w=[tau])
                    P.op("dve", lambda e, S=S: e.tensor_scalar(out=MB[:, 0:S], in0=score[:, 0:S], scalar1=tau[:, 0:1], scalar2=NEG,
                                                               op0=ALU.is_lt, op1=ALU.mult), r=scq + [tau], w=[MB, MBa])
                yield

        def stage_A(m):
            qT, MB, MBa = qTs[m % 2], MBs[m % 2], MBas[m % 2]
            chunks = chunks_of(m)
            nchunks = len(chunks)
            prev = None

            def emit_pv(ci, pts, vv, half):
                vv3 = vv[:].rearrange("p (h d) -> p h d", d=129)
                pt = pts[half]
                for hh in range(4):
                    h = half * 4 + hh
                    ob = Ops[h // 3]
                    o0 = (h % 3) * 129
                    P.op("pe", lambda e, ob=ob, o0=o0, pt=pt, hh=hh, h=h, vv3=vv3, ci=ci: e.matmul(
                        ob[:, o0:o0 + 129], lhsT=pt[:, hh * 128:(hh + 1) * 128], rhs=vv3[:, h, :],
                        start=(ci == 0 and h % 3 == 0), stop=(ci == nchunks - 1), skip_group_check=True),
                        r=[pt, vv], w=[ob])

            for ci, j in enumerate(chunks):
                kt = KTs[cnts["kv"] % NKV]
                vv = Vs[cnts["kv"] % NKV]
                cnts["kv"] += 1
                P.dma("sp", lambda e, kt=kt, j=j: e.dma_start(out=kt[:], in_=KT_d[j]), r=[KT_d], w=[kt])
                P.dma("sp", lambda e, vv=vv, j=j: e.dma_start(out=vv[:], in_=V_d[j]), r=[V_d], w=[vv])
                pts = []
                for half in range(2):
                    lp = Lps[half]
                    pt = PT[(cnts["l"] % 2) * 2 + half]
                    pts.append(pt)
                    P.op("pe", lambda e, lp=lp, ci=ci: e.matmul(
                        lp[:, :], lhsT=MB[:, ci * 128:(ci + 1) * 128], rhs=ident4[:].rearrange("p a t -> p (a t)"),
                        start=True, stop=False, skip_group_check=True), r=[MB, MBa, ident4], w=[lp])
                    for hh in range(4):
                        h = half * 4 + hh
                        P.op("pe", lambda e, lp=lp, hh=hh, h=h, kt=kt: e.matmul(
                            lp[:, hh * 128:(hh + 1) * 128], lhsT=kt[:, h * 128:(h + 1) * 128], rhs=qT[:, h, :],
                            start=False, stop=(hh == 3), skip_group_check=True), r=[kt, qT], w=[lp])
                    P.op("act", lambda e, lp=lp, pt=pt: e.activation(out=pt[:], in_=lp[:], func=AF.Exp, scale=SM_SCALE),
                         r=[lp], w=[pt])
                    yield
                    if prev is not None:
                        emit_pv(prev[0], prev[1], prev[2], half)
                        yield
                cnts["l"] += 1
                prev = (ci, pts, vv)
            emit_pv(prev[0], prev[1], prev[2], 0)
            emit_pv(prev[0], prev[1], prev[2], 1)
            yield

        def stage_N(m):
            for b3 in range(3):
                nh3 = 3 if b3 < 2 else 2
                ov = Ops[b3][:, 0:nh3 * 129].rearrange("p (h d) -> p h d", d=129)
                P.op("dve", lambda e, ov=ov, b3=b3, nh3=nh3: e.reciprocal(out=rden[:, b3 * 3:b3 * 3 + nh3], in_=ov[:, :, 128]),
                     r=[Ops[b3]], w=[rden])
            for h in range(NH):
                ob = Ops[h // 3]
                o0 = (h % 3) * 129
                P.op("act", lambda e, ob=ob, o0=o0, h=h: e.activation(out=yat[:, h * 128:(h + 1) * 128], in_=ob[:, o0:o0 + 128],
                                                                    func=AF.Copy, scale=rden[:, h:h + 1]), r=[ob, rden], w=[yat])
            yT3 = yT[:].rearrange("p (c t) -> p c t", t=128)
            dfn = lambda hf: yT3[:, hf * 4:(hf + 1) * 4, :]
            dfn.buf = yT
            pe_transpose(yat, 128, Lps, dfn)
            P.dma("pool", lambda e, m=m: e.dma_start(out=ya_d[m], in_=yT[:]), r=[yT], w=[ya_d], key="dr_ya_d")

        def n_I(m):
            return (m + 2) * (NH + 1)

        def n_A(m):
            return 4 * len(chunks_of(m)) - 1

        def drive(gI, nI, gA, nA, gB, nB):
            accA = accB = 0.0
            hold = min(B_HOLD, nI // 3)
            if gB is not None:
                next(gB, None)
            holdA = min(A_HOLD, nI // 3)
            for it, _ in enumerate(gI):
                if it >= holdA:
                    accA += nA / float(nI - holdA)
                if it >= hold:
                    accB += nB / float(nI - hold)
                while accA >= 1.0:
                    next(gA, None)
                    accA -= 1.0
                while gB is not None and accB >= 1.0:
                    next(gB, None)
                    accB -= 1.0
            if gB is not None:
                for _ in gB:
                    pass
            for _ in gA:
                pass

        load_qi(0)
        if NSLOT > 1:
            load_qi(1)
        load_qT(0)
        for _ in stage_I(0):
            pass
        if NSLOT > 1:
            for _ in stage_I(1):
                pass
        reload(0)
        for _ in stage_B(0):
            pass
        for m in range(NSLOT):
            gA = stage_A(m)
            for _ in range(5):
                next(gA, None)
            if m + 2 < NSLOT:
                load_qi(m + 2)
            if m + 1 < NSLOT:
                load_qT(m + 1)
            if m + 1 < NSLOT:
                reload(m + 1)
            gB = stage_B(m + 1) if m + 1 < NSLOT else None
            if m + 2 < NSLOT:
                drive(stage_I(m + 2), n_I(m + 2), gA, n_A(m), gB, NBIS)
            elif gB is not None:
                per = -(-n_A(m) // NBIS)
                for _ in gB:
                    for _ in range(per):
                        next(gA, None)
                for _ in gA:
                    pass
            else:
                for _ in gA:
                    pass
            stage_N(m)
        P.barrier()
    es_ki.close()

    with ExitStack() as es:
        NT = TT * 128
        WSLOT = 6
        wslot = [sb(es, "wsl%d" % i, [128, 4096], BF16) for i in range(WSLOT)]
        wcnt = [0]

        def wload(Wb, K_c, c0, ncols):
            s = wslot[wcnt[0] % WSLOT]
            wcnt[0] += 1
            view = s[:, 0:K_c * ncols].rearrange("p (k n) -> p k n", n=ncols)
            P.dma("sp", lambda e: e.dma_start(out=view, in_=Wb[:, :, c0:c0 + ncols]), r=[Wb], w=[s])
            return s, view

        gfin = sb(es, "gfin", [128, D], F32)
        cw = sb(es, "cw", [128, KC * 3], F32)
        P.dma("sp", lambda e: e.dma_start(out=gfin[:], in_=gfin_in[:, :]), w=[gfin])
        P.dma("sp", lambda e: e.dma_start(out=cw[:], in_=cw_in[:, :]), w=[cw])
        pg = [ps(es, "pg%d" % i, [128, 512]) for i in range(2)]
        rbc = rms_bufs(es, "c", pg)
        hx = [sb(es, "hx%d" % i, [128, D], F32) for i in range(TT)]
        ss_c = [sb(es, "ssc%d" % i, [128, 1], F32) for i in range(TT)]
        rstd_c = [sb(es, "rstdc%d" % i, [128, 1], F32) for i in range(TT)]
        abf_c = [rbc[4], sb(es, "abfc2", [128, D], BF16)]
        aT = sb(es, "aTc", [128, KC, NT], BF16)
        aTh = sb(es, "aTh", [128, KC, 2 * TT], BF16)
        yaT = sb(es, "yaT", [128, KC, NT], BF16)
        uT = sb(es, "uT", [128, KC, TT, 130], F32)
        cuT = sb(es, "cuT", [128, KC, TT, 130], BF16)
        acc = sb(es, "acc", [128, KC, TT, 128], F32)
        ycv = sb(es, "ycv", [128, KC, NT], BF16)
        sg = sb(es, "sg", [128, KC, NT], BF16)
        mixed = acc
        mixb = sb(es, "mixb", [128, KC, NT], BF16)
        fT = sb(es, "fT", [128, KC, NT], BF16)
        actT = sb(es, "actT", [128, FC, NT], BF16)
        sil = [sb(es, "sil%d" % i, [128, NT], F32) for i in range(2)]
        ss2 = sb(es, "ss2", [128, 1], F32)
        rs2 = sb(es, "rs2", [128, 1], F32)
        fbf = sb(es, "fbf", [128, D], BF16)
        sqj = sb(es, "sqj2", [128, D], BF16)
        ot = [sb(es, "ot%d" % i, [128, D], F32) for i in range(2)]
        NPA = 6
        pa = [ps(es, "pa%d" % i, [128, 512]) for i in range(NPA)]
        pac = [0]

        def nxt():
            b = pa[pac[0] % NPA]
            pac[0] += 1
            return b

        def proj_fm(Wb, c_base, consume):
            for hb in range(2):
                s, wv = wload(Wb, KC, c_base + hb * 512, 512)
                for c4 in range(4):
                    c = hb * 4 + c4
                    pm = nxt()
                    for kc in range(KC):
                        P.op("pe", lambda e, pm=pm, wv=wv, c4=c4, kc=kc: e.matmul(
                            pm[:, 0:NT], lhsT=wv[:, kc, c4 * 128:(c4 + 1) * 128], rhs=aT[:, kc, :], start=(kc == 0), stop=(kc == KC - 1)),
                            r=[s, aT], w=[pm])
                    consume(c, pm, s, wv, c4)

        for sp_i in range(NSUP):
            m0 = sp_i * TT
            for tt in range(TT):
                bufs_tt = (hx[tt], rbc[1], ss_c[tt], rstd_c[tt], abf_c[tt % 2], pg)
                rms_pre(bufs_tt, x_own[m0 + tt], 128)
                if tt >= 1:
                    bufs_p = (hx[tt - 1], rbc[1], ss_c[tt - 1], rstd_c[tt - 1], abf_c[(tt - 1) % 2], pg)
                    rms_tr(bufs_p, 128, aT, (tt - 1) * 128)
            bufs_p = (hx[TT - 1], rbc[1], ss_c[TT - 1], rstd_c[TT - 1], abf_c[(TT - 1) % 2], pg)
            rms_tr(bufs_p, 128, aT, (TT - 1) * 128)
            rms_transpose(rbc, x_halo[2 * m0:2 * m0 + 2 * TT, :], 2 * TT, aTh, 0)
            P.dma("sp", multi(TT, lambda e, m0=m0: [
                e.dma_start(out=yaT[:, :, tt * 128:(tt + 1) * 128], in_=ya_d[m0 + tt].rearrange("p (c t) -> p c t", t=128))
                for tt in range(TT)]), r=[ya_d], w=[yaT])

            def halo_mm(pm, s, wv, c4):
                for kc in range(KC):
                    P.op("pe", lambda e, pm=pm, wv=wv, c4=c4, kc=kc: e.matmul(
                        pm[:, 0:2 * TT], lhsT=wv[:, kc, c4 * 128:(c4 + 1) * 128], rhs=aTh[:, kc, :], start=(kc == 0),
                        stop=(kc == KC - 1)), r=[s, aTh], w=[pm])

            def c_cu(c, pm, s, wv, c4):
                P.op("act", lambda e, c=c, pm=pm: e.activation(out=cuT[:, c, :, 2:130], in_=pm[:, 0:NT].rearrange("p (b t) -> p b t", t=128),
                                                            func=AF.Copy), r=[pm], w=[cuT])
                p2_ = nxt()
                halo_mm(p2_, s, wv, c4)
                P.op("act", lambda e, c=c, p2_=p2_: e.activation(out=cuT[:, c, :, 0:2], in_=p2_[:, 0:2 * TT].rearrange("p (b t) -> p b t", t=2),
                                                             func=AF.Copy), r=[p2_], w=[cuT])

            proj_fm(Wb_in2, C_CU, c_cu)

            def c_cc(c, pm, s, wv, c4):
                P.op("dve", lambda e, c=c, pm=pm: e.tensor_tensor(out=uT[:, c, :, 2:130], in0=pm[:, 0:NT].rearrange("p (b t) -> p b t", t=128),
                                                               in1=cuT[:, c, :, 2:130], op=ALU.mult), r=[pm, cuT], w=[uT])
                p2_ = nxt()
                halo_mm(p2_, s, wv, c4)
                P.op("dve", lambda e, c=c, p2_=p2_: e.tensor_tensor(out=uT[:, c, :, 0:2], in0=p2_[:, 0:2 * TT].rearrange("p (b t) -> p b t", t=2),
                                                                in1=cuT[:, c, :, 0:2], op=ALU.mult), r=[p2_, cuT], w=[uT])
                P.op("act", lambda e, c=c: e.activation(out=acc[:, c], in_=uT[:, c, :, 0:128], func=AF.Copy,
                                                       scale=cw[:, c * 3:c * 3 + 1]), r=[uT, cw], w=[acc])
                P.op("dve", lambda e, c=c: e.scalar_tensor_tensor(out=acc[:, c], in0=uT[:, c, :, 1:129], scalar=cw[:, c * 3 + 1:c * 3 + 2],
                                                                 in1=acc[:, c], op0=ALU.mult, op1=ALU.add), r=[uT, cw, acc], w=[acc])
                P.op("dve", lambda e, c=c: e.scalar_tensor_tensor(out=acc[:, c], in0=uT[:, c, :, 2:130], scalar=cw[:, c * 3 + 2:c * 3 + 3],
                                                                 in1=acc[:, c], op0=ALU.mult, op1=ALU.add), r=[uT, cw, acc], w=[acc])

            proj_fm(Wb_in2, C_CC, c_cc)

            def c_cb(c, pm, s, wv, c4):
                P.op("dve", lambda e, c=c, pm=pm: e.tensor_tensor(out=ycv[:, c, :].rearrange("p (b t) -> p b t", t=128),
                                                               in0=pm[:, 0:NT].rearrange("p (b t) -> p b t", t=128), in1=acc[:, c],
                                                               op=ALU.mult), r=[pm, acc], w=[ycv])

            proj_fm(Wb_in2, C_CB, c_cb)

            def c_gate(c, pm, s, wv, c4):
                P.op("act", lambda e, c=c, pm=pm: e.activation(out=sg[:, c, :], in_=pm[:, 0:NT], func=AF.Sigmoid), r=[pm], w=[sg])

            def branch(Wb, src, first):
                for c in range(KC):
                    if c % 4 == 0:
                        s, wv = wload(Wb, KC, c * 128, 512)
                    c4 = c % 4
                    pm = nxt()
                    for kc in range(KC):
                        P.op("pe", lambda e, pm=pm, wv=wv, c4=c4, kc=kc, s=s: e.matmul(
                            pm[:, 0:NT], lhsT=wv[:, kc, c4 * 128:(c4 + 1) * 128], rhs=src[:, kc, :], start=(kc == 0), stop=(kc == KC - 1)),
                            r=[s, src], w=[pm])
                    if first:
                        P.op("dve", lambda e, c=c, pm=pm: e.tensor_tensor(out=mixed[:, c].rearrange("p b t -> p (b t)"), in0=pm[:, 0:NT], in1=sg[:, c, :], op=ALU.mult),
                             r=[pm, sg], w=[mixed])
                    else:
                        P.op("dve", lambda e, c=c, pm=pm: e.tensor_tensor(out=sil[0][:], in0=pm[:, 0:NT], in1=sg[:, c, :], op=ALU.mult),
                             r=[pm, sg], w=[sil[0]])
                        P.op("dve", lambda e, c=c: e.tensor_tensor(out=mixb[:, c, :], in0=sil[0][:], in1=mixed[:, c].rearrange("p b t -> p (b t)"), op=ALU.add),
                             r=[sil[0], mixed], w=[mixb])

            proj_fm(Wb_in2, C_GA, c_gate)
            branch(Wb_ao, yaT, True)
            proj_fm(Wb_in2, C_GB, c_gate)
            branch(Wb_co, ycv, False)

            for half in range(2):
                s_, wv = wload(Wb_o, KC, half * 512, 512)
                for tt in range(TT):
                    ph = nxt()
                    for kc in range(KC):
                        P.op("pe", lambda e, ph=ph, wv=wv, tt=tt, kc=kc: e.matmul(
                            ph[:, :], lhsT=mixb[:, kc, tt * 128:(tt + 1) * 128], rhs=wv[:, kc, :],
                            start=(kc == 0), stop=(kc == KC - 1)), r=[s_, mixb], w=[ph])
                    P.op("dve", lambda e, ph=ph, tt=tt, half=half: e.tensor_tensor(out=hx[tt][:, half * 512:(half + 1) * 512], in0=ph[:, :],
                                                                                  in1=hx[tt][:, half * 512:(half + 1) * 512], op=ALU.add),
                         r=[ph, hx[tt]], w=[hx[tt]])
            for tt in range(TT):
                h_ = hx[tt]
                P.op("act", lambda e, h_=h_: e.activation(out=sqj[:], in_=h_[:], func=AF.Square, accum_out=ss2[:, 0:1]), r=[h_], w=[sqj, ss2])
                P.op("act", lambda e: e.activation(out=rs2[:], in_=ss2[:], func=AF.Sqrt, scale=1.0 / D, bias=epsb[:, 0:1]), r=[ss2, epsb], w=[rs2])
                P.op("dve", lambda e: e.reciprocal(out=rs2[:], in_=rs2[:]), r=[rs2], w=[rs2])
                P.op("dve", lambda e, h_=h_: e.tensor_scalar(out=fbf[:], in0=h_[:], scalar1=rs2[:, 0:1], scalar2=None, op0=ALU.mult),
                     r=[h_, rs2], w=[fbf])
                dfn = lambda hf, tt=tt: fT[:, hf * 4:(hf + 1) * 4, tt * 128:(tt + 1) * 128]
                dfn.buf = fT
                pe_transpose(fbf, 128, pg, dfn)
            for f0 in range(0, FC, 4):
                nf = min(4, FC - f0)
                sg_, wg = wload(Wb_g, KC, f0 * 128, nf * 128)
                su_, wu = wload(Wb_u, KC, f0 * 128, nf * 128)
                for fc in range(nf):
                    for kc in range(KC):
                        P.op("pe", lambda e, wg=wg, fc=fc, kc=kc: e.matmul(pg[0][:, 0:NT], lhsT=wg[:, kc, fc * 128:(fc + 1) * 128],
                                                                         rhs=fT[:, kc, :], start=(kc == 0), stop=(kc == KC - 1)),
                             r=[sg_, fT], w=[pg[0]])
                    for kc in range(KC):
                        P.op("pe", lambda e, wu=wu, fc=fc, kc=kc: e.matmul(pg[1][:, 0:NT], lhsT=wu[:, kc, fc * 128:(fc + 1) * 128],
                                                                         rhs=fT[:, kc, :], start=(kc == 0), stop=(kc == KC - 1)),
                             r=[su_, fT], w=[pg[1]])
                    sl = sil[(f0 + fc) % 2]
                    P.op("act", lambda e, sl=sl: e.activation(out=sl[:], in_=pg[0][:, 0:NT], func=AF.Silu), r=[pg[0]], w=[sl])
                    P.op("dve", lambda e, sl=sl, f=f0 + fc: e.tensor_tensor(out=actT[:, f, :], in0=pg[1][:, 0:NT], in1=sl[:], op=ALU.mult),
                         r=[pg[1], sl], w=[actT])
            FG = [(0, 8), (8, 16), (16, FC)]
            for half in range(2):
                slots = []
                for (f0, f1) in FG:
                    s_ = wslot[wcnt[0] % WSLOT]
                    wcnt[0] += 1
                    nfc = f1 - f0
                    P.dma("sp", lambda e, s_=s_, half=half, f0=f0, f1=f1, nfc=nfc: e.dma_start(
                        out=s_[:, 0:nfc * 512], in_=Wb_d[:, half, f0:f1, :].rearrange("p f c -> p (f c)")), r=[Wb_d], w=[s_])
                    slots.append((s_, s_[:, 0:nfc * 512].rearrange("p (k n) -> p k n", n=512), f0, f1))
                for tt in range(TT):
                    pq = nxt()
                    for (s_, wv, f0, f1) in slots:
                        for fc in range(f0, f1):
                            P.op("pe", lambda e, pq=pq, wv=wv, tt=tt, fc=fc, f0=f0: e.matmul(
                                pq[:, :], lhsT=actT[:, fc, tt * 128:(tt + 1) * 128], rhs=wv[:, fc - f0, :],
                                start=(fc == 0), stop=(fc == FC - 1)), r=[s_, actT], w=[pq])
                    P.op("dve", lambda e, pq=pq, tt=tt, half=half: e.tensor_tensor(out=hx[tt][:, half * 512:(half + 1) * 512], in0=pq[:, :],
                                                                                  in1=hx[tt][:, half * 512:(half + 1) * 512], op=ALU.add),
                         r=[pq, hx[tt]], w=[hx[tt]])
            for tt in range(TT):
                h_ = hx[tt]
                o_ = ot[tt % 2]
                P.op("act", lambda e, h_=h_: e.activation(out=sqj[:], in_=h_[:], func=AF.Square, accum_out=ss2[:, 0:1]), r=[h_], w=[sqj, ss2])
                P.op("act", lambda e: e.activation(out=rs2[:], in_=ss2[:], func=AF.Sqrt, scale=1.0 / D, bias=epsb[:, 0:1]), r=[ss2, epsb], w=[rs2])
                P.op("dve", lambda e: e.reciprocal(out=rs2[:], in_=rs2[:]), r=[rs2], w=[rs2])
                P.op("dve", lambda e, h_=h_, o_=o_: e.scalar_tensor_tensor(out=o_[:], in0=h_[:], scalar=rs2[:, 0:1], in1=gfin[:],
                                                                        op0=ALU.mult, op1=ALU.mult), r=[h_, rs2, gfin], w=[o_])
                P.dma("pool", lambda e, o_=o_, mm=m0 + tt: e.dma_start(out=out[mm], in_=o_[:]), r=[o_], w=[], key="out")
        P.barrier()

    P.lower(top)
    top.close()
    return nc


_CACHE = {}


def _prep_inputs(x, meta_tokens, norm_mix_g, w_in, w_attn_out, conv_w, w_conv_out, w_out, norm_ffn_g, w_gate, w_up,
                 w_down, norm_final_g):
    f = np.float32
    x = np.asarray(x, f)
    B, SEQ, _ = x.shape
    meta = np.asarray(meta_tokens, f)
    NXC = SEQ // 128
    NSLOT = NXC // 4
    common = {
        "w_in": np.ascontiguousarray(np.asarray(w_in, f)[0]),
        "w_attn_out": np.ascontiguousarray(np.asarray(w_attn_out, f)[0]),
        "w_conv_out": np.ascontiguousarray(np.asarray(w_conv_out, f)[0]),
        "w_out": np.ascontiguousarray(np.asarray(w_out, f)[0]),
        "w_gate": np.ascontiguousarray(np.asarray(w_gate, f)[0]),
        "w_up": np.ascontiguousarray(np.asarray(w_up, f)[0]),
        "w_down": np.ascontiguousarray(np.asarray(w_down, f)[0]),
        "g_mix": np.ascontiguousarray(np.asarray(norm_mix_g, f)[0].reshape(KC, 128).T),
        "g_ffn": np.ascontiguousarray(np.asarray(norm_ffn_g, f)[0].reshape(KC, 128).T),
        "g_fin": np.ascontiguousarray(np.broadcast_to(np.asarray(norm_final_g, f)[None, :], (128, D))),
        "conv_w": np.ascontiguousarray(np.asarray(conv_w, f)[0].reshape(3, KC, 128).transpose(2, 1, 0).reshape(128, KC * 3)),
        "ident": np.eye(128, dtype=f),
    }
    cbm = np.zeros((128, 128), f)
    cbm[:, 16:] = -1e30
    in_maps = []
    for core in range(8):
        b, r = core // 4, core % 4
        x_all = np.zeros(((NXC + 1) * 128, D), f)
        x_all[:SEQ] = x[b]
        x_all[SEQ:SEQ + 16] = meta
        qbs = [4 * m + r for m in range(NSLOT)]
        x_own = np.stack([x[b, qb * 128:(qb + 1) * 128] for qb in qbs])
        halo = np.zeros((NSLOT * 2, D), f)
        for m, qb in enumerate(qbs):
            if qb == 0:
                halo[2 * m:2 * m + 2] = meta[14:16]
            else:
                halo[2 * m:2 * m + 2] = x[b, qb * 128 - 2:qb * 128]
        t = np.arange(128)[:, None]
        c = np.arange(512)[None, :]
        cbx = np.where(c <= r * 128 + t, 0.0, -1e30).astype(f)
        d = dict(common)
        d.update({"x_all": x_all, "x_own": np.ascontiguousarray(x_own), "x_halo": halo, "cbx": cbx, "cbm": cbm})
        in_maps.append(d)
    return in_maps, B, SEQ, NSLOT


def kernel(x, meta_tokens, norm_mix_g, w_in, w_attn_out, conv_w, w_conv_out, w_out, norm_ffn_g, w_gate, w_up, w_down,
           norm_final_g, _dbg=False):
    in_maps, B, SEQ, NSLOT = _prep_inputs(x, meta_tokens, norm_mix_g, w_in, w_attn_out, conv_w, w_conv_out, w_out,
                                          norm_ffn_g, w_gate, w_up, w_down, norm_final_g)
    key = (SEQ, _dbg)
    if key not in _CACHE:
        _CACHE[key] = build(SEQ, _dbg)
    nc = _CACHE[key]
    res = run_bass_kernel_spmd(nc, in_maps, core_ids=list(range(8)))
    outp = np.zeros((B, SEQ, D), np.float32)
    for core in range(8):
        b, r = core // 4, core % 4
        o = np.asarray(res.results[core]["out"], np.float32)
        for m in range(NSLOT):
            qb = 4 * m + r
            outp[b, qb * 128:(qb + 1) * 128] = o[m]
    if _dbg:
        return outp, res.results
    return outp
```

```python
import numpy as np
from contextlib import ExitStack
import concourse.bass as bass
import concourse.mybir as mybir
from concourse.bass_utils import run_bass_kernel_spmd

F32 = mybir.dt.float32
BF16 = mybir.dt.bfloat16
U8 = mybir.dt.uint8
AF = mybir.ActivationFunctionType
ALU = mybir.AluOpType
AX = mybir.AxisListType

D = 1024
KC = 8
NH = 8
DFF = 2816
FC = DFF // 128
PROJ = 8776
C_Q, C_K, C_V, C_QI, C_KI, C_WI, C_CU, C_CB, C_CC, C_GA, C_GB = 0, 1024, 2048, 3072, 3584, 3648, 3656, 4680, 5704, 6728, 7752
EPS = 1e-6
IDX_SCALE = (8 ** -0.5) * (64 ** -0.5)
SM_SCALE = 128 ** -0.5
NEG = -30000.0
NBIS = 16
TOPK = 256.0
ACT_COLS = 0
B_HOLD = 0
A_HOLD = 18


class Buf:
    def __init__(self, name, t, const=False):
        self.name, self.t, self.const = name, t, const
        self.lw = None
        self.rd = []

    def __getitem__(self, k):
        return self.t[k]


class Op:
    __slots__ = ("eng", "fn", "deps", "dma", "key", "signal", "tok", "idx")

    def __init__(self, eng, fn, deps, dma, key):
        self.eng, self.fn, self.deps, self.dma, self.key = eng, fn, deps, dma, key
        self.signal = False
        self.tok = None


class Prog:
    ENGS = ["pe", "act", "dve", "pool", "sp"]

    def __init__(self, nc):
        self.nc = nc
        self.ops = []
        self.last = {e: None for e in self.ENGS}
        self.dma_since = {}

    def op(self, eng, fn, r=(), w=(), dma=False, key=None):
        deps = []
        for b in r:
            if b.lw is not None:
                deps.append(b.lw)
        for b in w:
            if b.lw is not None:
                deps.append(b.lw)
            lastr = {}
            for d in b.rd:
                if d.dma:
                    deps.append(d)
                else:
                    lastr[d.eng] = d
            deps.extend(lastr.values())
        o = Op(eng, fn, None, dma, key)
        dd = []
        seen = set()
        for d in deps:
            if id(d) in seen:
                continue
            seen.add(id(d))
            if d.eng == "pe" and eng == "pe" and not d.dma and not dma:
                continue
            dd.append(d)
        o.deps = dd
        for b in r:
            if not b.const:
                b.rd.append(o)
        for b in w:
            b.lw = o
            b.rd = []
        self.ops.append(o)
        self.last[eng] = o
        if dma:
            self.dma_since[key] = o
        return o

    def dma(self, eng, fn, r=(), w=(), key=None):
        if key is None:
            key = w[0].name
        return self.op(eng, fn, r, w, dma=True, key=key)

    def barrier(self):
        lasts = [o for o in self.last.values() if o is not None] + list(self.dma_since.values())
        self.dma_since = {}
        new = []
        for e in self.ENGS:
            o = Op(e, (lambda en: en.nop()), [d for d in lasts], False, None)
            new.append(o)
        for o in new:
            self.ops.append(o)
            self.last[o.eng] = o

    def lower(self, es):
        nc = self.nc
        for o in self.ops:
            for d in o.deps:
                d.signal = True
        EPOCH = 12000
        eng_sem = {}
        eng_cnt = {}
        dma_sem = {}
        dma_cnt = {}

        def new_sem(nm):
            return es.enter_context(nc.semaphore(nm))

        nsem = [0]
        for o in self.ops:
            if o.dma:
                if o.key not in dma_sem or dma_cnt[o.key] > 28000:
                    nsem[0] += 1
                    dma_sem[o.key] = new_sem("d%d" % nsem[0])
                    dma_cnt[o.key] = 0
                o.tok = [dma_sem[o.key], dma_cnt[o.key]]
            elif o.signal:
                if o.eng not in eng_sem or eng_cnt[o.eng] >= EPOCH:
                    nsem[0] += 1
                    eng_sem[o.eng] = new_sem("e%d" % nsem[0])
                    eng_cnt[o.eng] = 0
                eng_cnt[o.eng] += 1
                o.tok = [eng_sem[o.eng], eng_cnt[o.eng]]
            if o.dma:
                n = getattr(o.fn, "ndma", 1)
                dma_cnt[o.key] += 16 * n
                o.tok = [dma_sem[o.key], dma_cnt[o.key]]
        per = {e: [o for o in self.ops if o.eng == e] for e in self.ENGS}
        block = es.enter_context(nc.Block())

        def emit(e, lst):
            seen = {}
            for o in lst:
                for d in o.deps:
                    s, v = d.tok
                    k = id(s)
                    if seen.get(k, 0) < v:
                        e.wait_ge(s, v)
                        seen[k] = v
                ins = o.fn(e)
                if o.dma:
                    if not isinstance(ins, (list, tuple)):
                        ins = [ins]
                    assert len(ins) == getattr(o.fn, "ndma", 1)
                    for i in ins:
                        i.then_inc(o.tok[0], 16)
                elif o.signal:
                    if isinstance(ins, (list, tuple)):
                        ins = ins[-1]
                    ins.then_inc(o.tok[0], 1)

        @block.tensor
        def _(e):
            emit(e, per["pe"])

        @block.scalar
        def _(e):
            emit(e, per["act"])

        @block.vector
        def _(e):
            emit(e, per["dve"])

        @block.gpsimd
        def _(e):
            emit(e, per["pool"])

        @block.sync
        def _(e):
            emit(e, per["sp"])


def multi(n, f):
    f.ndma = n
    return f


def build(SEQ, dbg=False):
    NXC = SEQ // 128
    NCH = NXC + 1
    NSLOT = NXC // 4
    TT = 4
    NSUP = NSLOT // TT
    SMAX = NCH * 128

    nc = bass.Bass("TRN2", target_bir_lowering=False)
    P = Prog(nc)

    def din(name, shape, dt=F32):
        return nc.dram_tensor(name, list(shape), dt, kind="ExternalInput").ap()

    x_all = din("x_all", [NCH * 128, D])
    x_own = din("x_own", [NSLOT, 128, D])
    x_halo = din("x_halo", [NSLOT * 2, D])
    cbx_in = din("cbx", [128, 512])
    cbm_in = din("cbm", [128, 128])
    w_in = din("w_in", [D, PROJ])
    w_ao = din("w_attn_out", [D, D])
    w_co = din("w_conv_out", [D, D])
    w_o = din("w_out", [D, D])
    w_g = din("w_gate", [D, DFF])
    w_u = din("w_up", [D, DFF])
    w_d = din("w_down", [DFF, D])
    gmix_in = din("g_mix", [128, KC])
    gffn_in = din("g_ffn", [128, KC])
    gfin_in = din("g_fin", [128, D])
    cw_in = din("conv_w", [128, KC * 3])
    ident_in = din("ident", [128, 128])
    out = nc.dram_tensor("out", [NSLOT, 128, D], F32, kind="ExternalOutput").ap()

    skind = "ExternalOutput" if dbg else "Internal"

    def dscr(name, shape, dt=BF16, kind=None):
        t = nc.dram_tensor(name, list(shape), dt, kind=kind or skind).ap()
        return Buf("dr_" + name, t)

    Wb_in = dscr("Wb_in", [128, KC, PROJ], kind="Internal")
    Wb_ao = dscr("Wb_ao", [128, KC, D], kind="Internal")
    Wb_co = dscr("Wb_co", [128, KC, D], kind="Internal")
    Wb_o = dscr("Wb_o", [128, KC, D], kind="Internal")
    Wb_g = dscr("Wb_g", [128, KC, DFF], kind="Internal")
    Wb_u = dscr("Wb_u", [128, KC, DFF], kind="Internal")
    Wb_d = dscr("Wb_d", [128, 2, FC, 512], kind="Internal")
    KT_d = dscr("KT_d", [NCH, 128, NH * 128])
    V_d = dscr("V_d", [NCH, 128, NH * 129])
    ya_d = dscr("ya_d", [NSLOT, 128, KC * 128])

    top = ExitStack()

    def sb(es, name, shape, dt, const=False):
        t = es.enter_context(nc.sbuf_tensor("s_" + name, list(shape), dt))
        return Buf(name, t, const)

    def ps(es, name, shape, dt=F32):
        t = es.enter_context(nc.psum_tensor("p_" + name, list(shape), dt))
        return Buf(name, t)

    ident_f = sb(top, "ident_f", [128, 128], F32)
    ident = sb(top, "ident", [128, 128], BF16)
    ident4 = sb(top, "ident4", [128, 4, 128], BF16)
    gmix = sb(top, "gmix", [128, KC], F32)
    gffn = sb(top, "gffn", [128, KC], F32)
    P.dma("sp", lambda e: e.dma_start(out=ident_f[:], in_=ident_in[:, :]), w=[ident_f])
    P.dma("sp", lambda e: e.dma_start(out=gmix[:], in_=gmix_in[:, :]), w=[gmix])
    P.dma("sp", lambda e: e.dma_start(out=gffn[:], in_=gffn_in[:, :]), w=[gffn])
    P.op("dve", lambda e: e.tensor_copy(out=ident[:], in_=ident_f[:]), r=[ident_f], w=[ident])
    for i in range(4):
        P.op("dve", lambda e, i=i: e.tensor_copy(out=ident4[:, i, :], in_=ident_f[:]), r=[ident_f], w=[ident4])

    Wb_in2 = Buf("dr_Wb_in2", Wb_in.t)
    epsb = sb(top, "epsb", [128, 1], F32)
    P.op("dve", lambda e: e.memset(epsb[:], EPS), w=[epsb])
    es_ki = ExitStack()
    kiT = sb(es_ki, "kiT", [128, SMAX], BF16)
    es_p0 = ExitStack()
    NST = 3
    CBLK = 2816
    stg = [sb(es_p0, "p0s%d" % i, [128, CBLK], F32) for i in range(NST)]
    stb = [sb(es_p0, "p0b%d" % i, [128, CBLK], BF16) for i in range(NST)]
    p0cnt = [0]

    def prep(src, K, c_lo, c_hi, dst, g, dst_t=None):
        for kc in range(K // 128):
            for c0 in range(c_lo, c_hi, CBLK):
                cw = min(CBLK, c_hi - c0)
                i = p0cnt[0] % NST
                p0cnt[0] += 1
                s_, b_ = stg[i], stb[i]
                P.dma("sp", lambda e, s_=s_, kc=kc, c0=c0, cw=cw, src=src: e.dma_start(
                    out=s_[:, 0:cw], in_=src[kc * 128:(kc + 1) * 128, c0:c0 + cw]), w=[s_])
                if g is None:
                    if p0cnt[0] % 2 == 0:
                        P.op("act", lambda e, s_=s_, b_=b_, cw=cw: e.activation(out=b_[:, 0:cw], in_=s_[:, 0:cw], func=AF.Copy),
                             r=[s_], w=[b_])
                    else:
                        P.op("dve", lambda e, s_=s_, b_=b_, cw=cw: e.tensor_copy(out=b_[:, 0:cw], in_=s_[:, 0:cw]),
                             r=[s_], w=[b_])
                else:
                    if p0cnt[0] % 2 == 0:
                        P.op("act", lambda e, s_=s_, b_=b_, cw=cw, g=g, kc=kc: e.activation(
                            out=b_[:, 0:cw], in_=s_[:, 0:cw], func=AF.Copy, scale=g[:, kc:kc + 1]), r=[s_, g], w=[b_])
                    else:
                        P.op("dve", lambda e, s_=s_, b_=b_, cw=cw, g=g, kc=kc: e.tensor_scalar(
                            out=b_[:, 0:cw], in0=s_[:, 0:cw], scalar1=g[:, kc:kc + 1], scalar2=None, op0=ALU.mult),
                            r=[s_, g], w=[b_])
                if dst is Wb_d:
                    P.dma("pool", lambda e, b_=b_, kc=kc, dst=dst: e.dma_start(
                        out=dst[:, :, kc, :], in_=b_[:, 0:D].rearrange("p (g c) -> p g c", c=512)), r=[b_], w=[dst], key=dst.name)
                else:
                    P.dma("pool", lambda e, b_=b_, kc=kc, c0=c0, cw=cw, dst=dst: e.dma_start(
                        out=dst[:, kc, c0:c0 + cw], in_=b_[:, 0:cw]), r=[b_], w=[dst], key=dst.name)
                yield

    def prep_rest():
        yield from prep(w_in, D, C_CU, PROJ, Wb_in2, gmix)
        yield from prep(w_ao, D, 0, D, Wb_ao, None)
        yield from prep(w_co, D, 0, D, Wb_co, None)
        yield from prep(w_o, D, 0, D, Wb_o, None)
        yield from prep(w_g, D, 0, DFF, Wb_g, gffn)
        yield from prep(w_u, D, 0, DFF, Wb_u, gffn)
        yield from prep(w_d, DFF, 0, D, Wb_d, None)

    for _ in prep(w_in, D, 0, C_CU, Wb_in, gmix):
        pass
    g_rest = prep_rest()

    def rms_transpose(es_bufs, x_src_ap, nrows, aT, col0, keep_x=None):
        rms_pre(es_bufs, x_src_ap, nrows)
        rms_tr(es_bufs, nrows, aT, col0)

    def rms_tr(es_bufs, nrows, aT, col0):
        xt, sq_junk, ss, rstd, a_bf, tp = es_bufs
        dfn = lambda hf: aT[:, hf * 4:(hf + 1) * 4, col0:col0 + nrows]
        dfn.buf = aT
        pe_transpose(a_bf, nrows, tp, dfn)

    def rms_pre(es_bufs, x_src_ap, nrows):
        xt, sq_junk, ss, rstd, a_bf, tp = es_bufs
        P.dma("sp", lambda e: e.dma_start(out=xt[0:nrows, :], in_=x_src_ap), w=[xt])
        P.op("act", lambda e: e.activation(out=sq_junk[0:nrows, :], in_=xt[0:nrows, :], func=AF.Square,
                                           accum_out=ss[0:nrows, 0:1]), r=[xt], w=[sq_junk, ss])
        P.op("act", lambda e: e.activation(out=rstd[0:nrows, 0:1], in_=ss[0:nrows, 0:1], func=AF.Sqrt,
                                           scale=1.0 / D, bias=epsb[0:nrows, 0:1]), r=[ss, epsb], w=[rstd])
        P.op("dve", lambda e: e.reciprocal(out=rstd[0:nrows, 0:1], in_=rstd[0:nrows, 0:1]), r=[rstd], w=[rstd])
        P.op("dve", lambda e: e.tensor_scalar(out=a_bf[0:nrows, :], in0=xt[0:nrows, :], scalar1=rstd[0:nrows, 0:1],
                                              scalar2=None, op0=ALU.mult), r=[xt, rstd], w=[a_bf])

    def pe_transpose(src, nrows, tp, dst_fn):
        for hf in range(2):
            t_ = tp[hf]
            for c4 in range(4):
                kc = hf * 4 + c4
                P.op("pe", lambda e, t_=t_, c4=c4, kc=kc: e.matmul(
                    t_[:, c4 * 128:c4 * 128 + nrows], lhsT=src[0:nrows, kc * 128:(kc + 1) * 128], rhs=ident[0:nrows, 0:nrows],
                    start=True, stop=True), r=[src, ident], w=[t_])
            tv = t_[:].rearrange("p (c t) -> p c t", t=128)
            if hf == 0:
                P.op("act", lambda e, tv=tv, hf=hf: e.activation(out=dst_fn(hf), in_=tv[:, :, 0:nrows], func=AF.Copy), r=[t_], w=[dst_fn.buf])
            else:
                P.op("dve", lambda e, tv=tv, hf=hf: e.tensor_copy(out=dst_fn(hf), in_=tv[:, :, 0:nrows]), r=[t_], w=[dst_fn.buf])


    def rms_bufs(es, tag, tp):
        return (sb(es, "xt" + tag, [128, D], F32), sb(es, "sqj" + tag, [128, D], BF16), sb(es, "ss" + tag, [128, 1], F32),
                sb(es, "rstd" + tag, [128, 1], F32), sb(es, "abf" + tag, [128, D], BF16), tp)


    with ExitStack() as es:
        Wkv = sb(es, "Wkv", [128, KC, 2176], BF16)
        P.dma("sp", multi(3, lambda e: [
            e.dma_start(out=Wkv[:, :, 0:2048], in_=Wb_in[:, :, C_K:C_K + 2048]),
            e.dma_start(out=Wkv[:, :, 2048:2112], in_=Wb_in[:, :, C_KI:C_KI + 64]),
            e.dma_start(out=Wkv[:, :, 2112:2176], in_=Wb_in[:, :, C_KI:C_KI + 64])]), r=[Wb_in], w=[Wkv])
        tp1 = [ps(es, "tp1_%d" % i, [128, 512]) for i in range(2)]
        xts1 = [sb(es, "xt1_%d" % i, [128, D], F32) for i in range(4)]
        sqj1 = [sb(es, "sqj1_%d" % i, [128, D], BF16) for i in range(2)]
        rb = [(xts1[i % 4], sqj1[i % 2], sb(es, "ss1_%d" % i, [128, 1], F32), sb(es, "rstd1_%d" % i, [128, 1], F32),
               sb(es, "abf1_%d" % i, [128, D], BF16), tp1) for i in range(8)]
        aTs = [sb(es, "aTs%d" % i, [128, KC, 512], BF16) for i in range(2)]
        KTst = [sb(es, "KTst%d" % i, [128, 4, NH, 128], BF16) for i in range(2)]
        Vst = [sb(es, "Vst%d" % i, [128, NH, 129], BF16) for i in range(2)]
        for v in Vst:
            P.op("dve", lambda e, v=v: e.memset(v[:, :, 128:129], 1.0), w=[v])
        pk = [ps(es, "pk%d" % i, [128, 512]) for i in range(2)]
        pv = [ps(es, "pv%d" % i, [128, 512]) for i in range(2)]
        pki = ps(es, "pki", [128, 512])
        pkc = 0

        def pre_tile(j):
            if j < NCH:
                rms_pre(rb[j % 8], x_all[j * 128:(j + 1) * 128, :], 128)
                next(g_rest, None)

        for j in range(min(4, NCH)):
            pre_tile(j)
        for g0 in range(0, NCH, 4):
            nt = min(4, NCH - g0)
            N = nt * 128
            aT = aTs[(g0 // 4) % 2]
            for jj in range(nt):
                rms_tr(rb[(g0 + jj) % 8], 128, aT, jj * 128)
            KS = KTst[(g0 // 4) % 2]
            for h in range(NH):
                if h % 2 == 1:
                    pre_tile(g0 + 4 + h // 2)
                pb = pk[pkc % 2]
                pkc += 1
                for kc in range(KC):
                    P.op("pe", lambda e, pb=pb, kc=kc, h=h, aT=aT, N=N: e.matmul(
                        pb[:, 0:N], lhsT=Wkv[:, kc, h * 128:(h + 1) * 128], rhs=aT[:, kc, 0:N],
                        start=(kc == 0), stop=(kc == KC - 1)), r=[Wkv, aT], w=[pb])
                pbv = pb[:].rearrange("p (j s) -> p j s", s=128)
                if h % 2 == 0:
                    P.op("act", lambda e, pbv=pbv, KS=KS, h=h, nt=nt: e.activation(out=KS[:, 0:nt, h, :], in_=pbv[:, 0:nt, :],
                                                                                func=AF.Copy), r=[pb], w=[KS])
                else:
                    P.op("dve", lambda e, pbv=pbv, KS=KS, h=h, nt=nt: e.tensor_copy(out=KS[:, 0:nt, h, :], in_=pbv[:, 0:nt, :]),
                         r=[pb], w=[KS])
            P.dma("pool", lambda e, KS=KS, g0=g0, nt=nt: e.dma_start(
                out=KT_d[g0:g0 + nt].rearrange("j d f -> d j f"),
                in_=KS[:, 0:nt].rearrange("p j h s -> p j (h s)")), r=[KS], w=[KT_d], key="dr_KT_d")
            for kc in range(KC):
                P.op("pe", lambda e, kc=kc, aT=aT, N=N: e.matmul(pki[:, 0:N], lhsT=Wkv[:, kc, 2048:2176], rhs=aT[:, kc, 0:N],
                                                              start=(kc == 0), stop=(kc == KC - 1)), r=[Wkv, aT], w=[pki])
            P.op("act", lambda e, g0=g0, N=N: e.activation(out=kiT[:, g0 * 128:g0 * 128 + N], in_=pki[:, 0:N], func=AF.Copy),
                 r=[pki], w=[kiT])
            for jj in range(nt):
                VS = Vst[(g0 + jj) % 2]
                for half in range(2):
                    pb = pv[half]
                    for kc in range(KC):
                        P.op("pe", lambda e, pb=pb, kc=kc, half=half, aT=aT, jj=jj: e.matmul(
                            pb[:, :], lhsT=aT[:, kc, jj * 128:(jj + 1) * 128],
                            rhs=Wkv[:, kc, 1024 + half * 512:1024 + (half + 1) * 512],
                            start=(kc == 0), stop=(kc == KC - 1)), r=[Wkv, aT], w=[pb])
                    pbv = pb[:].rearrange("p (h d) -> p h d", d=128)
                    if half == 0:
                        P.op("act", lambda e, pbv=pbv, VS=VS: e.activation(out=VS[:, 0:4, 0:128], in_=pbv, func=AF.Copy),
                             r=[pb], w=[VS])
                    else:
                        P.op("dve", lambda e, pbv=pbv, VS=VS: e.tensor_copy(out=VS[:, 4:8, 0:128], in_=pbv), r=[pb], w=[VS])
                P.dma("pool", lambda e, VS=VS, j=g0 + jj: e.dma_start(out=V_d[j], in_=VS[:].rearrange("p h d -> p (h d)")),
                      r=[VS], w=[V_d], key="dr_V_d")
        for _ in g_rest:
            pass
        P.barrier()
    es_p0.close()

    q_d = dscr("q_d", [NSLOT, 128, NH * 128], kind="Internal")
    qi_d = dscr("qi_d", [NSLOT, 128, NH * 128], kind="Internal")
    dg_d = dscr("dg_d", [NSLOT, 128, NH * 128], kind="Internal")
    with ExitStack() as es:
        Wq = sb(es, "Wq", [128, KC, 1024 + 512], BF16)
        P.dma("sp", multi(2, lambda e: [
            e.dma_start(out=Wq[:, :, 0:1024], in_=Wb_in[:, :, C_Q:C_Q + 1024]),
            e.dma_start(out=Wq[:, :, 1024:1536], in_=Wb_in[:, :, C_QI:C_QI + 512])]), r=[Wb_in], w=[Wq])
        Ww = sb(es, "Ww", [128, KC, 8], BF16)
        P.dma("sp", lambda e: e.dma_start(out=Ww[:], in_=Wb_in[:, :, C_WI:C_WI + 8]), r=[Wb_in], w=[Ww])
        tpa = [ps(es, "tpa%d" % i, [128, 512]) for i in range(2)]
        rba = [rms_bufs(es, "qa", tpa), rms_bufs(es, "qb", tpa)]
        G4 = 4
        NG = NSLOT // G4
        aT4 = [sb(es, "aT4_%d" % i, [128, KC, G4 * 128], BF16) for i in range(2)]
        qT4 = [sb(es, "qT4_%d" % i, [128, G4, NH, 128], BF16) for i in range(2)]
        qiT4 = [sb(es, "qiT4_%d" % i, [128, G4, 4, 2, 128], BF16) for i in range(2)]
        for q_ in qiT4:
            P.op("dve", lambda e, q_=q_: e.memset(q_[:], 0.0), w=[q_])
        Dgs = [sb(es, "Dga%d" % i, [128, NH, 128], BF16) for i in range(2)]
        wsb4 = [sb(es, "wsb4_%d" % i, [128, G4 * 8], F32) for i in range(2)]
        pqa = [ps(es, "pqa%d" % i, [128, 512]) for i in range(2)]
        pw = ps(es, "pw", [128, 512])
        rms_pre(rba[0], x_own[0], 128)
        pcnt = 0
        for g in range(NG):
            aT, qTg, qiTg, wsbg = aT4[g % 2], qT4[g % 2], qiT4[g % 2], wsb4[g % 2]
            for j in range(G4):
                m = g * G4 + j
                rms_tr(rba[m % 2], 128, aT, j * 128)
                if m + 1 < NSLOT:
                    rms_pre(rba[(m + 1) % 2], x_own[m + 1], 128)
            for h in range(NH):
                pb = pqa[pcnt % 2]
                pcnt += 1
                for kc in range(KC):
                    P.op("pe", lambda e, pb=pb, h=h, kc=kc, aT=aT: e.matmul(
                        pb[:, :], lhsT=Wq[:, kc, h * 128:(h + 1) * 128], rhs=aT[:, kc, :],
                        start=(kc == 0), stop=(kc == KC - 1)), r=[Wq, aT], w=[pb])
                pbv = pb[:].rearrange("p (b t) -> p b t", t=128)
                if h % 2 == 0:
                    P.op("act", lambda e, pbv=pbv, h=h, qTg=qTg: e.activation(out=qTg[:, :, h, :], in_=pbv, func=AF.Copy),
                         r=[pb], w=[qTg])
                else:
                    P.op("dve", lambda e, pbv=pbv, h=h, qTg=qTg: e.tensor_copy(out=qTg[:, :, h, :], in_=pbv), r=[pb], w=[qTg])
            for c in range(4):
                pb = pqa[pcnt % 2]
                pcnt += 1
                for kc in range(KC):
                    P.op("pe", lambda e, pb=pb, c=c, kc=kc, aT=aT: e.matmul(
                        pb[:, :], lhsT=Wq[:, kc, 1024 + c * 128:1024 + (c + 1) * 128], rhs=aT[:, kc, :],
                        start=(kc == 0), stop=(kc == KC - 1)), r=[Wq, aT], w=[pb])
                pbv = pb[:].rearrange("p (b t) -> p b t", t=128)
                P.op("act", lambda e, pbv=pbv, c=c, qiTg=qiTg: e.activation(out=qiTg[0:64, :, c, 0, :], in_=pbv[0:64, :, :],
                                                                          func=AF.Copy), r=[pb], w=[qiTg])
                P.op("dve", lambda e, pbv=pbv, c=c, qiTg=qiTg: e.tensor_copy(out=qiTg[64:128, :, c, 1, :], in_=pbv[64:128, :, :]),
                     r=[pb], w=[qiTg])
            for j in range(G4):
                for kc in range(KC):
                    P.op("pe", lambda e, j=j, kc=kc, aT=aT: e.matmul(pw[:, j * 8:(j + 1) * 8], lhsT=aT[:, kc, j * 128:(j + 1) * 128],
                                                                    rhs=Ww[:, kc, :], start=(kc == 0), stop=(kc == KC - 1)),
                         r=[Ww, aT], w=[pw])
            P.op("dve", lambda e, wsbg=wsbg: e.tensor_scalar(out=wsbg[:], in0=pw[:, 0:G4 * 8], scalar1=IDX_SCALE, scalar2=None,
                                                            op0=ALU.mult), r=[pw], w=[wsbg])
            for j in range(G4):
                m = g * G4 + j
                Dg = Dgs[m % 2]
                for h in range(NH):
                    col = j * 8 + h
                    if h % 2 == 0:
                        P.op("dve", lambda e, h=h, Dg=Dg, wsbg=wsbg, col=col: e.tensor_scalar(
                            out=Dg[:, h, :], in0=ident_f[:], scalar1=wsbg[:, col:col + 1], scalar2=None, op0=ALU.mult),
                            r=[ident_f, wsbg], w=[Dg])
                    else:
                        P.op("act", lambda e, h=h, Dg=Dg, wsbg=wsbg, col=col: e.activation(
                            out=Dg[:, h, :], in_=ident_f[:], func=AF.Copy, scale=wsbg[:, col:col + 1]), r=[ident_f, wsbg], w=[Dg])
                P.dma("pool", lambda e, m=m, j=j, qTg=qTg: e.dma_start(out=q_d[m], in_=qTg[:, j].rearrange("p h t -> p (h t)")),
                      r=[qTg], w=[q_d], key="dr_q_d")
                P.dma("pool", lambda e, m=m, j=j, qiTg=qiTg: e.dma_start(out=qi_d[m], in_=qiTg[:, j].rearrange("p c two t -> p (c two t)")),
                      r=[qiTg], w=[qi_d], key="dr_qi_d")
                P.dma("pool", lambda e, m=m, Dg=Dg: e.dma_start(out=dg_d[m], in_=Dg[:].rearrange("p h t -> p (h t)")),
                      r=[Dg], w=[dg_d], key="dr_dg_d")
        P.barrier()

    with ExitStack() as es:
        cbx = sb(es, "cbx", [128, 512], F32)
        cbm = sb(es, "cbm", [128, 128], F32)
        P.dma("sp", lambda e: e.dma_start(out=cbx[:], in_=cbx_in[:, :]), w=[cbx])
        P.dma("sp", lambda e: e.dma_start(out=cbm[:], in_=cbm_in[:, :]), w=[cbm])
        qTs = [sb(es, "qT%d" % i, [128, NH, 128], BF16) for i in range(2)]
        qiTs = [sb(es, "qiT%d" % i, [128, NH, 128], BF16) for i in range(2)]
        Dgs = [sb(es, "Dg%d" % i, [128, NH, 128], BF16) for i in range(2)]
        score = sb(es, "score", [128, SMAX], F32)
        MBs = [sb(es, "MB%d" % i, [128, SMAX], BF16) for i in range(2)]
        MBas = [Buf("MBact%d" % i, MBs[i].t) for i in range(2)]
        sacc = sb(es, "sacc", [128, 1], F32)
        cnte = sb(es, "cnte", [128, 1], F32)
        NR = 3
        Rsb = [sb(es, "Rsb%d" % i, [128, 512], BF16) for i in range(NR)]
        stile = [sb(es, "stile%d" % i, [128, 512], F32) for i in range(2)]
        score_d = [dscr("score_d%d" % i, [128, SMAX], F32, kind="Internal") for i in range(2)]
        PT = [sb(es, "PT%d" % i, [128, 512], BF16) for i in range(4)]
        NKV = 3
        KTs = [sb(es, "KTs%d" % i, [128, NH * 128], BF16) for i in range(NKV)]
        Vs = [sb(es, "Vs%d" % i, [128, NH * 129], BF16) for i in range(NKV)]
        amax = sb(es, "amax", [128, 1], F32)
        hk = sb(es, "hk", [128, NBIS + 2], F32)
        p2 = sb(es, "p2", [128, NBIS + 2], F32)
        mid = sb(es, "mid", [128, 1], F32)
        cntb = sb(es, "cntb", [128, 1], F32)
        ub = sb(es, "ub", [128, 1], F32)
        tau = sb(es, "tau", [128, 1], F32)
        rden = sb(es, "rden", [128, NH], F32)
        yat = sb(es, "yat", [128, D], BF16)
        yT = sb(es, "yT", [128, D], BF16)
        for k in range(NBIS + 2):
            P.op("dve", lambda e, k=k: e.memset(p2[:, k:k + 1], 2.0 ** (-k)), w=[p2])
        Rps = [ps(es, "Rps%d" % i, [128, 512]) for i in range(2)]
        SCps = ps(es, "SCps", [128, 512])
        Lps = [ps(es, "Lps%d" % i, [128, 512]) for i in range(2)]
        Ops = [ps(es, "Ops%d" % i, [128, 512]) for i in range(3)]
        cnts = {"r": 0, "kv": 0, "l": 0, "st": 0}

        def chunks_of(m):
            return list(range(4 * m + 4)) + [NXC]

        def load_qT(m):
            P.dma("sp", lambda e: e.dma_start(out=qTs[m % 2][:].rearrange("p h t -> p (h t)"), in_=q_d[m]), r=[q_d], w=[qTs[m % 2]])

        def load_qi(m):
            P.dma("sp", lambda e: e.dma_start(out=qiTs[m % 2][:].rearrange("p h t -> p (h t)"), in_=qi_d[m]), r=[qi_d], w=[qiTs[m % 2]])
            P.dma("sp", lambda e: e.dma_start(out=Dgs[m % 2][:].rearrange("p h t -> p (h t)"), in_=dg_d[m]), r=[dg_d], w=[Dgs[m % 2]])

        def stage_I(m):
            qiT, Dg = qiTs[m % 2], Dgs[m % 2]
            par = m % 2
            nxg = m + 1
            groups = [(g * 512, 512, g * 512) for g in range(nxg)] + [(NXC * 128, 128, nxg * 512)]
            for gi, (k0, N, s0) in enumerate(groups):
                pend = None
                for h in range(NH):
                    rp = Rps[cnts["r"] % 2]
                    rs = Rsb[cnts["r"] % NR]
                    cnts["r"] += 1
                    P.op("pe", lambda e, rp=rp, h=h, k0=k0, N=N: e.matmul(
                        rp[:, 0:N], lhsT=qiT[:, h, :], rhs=kiT[:, k0:k0 + N], start=True, stop=True),
                        r=[qiT, kiT], w=[rp])
                    P.op("act", lambda e, rp=rp, rs=rs, N=N: e.activation(out=rs[:, 0:N], in_=rp[:, 0:N], func=AF.Relu),
                         r=[rp], w=[rs])
                    if pend is not None:
                        ph_, prs = pend
                        P.op("pe", lambda e, prs=prs, ph_=ph_, N=N: e.matmul(SCps[:, 0:N], lhsT=Dg[:, ph_, :], rhs=prs[:, 0:N],
                                                                           start=(ph_ == 0), stop=False), r=[Dg, prs], w=[SCps])
                    pend = (h, rs)
                    yield
                ph_, prs = pend
                P.op("pe", lambda e, prs=prs, ph_=ph_, N=N: e.matmul(SCps[:, 0:N], lhsT=Dg[:, ph_, :], rhs=prs[:, 0:N],
                                                                   start=False, stop=True), r=[Dg, prs], w=[SCps])
                st = stile[cnts["st"] % 2]
                cnts["st"] += 1
                P.op("act", lambda e, st=st, N=N: e.activation(out=st[:, 0:N], in_=SCps[:, 0:N], func=AF.Copy), r=[SCps], w=[st])
                P.dma("pool", lambda e, st=st, s0=s0, N=N, par=par: e.dma_start(out=score_d[par][:, s0:s0 + N], in_=st[:, 0:N]),
                      r=[st], w=[score_d[par]], key=score_d[par].name)
                yield

        scq = [Buf("scq%d" % i, score.t) for i in range(4)]
        amax4 = sb(es, "amax4", [128, 4], F32)
        hk2 = sb(es, "hk2", [128, NBIS + 2], F32)
        thr0 = sb(es, "thr0", [128, 1], F32)

        def quarters(m):
            S = len(chunks_of(m)) * 128
            return [(i * S // 4 // 128) * 128 for i in range(4)] + [S]

        def reload(m):
            par = m % 2
            q4 = quarters(m)
            for i in range(4):
                P.dma("sp", lambda e, i=i: e.dma_start(out=score[:, q4[i]:q4[i + 1]], in_=score_d[par][:, q4[i]:q4[i + 1]]),
                      r=[score_d[par]], w=[scq[i]], key="scq%d" % i)

        def stage_B(m):
            S = len(chunks_of(m)) * 128
            MB, MBa = MBs[m % 2], MBas[m % 2]
            n_act = min(ACT_COLS, (S // 3 // 128) * 128)
            c0 = S - n_act
            nxg = m + 1
            q4 = quarters(m)
            for i in range(4):
                P.op("dve", lambda e, i=i: e.tensor_reduce(out=amax4[:, i:i + 1], in_=score[:, q4[i]:q4[i + 1]], axis=AX.X, op=ALU.max,
                                                           apply_absolute_value=True), r=[scq[i]], w=[amax4])
            P.op("dve", lambda e: e.tensor_reduce(out=amax[:], in_=amax4[:], axis=AX.X, op=ALU.max), r=[amax4], w=[amax])
            sx = (nxg - 1) * 512
            P.op("dve", lambda e, sx=sx: e.tensor_tensor(out=score[:, sx:sx + 512], in0=score[:, sx:sx + 512], in1=cbx[:],
                                                         op=ALU.add), r=scq + [cbx], w=scq)
            sm = nxg * 512
            P.op("dve", lambda e, sm=sm: e.tensor_tensor(out=score[:, sm:sm + 128], in0=score[:, sm:sm + 128], in1=cbm[:],
                                                         op=ALU.add), r=scq + [cbm], w=scq)
            P.op("dve", lambda e: e.tensor_scalar(out=amax[:], in0=amax[:], scalar1=1.001, scalar2=1e-3, op0=ALU.mult,
                                                  op1=ALU.add), r=[amax], w=[amax])
            P.op("dve", lambda e: e.tensor_scalar(out=hk[:], in0=p2[:], scalar1=amax[:, 0:1], scalar2=None, op0=ALU.mult),
                 r=[p2, amax], w=[hk])
            P.op("dve", lambda e: e.tensor_copy(out=mid[:], in_=hk[:, NBIS + 1:NBIS + 2]), r=[hk], w=[mid])
            P.op("dve", lambda e: e.tensor_scalar(out=hk2[:], in0=hk[:], scalar1=2.0, scalar2=None, op0=ALU.mult), r=[hk], w=[hk2])
            P.op("dve", lambda e, n_act=n_act: e.memset(thr0[:], TOPK - 0.5 - 0.5 * n_act), w=[thr0])
            if n_act == 0:
                P.op("dve", lambda e: e.memset(cnte[:], TOPK - 0.5), w=[cnte])
            yield
            for k in range(NBIS):
                if n_act > 0:
                    P.op("act", lambda e, S=S, c0=c0: e.activation(out=MBa[:, c0:S], in_=score[:, c0:S], func=AF.Sign, scale=-1.0,
                                                                   bias=mid[:, 0:1], accum_out=sacc[:, 0:1]),
                         r=scq + [mid], w=[MBa, sacc])
                    P.op("act", lambda e, n_act=n_act: e.activation(out=cnte[:], in_=sacc[:], func=AF.Identity, scale=0.5,
                                                                    bias=thr0[:, 0:1]), r=[sacc, thr0], w=[cnte])
                hcol = k + 1 if k < NBIS - 1 else k
                P.op("dve", lambda e, hcol=hcol: e.tensor_scalar(out=tau[:], in0=mid[:], scalar1=hk[:, hcol:hcol + 1], scalar2=None,
                                                                 op0=ALU.subtract), r=[mid, hk], w=[tau])
                P.op("dve", lambda e, c0=c0: e.tensor_scalar(out=MB[:, 0:c0], in0=score[:, 0:c0], scalar1=mid[:, 0:1], scalar2=None,
                                                             op0=ALU.is_ge, op1=ALU.add, accum_out=cntb[:, 0:1]),
                     r=scq + [mid], w=[MB, cntb])
                if k < NBIS - 1:
                    P.op("dve", lambda e, k=k: e.tensor_scalar(out=ub[:], in0=cntb[:], scalar1=cnte[:, 0:1], scalar2=hk2[:, k + 1:k + 2],
                                                               op0=ALU.is_ge, op1=ALU.mult), r=[cntb, cnte, hk2], w=[ub])
                    P.op("dve", lambda e: e.tensor_tensor(out=mid[:], in0=ub[:], in1=tau[:], op=ALU.add), r=[ub, tau], w=[mid])
                else:
                    P.op("dve", lambda e, k=k: e.tensor_scalar(out=ub[:], in0=cntb[:], scalar1=cnte[:, 0:1], scalar2=hk[:, k:k + 1],
                                                               op0=ALU.is_ge, op1=ALU.mult), r=[cntb, cnte, hk], w=[ub])
                    P.op("dve", lambda e: e.tensor_tensor(out=tau[:], in0=ub[:], in1=tau[:], op=ALU.add), r=[ub, tau], w=[tau])
                    P.op("dve", lambda e, S=S: e.tensor_scalar(out=MB[:, 0:S], in0=score[:, 0:S], scalar1=tau[:, 0:1], scalar2=NEG,
                                                               op0=ALU.is_lt, op1=ALU.mult), r=scq + [tau], w=[MB, MBa])
                yield

        def stage_A(m):
            qT, MB, MBa = qTs[m % 2], MBs[m % 2], MBas[m % 2]
            chunks = chunks_of(m)
            nchunks = len(chunks)
            prev = None

            def emit_pv(ci, pts, vv, half):
                vv3 = vv[:].rearrange("p (h d) -> p h d", d=129)
                pt = pts[half]
                for hh in range(4):
                    h = half * 4 + hh
                    ob = Ops[h // 3]
                    o0 = (h % 3) * 129
                    P.op("pe", lambda e, ob=ob, o0=o0, pt=pt, hh=hh, h=h, vv3=vv3, ci=ci: e.matmul(
                        ob[:, o0:o0 + 129], lhsT=pt[:, hh * 128:(hh + 1) * 128], rhs=vv3[:, h, :],
                        start=(ci == 0 and h % 3 == 0), stop=(ci == nchunks - 1), skip_group_check=True),
                        r=[pt, vv], w=[ob])

            for ci, j in enumerate(chunks):
                kt = KTs[cnts["kv"] % NKV]
                vv = Vs[cnts["kv"] % NKV]
                cnts["kv"] += 1
                P.dma("sp", lambda e, kt=kt, j=j: e.dma_start(out=kt[:], in_=KT_d[j]), r=[KT_d], w=[kt])
                P.dma("sp", lambda e, vv=vv, j=j: e.dma_start(out=vv[:], in_=V_d[j]), r=[V_d], w=[vv])
                pts = []
                for half in range(2):
                    lp = Lps[half]
                    pt = PT[(cnts["l"] % 2) * 2 + half]
                    pts.append(pt)
                    P.op("pe", lambda e, lp=lp, ci=ci: e.matmul(
                        lp[:, :], lhsT=MB[:, ci * 128:(ci + 1) * 128], rhs=ident4[:].rearrange("p a t -> p (a t)"),
                        start=True, stop=False, skip_group_check=True), r=[MB, MBa, ident4], w=[lp])
                    for hh in range(4):
                        h = half * 4 + hh
                        P.op("pe", lambda e, lp=lp, hh=hh, h=h, kt=kt: e.matmul(
                            lp[:, hh * 128:(hh + 1) * 128], lhsT=kt[:, h * 128:(h + 1) * 128], rhs=qT[:, h, :],
                            start=False, stop=(hh == 3), skip_group_check=True), r=[kt, qT], w=[lp])
                    P.op("act", lambda e, lp=lp, pt=pt: e.activation(out=pt[:], in_=lp[:], func=AF.Exp, scale=SM_SCALE),
                         r=[lp], w=[pt])
                    yield
                    if prev is not None:
                        emit_pv(prev[0], prev[1], prev[2], half)
                        yield
                cnts["l"] += 1
                prev = (ci, pts, vv)
            emit_pv(prev[0], prev[1], prev[2], 0)
            emit_pv(prev[0], prev[1], prev[2], 1)
            yield

        def stage_N(m):
            for b3 in range(3):
                nh3 = 3 if b3 < 2 else 2
                ov = Ops[b3][:, 0:nh3 * 129].rearrange("p (h d) -> p h d", d=129)
                P.op("dve", lambda e, ov=ov, b3=b3, nh3=nh3: e.reciprocal(out=rden[:, b3 * 3:b3 * 3 + nh3], in_=ov[:, :, 128]),
                     r=[Ops[b3]], w=[rden])
            for h in range(NH):
                ob = Ops[h // 3]
                o0 = (h % 3) * 129
                P.op("act", lambda e, ob=ob, o0=o0, h=h: e.activation(out=yat[:, h * 128:(h + 1) * 128], in_=ob[:, o0:o0 + 128],
                                                                    func=AF.Copy, scale=rden[:, h:h + 1]), r=[ob, rden], w=[yat])
            yT3 = yT[:].rearrange("p (c t) -> p c t", t=128)
            dfn = lambda hf: yT3[:, hf * 4:(hf + 1) * 4, :]
            dfn.buf = yT
            pe_transpose(yat, 128, Lps, dfn)
            P.dma("pool", lambda e, m=m: e.dma_start(out=ya_d[m], in_=yT[:]), r=[yT], w=[ya_d], key="dr_ya_d")

        def n_I(m):
            return (m + 2) * (NH + 1)

        def n_A(m):
            return 4 * len(chunks_of(m)) - 1

        def drive(gI, nI, gA, nA, gB, nB):
            accA = accB = 0.0
            hold = min(B_HOLD, nI // 3)
            if gB is not None:
                next(gB, None)
            holdA = min(A_HOLD, nI // 3)
            for it, _ in enumerate(gI):
                if it >= holdA:
                    accA += nA / float(nI - holdA)
                if it >= hold:
                    accB += nB / float(nI - hold)
                while accA >= 1.0:
                    next(gA, None)
                    accA -= 1.0
                while gB is not None and accB >= 1.0:
                    next(gB, None)
                    accB -= 1.0
            if gB is not None:
                for _ in gB:
                    pass
            for _ in gA:
                pass

        load_qi(0)
        if NSLOT > 1:
            load_qi(1)
        load_qT(0)
        for _ in stage_I(0):
            pass
        if NSLOT > 1:
            for _ in stage_I(1):
                pass
        reload(0)
        for _ in stage_B(0):
            pass
        for m in range(NSLOT):
            gA = stage_A(m)
            for _ in range(5):
                next(gA, None)
            if m + 2 < NSLOT:
                load_qi(m + 2)
            if m + 1 < NSLOT:
                load_qT(m + 1)
            if m + 1 < NSLOT:
                reload(m + 1)
            gB = stage_B(m + 1) if m + 1 < NSLOT else None
            if m + 2 < NSLOT:
                drive(stage_I(m + 2), n_I(m + 2), gA, n_A(m), gB, NBIS)
            elif gB is not None:
                per = -(-n_A(m) // NBIS)
                for _ in gB:
                    for _ in range(per):
                        next(gA, None)
                for _ in gA:
                    pass
            else:
                for _ in gA:
                    pass
            stage_N(m)
        P.barrier()
    es_ki.close()

    with ExitStack() as es:
        NT = TT * 128
        WSLOT = 6
        wslot = [sb(es, "wsl%d" % i, [128, 4096], BF16) for i in range(WSLOT)]
        wcnt = [0]

        def wload(Wb, K_c, c0, ncols):
            s = wslot[wcnt[0] % WSLOT]
            wcnt[0] += 1
            view = s[:, 0:K_c * ncols].rearrange("p (k n) -> p k n", n=ncols)
            P.dma("sp", lambda e: e.dma_start(out=view, in_=Wb[:, :, c0:c0 + ncols]), r=[Wb], w=[s])
            return s, view

        gfin = sb(es, "gfin", [128, D], F32)
        cw = sb(es, "cw", [128, KC * 3], F32)
        P.dma("sp", lambda e: e.dma_start(out=gfin[:], in_=gfin_in[:, :]), w=[gfin])
        P.dma("sp", lambda e: e.dma_start(out=cw[:], in_=cw_in[:, :]), w=[cw])
        pg = [ps(es, "pg%d" % i, [128, 512]) for i in range(2)]
        rbc = rms_bufs(es, "c", pg)
        hx = [sb(es, "hx%d" % i, [128, D], F32) for i in range(TT)]
        ss_c = [sb(es, "ssc%d" % i, [128, 1], F32) for i in range(TT)]
        rstd_c = [sb(es, "rstdc%d" % i, [128, 1], F32) for i in range(TT)]
        abf_c = [rbc[4], sb(es, "abfc2", [128, D], BF16)]
        aT = sb(es, "aTc", [128, KC, NT], BF16)
        aTh = sb(es, "aTh", [128, KC, 2 * TT], BF16)
        yaT = sb(es, "yaT", [128, KC, NT], BF16)
        uT = sb(es, "uT", [128, KC, TT, 130], F32)
        cuT = sb(es, "cuT", [128, KC, TT, 130], BF16)
        acc = sb(es, "acc", [128, KC, TT, 128], F32)
        ycv = sb(es, "ycv", [128, KC, NT], BF16)
        sg = sb(es, "sg", [128, KC, NT], BF16)
        mixed = acc
        mixb = sb(es, "mixb", [128, KC, NT], BF16)
        fT = sb(es, "fT", [128, KC, NT], BF16)
        actT = sb(es, "actT", [128, FC, NT], BF16)
        sil = [sb(es, "sil%d" % i, [128, NT], F32) for i in range(2)]
        ss2 = sb(es, "ss2", [128, 1], F32)
        rs2 = sb(es, "rs2", [128, 1], F32)
        fbf = sb(es, "fbf", [128, D], BF16)
        sqj = sb(es, "sqj2", [128, D], BF16)
        ot = [sb(es, "ot%d" % i, [128, D], F32) for i in range(2)]
        NPA = 6
        pa = [ps(es, "pa%d" % i, [128, 512]) for i in range(NPA)]
        pac = [0]

        def nxt():
            b = pa[pac[0] % NPA]
            pac[0] += 1
            return b

        def proj_fm(Wb, c_base, consume):
            for hb in range(2):
                s, wv = wload(Wb, KC, c_base + hb * 512, 512)
                for c4 in range(4):
                    c = hb * 4 + c4
                    pm = nxt()
                    for kc in range(KC):
                        P.op("pe", lambda e, pm=pm, wv=wv, c4=c4, kc=kc: e.matmul(
                            pm[:, 0:NT], lhsT=wv[:, kc, c4 * 128:(c4 + 1) * 128], rhs=aT[:, kc, :], start=(kc == 0), stop=(kc == KC - 1)),
                            r=[s, aT], w=[pm])
                    consume(c, pm, s, wv, c4)

        for sp_i in range(NSUP):
            m0 = sp_i * TT
            for tt in range(TT):
                bufs_tt = (hx[tt], rbc[1], ss_c[tt], rstd_c[tt], abf_c[tt % 2], pg)
                rms_pre(bufs_tt, x_own[m0 + tt], 128)
                if tt >= 1:
                    bufs_p = (hx[tt - 1], rbc[1], ss_c[tt - 1], rstd_c[tt - 1], abf_c[(tt - 1) % 2], pg)
                    rms_tr(bufs_p, 128, aT, (tt - 1) * 128)
            bufs_p = (hx[TT - 1], rbc[1], ss_c[TT - 1], rstd_c[TT - 1], abf_c[(TT - 1) % 2], pg)
            rms_tr(bufs_p, 128, aT, (TT - 1) * 128)
            rms_transpose(rbc, x_halo[2 * m0:2 * m0 + 2 * TT, :], 2 * TT, aTh, 0)
            P.dma("sp", multi(TT, lambda e, m0=m0: [
                e.dma_start(out=yaT[:, :, tt * 128:(tt + 1) * 128], in_=ya_d[m0 + tt].rearrange("p (c t) -> p c t", t=128))
                for tt in range(TT)]), r=[ya_d], w=[yaT])

            def halo_mm(pm, s, wv, c4):
                for kc in range(KC):
                    P.op("pe", lambda e, pm=pm, wv=wv, c4=c4, kc=kc: e.matmul(
                        pm[:, 0:2 * TT], lhsT=wv[:, kc, c4 * 128:(c4 + 1) * 128], rhs=aTh[:, kc, :], start=(kc == 0),
                        stop=(kc == KC - 1)), r=[s, aTh], w=[pm])

            def c_cu(c, pm, s, wv, c4):
                P.op("act", lambda e, c=c, pm=pm: e.activation(out=cuT[:, c, :, 2:130], in_=pm[:, 0:NT].rearrange("p (b t) -> p b t", t=128),
                                                            func=AF.Copy), r=[pm], w=[cuT])
                p2_ = nxt()
                halo_mm(p2_, s, wv, c4)
                P.op("act", lambda e, c=c, p2_=p2_: e.activation(out=cuT[:, c, :, 0:2], in_=p2_[:, 0:2 * TT].rearrange("p (b t) -> p b t", t=2),
                                                             func=AF.Copy), r=[p2_], w=[cuT])

            proj_fm(Wb_in2, C_CU, c_cu)

            def c_cc(c, pm, s, wv, c4):
                P.op("dve", lambda e, c=c, pm=pm: e.tensor_tensor(out=uT[:, c, :, 2:130], in0=pm[:, 0:NT].rearrange("p (b t) -> p b t", t=128),
                                                               in1=cuT[:, c, :, 2:130], op=ALU.mult), r=[pm, cuT], w=[uT])
                p2_ = nxt()
                halo_mm(p2_, s, wv, c4)
                P.op("dve", lambda e, c=c, p2_=p2_: e.tensor_tensor(out=uT[:, c, :, 0:2], in0=p2_[:, 0:2 * TT].rearrange("p (b t) -> p b t", t=2),
                                                                in1=cuT[:, c, :, 0:2], op=ALU.mult), r=[p2_, cuT], w=[uT])
                P.op("act", lambda e, c=c: e.activation(out=acc[:, c], in_=uT[:, c, :, 0:128], func=AF.Copy,
                                                       scale=cw[:, c * 3:c * 3 + 1]), r=[uT, cw], w=[acc])
                P.op("dve", lambda e, c=c: e.scalar_tensor_tensor(out=acc[:, c], in0=uT[:, c, :, 1:129], scalar=cw[:, c * 3 + 1:c * 3 + 2],
                                                                 in1=acc[:, c], op0=ALU.mult, op1=ALU.add), r=[uT, cw, acc], w=[acc])
                P.op("dve", lambda e, c=c: e.scalar_tensor_tensor(out=acc[:, c], in0=uT[:, c, :, 2:130], scalar=cw[:, c * 3 + 2:c * 3 + 3],
                                                                 in1=acc[:, c], op0=ALU.mult, op1=ALU.add), r=[uT, cw, acc], w=[acc])

            proj_fm(Wb_in2, C_CC, c_cc)

            def c_cb(c, pm, s, wv, c4):
                P.op("dve", lambda e, c=c, pm=pm: e.tensor_tensor(out=ycv[:, c, :].rearrange("p (b t) -> p b t", t=128),
                                                               in0=pm[:, 0:NT].rearrange("p (b t) -> p b t", t=128), in1=acc[:, c],
                                                               op=ALU.mult), r=[pm, acc], w=[ycv])

            proj_fm(Wb_in2, C_CB, c_cb)

            def c_gate(c, pm, s, wv, c4):
                P.op("act", lambda e, c=c, pm=pm: e.activation(out=sg[:, c, :], in_=pm[:, 0:NT], func=AF.Sigmoid), r=[pm], w=[sg])

            def branch(Wb, src, first):
                for c in range(KC):
                    if c % 4 == 0:
                        s, wv = wload(Wb, KC, c * 128, 512)
                    c4 = c % 4
                    pm = nxt()
                    for kc in range(KC):
                        P.op("pe", lambda e, pm=pm, wv=wv, c4=c4, kc=kc, s=s: e.matmul(
                            pm[:, 0:NT], lhsT=wv[:, kc, c4 * 128:(c4 + 1) * 128], rhs=src[:, kc, :], start=(kc == 0), stop=(kc == KC - 1)),
                            r=[s, src], w=[pm])
                    if first:
                        P.op("dve", lambda e, c=c, pm=pm: e.tensor_tensor(out=mixed[:, c].rearrange("p b t -> p (b t)"), in0=pm[:, 0:NT], in1=sg[:, c, :], op=ALU.mult),
                             r=[pm, sg], w=[mixed])
                    else:
                        P.op("dve", lambda e, c=c, pm=pm: e.tensor_tensor(out=sil[0][:], in0=pm[:, 0:NT], in1=sg[:, c, :], op=ALU.mult),
                             r=[pm, sg], w=[sil[0]])
                        P.op("dve", lambda e, c=c: e.tensor_tensor(out=mixb[:, c, :], in0=sil[0][:], in1=mixed[:, c].rearrange("p b t -> p (b t)"), op=ALU.add),
                             r=[sil[0], mixed], w=[mixb])

            proj_fm(Wb_in2, C_GA, c_gate)
            branch(Wb_ao, yaT, True)
            proj_fm(Wb_in2, C_GB, c_gate)
            branch(Wb_co, ycv, False)

            for half in range(2):
                s_, wv = wload(Wb_o, KC, half * 512, 512)
                for tt in range(TT):
                    ph = nxt()
                    for kc in range(KC):
                        P.op("pe", lambda e, ph=ph, wv=wv, tt=tt, kc=kc: e.matmul(
                            ph[:, :], lhsT=mixb[:, kc, tt * 128:(tt + 1) * 128], rhs=wv[:, kc, :],
                            start=(kc == 0), stop=(kc == KC - 1)), r=[s_, mixb], w=[ph])
                    P.op("dve", lambda e, ph=ph, tt=tt, half=half: e.tensor_tensor(out=hx[tt][:, half * 512:(half + 1) * 512], in0=ph[:, :],
                                                                                  in1=hx[tt][:, half * 512:(half + 1) * 512], op=ALU.add),
                         r=[ph, hx[tt]], w=[hx[tt]])
            for tt in range(TT):
                h_ = hx[tt]
                P.op("act", lambda e, h_=h_: e.activation(out=sqj[:], in_=h_[:], func=AF.Square, accum_out=ss2[:, 0:1]), r=[h_], w=[sqj, ss2])
                P.op("act", lambda e: e.activation(out=rs2[:], in_=ss2[:], func=AF.Sqrt, scale=1.0 / D, bias=epsb[:, 0:1]), r=[ss2, epsb], w=[rs2])
                P.op("dve", lambda e: e.reciprocal(out=rs2[:], in_=rs2[:]), r=[rs2], w=[rs2])
                P.op("dve", lambda e, h_=h_: e.tensor_scalar(out=fbf[:], in0=h_[:], scalar1=rs2[:, 0:1], scalar2=None, op0=ALU.mult),
                     r=[h_, rs2], w=[fbf])
                dfn = lambda hf, tt=tt: fT[:, hf * 4:(hf + 1) * 4, tt * 128:(tt + 1) * 128]
                dfn.buf = fT
                pe_transpose(fbf, 128, pg, dfn)
            for f0 in range(0, FC, 4):
                nf = min(4, FC - f0)
                sg_, wg = wload(Wb_g, KC, f0 * 128, nf * 128)
                su_, wu = wload(Wb_u, KC, f0 * 128, nf * 128)
                for fc in range(nf):
                    for kc in range(KC):
                        P.op("pe", lambda e, wg=wg, fc=fc, kc=kc: e.matmul(pg[0][:, 0:NT], lhsT=wg[:, kc, fc * 128:(fc + 1) * 128],
                                                                         rhs=fT[:, kc, :], start=(kc == 0), stop=(kc == KC - 1)),
                             r=[sg_, fT], w=[pg[0]])
                    for kc in range(KC):
                        P.op("pe", lambda e, wu=wu, fc=fc, kc=kc: e.matmul(pg[1][:, 0:NT], lhsT=wu[:, kc, fc * 128:(fc + 1) * 128],
                                                                         rhs=fT[:, kc, :], start=(kc == 0), stop=(kc == KC - 1)),
                             r=[su_, fT], w=[pg[1]])
                    sl = sil[(f0 + fc) % 2]
                    P.op("act", lambda e, sl=sl: e.activation(out=sl[:], in_=pg[0][:, 0:NT], func=AF.Silu), r=[pg[0]], w=[sl])
                    P.op("dve", lambda e, sl=sl, f=f0 + fc: e.tensor_tensor(out=actT[:, f, :], in0=pg[1][:, 0:NT], in1=sl[:], op=ALU.mult),
                         r=[pg[1], sl], w=[actT])
            FG = [(0, 8), (8, 16), (16, FC)]
            for half in range(2):
                slots = []
                for (f0, f1) in FG:
                    s_ = wslot[wcnt[0] % WSLOT]
                    wcnt[0] += 1
                    nfc = f1 - f0
                    P.dma("sp", lambda e, s_=s_, half=half, f0=f0, f1=f1, nfc=nfc: e.dma_start(
                        out=s_[:, 0:nfc * 512], in_=Wb_d[:, half, f0:f1, :].rearrange("p f c -> p (f c)")), r=[Wb_d], w=[s_])
                    slots.append((s_, s_[:, 0:nfc * 512].rearrange("p (k n) -> p k n", n=512), f0, f1))
                for tt in range(TT):
                    pq = nxt()
                    for (s_, wv, f0, f1) in slots:
                        for fc in range(f0, f1):
                            P.op("pe", lambda e, pq=pq, wv=wv, tt=tt, fc=fc, f0=f0: e.matmul(
                                pq[:, :], lhsT=actT[:, fc, tt * 128:(tt + 1) * 128], rhs=wv[:, fc - f0, :],
                                start=(fc == 0), stop=(fc == FC - 1)), r=[s_, actT], w=[pq])
                    P.op("dve", lambda e, pq=pq, tt=tt, half=half: e.tensor_tensor(out=hx[tt][:, half * 512:(half + 1) * 512], in0=pq[:, :],
                                                                                  in1=hx[tt][:, half * 512:(half + 1) * 512], op=ALU.add),
                         r=[pq, hx[tt]], w=[hx[tt]])
            for tt in range(TT):
                h_ = hx[tt]
                o_ = ot[tt % 2]
                P.op("act", lambda e, h_=h_: e.activation(out=sqj[:], in_=h_[:], func=AF.Square, accum_out=ss2[:, 0:1]), r=[h_], w=[sqj, ss2])
                P.op("act", lambda e: e.activation(out=rs2[:], in_=ss2[:], func=AF.Sqrt, scale=1.0 / D, bias=epsb[:, 0:1]), r=[ss2, epsb], w=[rs2])
                P.op("dve", lambda e: e.reciprocal(out=rs2[:], in_=rs2[:]), r=[rs2], w=[rs2])
                P.op("dve", lambda e, h_=h_, o_=o_: e.scalar_tensor_tensor(out=o_[:], in0=h_[:], scalar=rs2[:, 0:1], in1=gfin[:],
                                                                        op0=ALU.mult, op1=ALU.mult), r=[h_, rs2, gfin], w=[o_])
                P.dma("pool", lambda e, o_=o_, mm=m0 + tt: e.dma_start(out=out[mm], in_=o_[:]), r=[o_], w=[], key="out")
        P.barrier()

    P.lower(top)
    top.close()
    return nc


_CACHE = {}


def _prep_inputs(x, meta_tokens, norm_mix_g, w_in, w_attn_out, conv_w, w_conv_out, w_out, norm_ffn_g, w_gate, w_up,
                 w_down, norm_final_g):
    f = np.float32
    x = np.asarray(x, f)
    B, SEQ, _ = x.shape
    meta = np.asarray(meta_tokens, f)
    NXC = SEQ // 128
    NSLOT = NXC // 4
    common = {
        "w_in": np.ascontiguousarray(np.asarray(w_in, f)[0]),
        "w_attn_out": np.ascontiguousarray(np.asarray(w_attn_out, f)[0]),
        "w_conv_out": np.ascontiguousarray(np.asarray(w_conv_out, f)[0]),
        "w_out": np.ascontiguousarray(np.asarray(w_out, f)[0]),
        "w_gate": np.ascontiguousarray(np.asarray(w_gate, f)[0]),
        "w_up": np.ascontiguousarray(np.asarray(w_up, f)[0]),
        "w_down": np.ascontiguousarray(np.asarray(w_down, f)[0]),
        "g_mix": np.ascontiguousarray(np.asarray(norm_mix_g, f)[0].reshape(KC, 128).T),
        "g_ffn": np.ascontiguousarray(np.asarray(norm_ffn_g, f)[0].reshape(KC, 128).T),
        "g_fin": np.ascontiguousarray(np.broadcast_to(np.asarray(norm_final_g, f)[None, :], (128, D))),
        "conv_w": np.ascontiguousarray(np.asarray(conv_w, f)[0].reshape(3, KC, 128).transpose(2, 1, 0).reshape(128, KC * 3)),
        "ident": np.eye(128, dtype=f),
    }
    cbm = np.zeros((128, 128), f)
    cbm[:, 16:] = -1e30
    in_maps = []
    for core in range(8):
        b, r = core // 4, core % 4
        x_all = np.zeros(((NXC + 1) * 128, D), f)
        x_all[:SEQ] = x[b]
        x_all[SEQ:SEQ + 16] = meta
        qbs = [4 * m + r for m in range(NSLOT)]
        x_own = np.stack([x[b, qb * 128:(qb + 1) * 128] for qb in qbs])
        halo = np.zeros((NSLOT * 2, D), f)
        for m, qb in enumerate(qbs):
            if qb == 0:
                halo[2 * m:2 * m + 2] = meta[14:16]
            else:
                halo[2 * m:2 * m + 2] = x[b, qb * 128 - 2:qb * 128]
        t = np.arange(128)[:, None]
        c = np.arange(512)[None, :]
        cbx = np.where(c <= r * 128 + t, 0.0, -1e30).astype(f)
        d = dict(common)
        d.update({"x_all": x_all, "x_own": np.ascontiguousarray(x_own), "x_halo": halo, "cbx": cbx, "cbm": cbm})
        in_maps.append(d)
    return in_maps, B, SEQ, NSLOT


def kernel(x, meta_tokens, norm_mix_g, w_in, w_attn_out, conv_w, w_conv_out, w_out, norm_ffn_g, w_gate, w_up, w_down,
           norm_final_g, _dbg=False):
    in_maps, B, SEQ, NSLOT = _prep_inputs(x, meta_tokens, norm_mix_g, w_in, w_attn_out, conv_w, w_conv_out, w_out,
                                          norm_ffn_g, w_gate, w_up, w_down, norm_final_g)
    key = (SEQ, _dbg)
    if key not in _CACHE:
        _CACHE[key] = build(SEQ, _dbg)
    nc = _CACHE[key]
    res = run_bass_kernel_spmd(nc, in_maps, core_ids=list(range(8)))
    outp = np.zeros((B, SEQ, D), np.float32)
    for core in range(8):
        b, r = core // 4, core % 4
        o = np.asarray(res.results[core]["out"], np.float32)
        for m in range(NSLOT):
            qb = 4 * m + r
            outp[b, qb * 128:(qb + 1) * 128] = o[m]
    if _dbg:
        return outp, res.results
    return outp
```

```python
import numpy as np
from contextlib import ExitStack
import concourse.bass as bass
import concourse.mybir as mybir
from concourse.bass_utils import run_bass_kernel_spmd

F32 = mybir.dt.float32
BF16 = mybir.dt.bfloat16
U8 = mybir.dt.uint8
AF = mybir.ActivationFunctionType
ALU = mybir.AluOpType
AX = mybir.AxisListType

D = 1024
KC = 8
NH = 8
DFF = 2816
FC = DFF // 128
PROJ = 8776
C_Q, C_K, C_V, C_QI, C_KI, C_WI, C_CU, C_CB, C_CC, C_GA, C_GB = 0, 1024, 2048, 3072, 3584, 3648, 3656, 4680, 5704, 6728, 7752
EPS = 1e-6
IDX_SCALE = (8 ** -0.5) * (64 ** -0.5)
SM_SCALE = 128 ** -0.5
NEG = -30000.0
NBIS = 16
TOPK = 256.0
ACT_COLS = 0
B_HOLD = 0
A_HOLD = 18


class Buf:
    def __init__(self, name, t, const=False):
        self.name, self.t, self.const = name, t, const
        self.lw = None
        self.rd = []

    def __getitem__(self, k):
        return self.t[k]


class Op:
    __slots__ = ("eng", "fn", "deps", "dma", "key", "signal", "tok", "idx")

    def __init__(self, eng, fn, deps, dma, key):
        self.eng, self.fn, self.deps, self.dma, self.key = eng, fn, deps, dma, key
        self.signal = False
        self.tok = None


class Prog:
    ENGS = ["pe", "act", "dve", "pool", "sp"]

    def __init__(self, nc):
        self.nc = nc
        self.ops = []
        self.last = {e: None for e in self.ENGS}
        self.dma_since = {}

    def op(self, eng, fn, r=(), w=(), dma=False, key=None):
        deps = []
        for b in r:
            if b.lw is not None:
                deps.append(b.lw)
        for b in w:
            if b.lw is not None:
                deps.append(b.lw)
            lastr = {}
            for d in b.rd:
                if d.dma:
                    deps.append(d)
                else:
                    lastr[d.eng] = d
            deps.extend(lastr.values())
        o = Op(eng, fn, None, dma, key)
        dd = []
        seen = set()
        for d in deps:
            if id(d) in seen:
                continue
            seen.add(id(d))
            if d.eng == "pe" and eng == "pe" and not d.dma and not dma:
                continue
            dd.append(d)
        o.deps = dd
        for b in r:
            if not b.const:
                b.rd.append(o)
        for b in w:
            b.lw = o
            b.rd = []
        self.ops.append(o)
        self.last[eng] = o
        if dma:
            self.dma_since[key] = o
        return o

    def dma(self, eng, fn, r=(), w=(), key=None):
        if key is None:
            key = w[0].name
        return self.op(eng, fn, r, w, dma=True, key=key)

    def barrier(self):
        lasts = [o for o in self.last.values() if o is not None] + list(self.dma_since.values())
        self.dma_since = {}
        new = []
        for e in self.ENGS:
            o = Op(e, (lambda en: en.nop()), [d for d in lasts], False, None)
            new.append(o)
        for o in new:
            self.ops.append(o)
            self.last[o.eng] = o

    def lower(self, es):
        nc = self.nc
        for o in self.ops:
            for d in o.deps:
                d.signal = True
        EPOCH = 12000
        eng_sem = {}
        eng_cnt = {}
        dma_sem = {}
        dma_cnt = {}

        def new_sem(nm):
            return es.enter_context(nc.semaphore(nm))

        nsem = [0]
        for o in self.ops:
            if o.dma:
                if o.key not in dma_sem or dma_cnt[o.key] > 28000:
                    nsem[0] += 1
                    dma_sem[o.key] = new_sem("d%d" % nsem[0])
                    dma_cnt[o.key] = 0
                o.tok = [dma_sem[o.key], dma_cnt[o.key]]
            elif o.signal:
                if o.eng not in eng_sem or eng_cnt[o.eng] >= EPOCH:
                    nsem[0] += 1
                    eng_sem[o.eng] = new_sem("e%d" % nsem[0])
                    eng_cnt[o.eng] = 0
                eng_cnt[o.eng] += 1
                o.tok = [eng_sem[o.eng], eng_cnt[o.eng]]
            if o.dma:
                n = getattr(o.fn, "ndma", 1)
                dma_cnt[o.key] += 16 * n
                o.tok = [dma_sem[o.key], dma_cnt[o.key]]
        per = {e: [o for o in self.ops if o.eng == e] for e in self.ENGS}
        block = es.enter_context(nc.Block())

        def emit(e, lst):
            seen = {}
            for o in lst:
                for d in o.deps:
                    s, v = d.tok
                    k = id(s)
                    if seen.get(k, 0) < v:
                        e.wait_ge(s, v)
                        seen[k] = v
                ins = o.fn(e)
                if o.dma:
                    if not isinstance(ins, (list, tuple)):
                        ins = [ins]
                    assert len(ins) == getattr(o.fn, "ndma", 1)
                    for i in ins:
                        i.then_inc(o.tok[0], 16)
                elif o.signal:
                    if isinstance(ins, (list, tuple)):
                        ins = ins[-1]
                    ins.then_inc(o.tok[0], 1)

        @block.tensor
        def _(e):
            emit(e, per["pe"])

        @block.scalar
        def _(e):
            emit(e, per["act"])

        @block.vector
        def _(e):
            emit(e, per["dve"])

        @block.gpsimd
        def _(e):
            emit(e, per["pool"])

        @block.sync
        def _(e):
            emit(e, per["sp"])


def multi(n, f):
    f.ndma = n
    return f


def build(SEQ, dbg=False):
    NXC = SEQ // 128
    NCH = NXC + 1
    NSLOT = NXC // 4
    TT = 4
    NSUP = NSLOT // TT
    SMAX = NCH * 128

    nc = bass.Bass("TRN2", target_bir_lowering=False)
    P = Prog(nc)

    def din(name, shape, dt=F32):
        return nc.dram_tensor(name, list(shape), dt, kind="ExternalInput").ap()

    x_all = din("x_all", [NCH * 128, D])
    x_own = din("x_own", [NSLOT, 128, D])
    x_halo = din("x_halo", [NSLOT * 2, D])
    cbx_in = din("cbx", [128, 512])
    cbm_in = din("cbm", [128, 128])
    w_in = din("w_in", [D, PROJ])
    w_ao = din("w_attn_out", [D, D])
    w_co = din("w_conv_out", [D, D])
    w_o = din("w_out", [D, D])
    w_g = din("w_gate", [D, DFF])
    w_u = din("w_up", [D, DFF])
    w_d = din("w_down", [DFF, D])
    gmix_in = din("g_mix", [128, KC])
    gffn_in = din("g_ffn", [128, KC])
    gfin_in = din("g_fin", [128, D])
    cw_in = din("conv_w", [128, KC * 3])
    ident_in = din("ident", [128, 128])
    out = nc.dram_tensor("out", [NSLOT, 128, D], F32, kind="ExternalOutput").ap()

    skind = "ExternalOutput" if dbg else "Internal"

    def dscr(name, shape, dt=BF16, kind=None):
        t = nc.dram_tensor(name, list(shape), dt, kind=kind or skind).ap()
        return Buf("dr_" + name, t)

    Wb_in = dscr("Wb_in", [128, KC, PROJ], kind="Internal")
    Wb_ao = dscr("Wb_ao", [128, KC, D], kind="Internal")
    Wb_co = dscr("Wb_co", [128, KC, D], kind="Internal")
    Wb_o = dscr("Wb_o", [128, KC, D], kind="Internal")
    Wb_g = dscr("Wb_g", [128, KC, DFF], kind="Internal")
    Wb_u = dscr("Wb_u", [128, KC, DFF], kind="Internal")
    Wb_d = dscr("Wb_d", [128, 2, FC, 512], kind="Internal")
    KT_d = dscr("KT_d", [NCH, 128, NH * 128])
    V_d = dscr("V_d", [NCH, 128, NH * 129])
    ya_d = dscr("ya_d", [NSLOT, 128, KC * 128])

    top = ExitStack()

    def sb(es, name, shape, dt, const=False):
        t = es.enter_context(nc.sbuf_tensor("s_" + name, list(shape), dt))
        return Buf(name, t, const)

    def ps(es, name, shape, dt=F32):
        t = es.enter_context(nc.psum_tensor("p_" + name, list(shape), dt))
        return Buf(name, t)

    ident_f = sb(top, "ident_f", [128, 128], F32)
    ident = sb(top, "ident", [128, 128], BF16)
    ident4 = sb(top, "ident4", [128, 4, 128], BF16)
    gmix = sb(top, "gmix", [128, KC], F32)
    gffn = sb(top, "gffn", [128, KC], F32)
    P.dma("sp", lambda e: e.dma_start(out=ident_f[:], in_=ident_in[:, :]), w=[ident_f])
    P.dma("sp", lambda e: e.dma_start(out=gmix[:], in_=gmix_in[:, :]), w=[gmix])
    P.dma("sp", lambda e: e.dma_start(out=gffn[:], in_=gffn_in[:, :]), w=[gffn])
    P.op("dve", lambda e: e.tensor_copy(out=ident[:], in_=ident_f[:]), r=[ident_f], w=[ident])
    for i in range(4):
        P.op("dve", lambda e, i=i: e.tensor_copy(out=ident4[:, i, :], in_=ident_f[:]), r=[ident_f], w=[ident4])

    Wb_in2 = Buf("dr_Wb_in2", Wb_in.t)
    epsb = sb(top, "epsb", [128, 1], F32)
    P.op("dve", lambda e: e.memset(epsb[:], EPS), w=[epsb])
    es_ki = ExitStack()
    kiT = sb(es_ki, "kiT", [128, SMAX], BF16)
    es_p0 = ExitStack()
    NST = 3
    CBLK = 2816
    stg = [sb(es_p0, "p0s%d" % i, [128, CBLK], F32) for i in range(NST)]
    stb = [sb(es_p0, "p0b%d" % i, [128, CBLK], BF16) for i in range(NST)]
    p0cnt = [0]

    def prep(src, K, c_lo, c_hi, dst, g, dst_t=None):
        for kc in range(K // 128):
            for c0 in range(c_lo, c_hi, CBLK):
                cw = min(CBLK, c_hi - c0)
                i = p0cnt[0] % NST
                p0cnt[0] += 1
                s_, b_ = stg[i], stb[i]
                P.dma("sp", lambda e, s_=s_, kc=kc, c0=c0, cw=cw, src=src: e.dma_start(
                    out=s_[:, 0:cw], in_=src[kc * 128:(kc + 1) * 128, c0:c0 + cw]), w=[s_])
                if g is None:
                    if p0cnt[0] % 2 == 0:
                        P.op("act", lambda e, s_=s_, b_=b_, cw=cw: e.activation(out=b_[:, 0:cw], in_=s_[:, 0:cw], func=AF.Copy),
                             r=[s_], w=[b_])
                    else:
                        P.op("dve", lambda e, s_=s_, b_=b_, cw=cw: e.tensor_copy(out=b_[:, 0:cw], in_=s_[:, 0:cw]),
                             r=[s_], w=[b_])
                else:
                    if p0cnt[0] % 2 == 0:
                        P.op("act", lambda e, s_=s_, b_=b_, cw=cw, g=g, kc=kc: e.activation(
                            out=b_[:, 0:cw], in_=s_[:, 0:cw], func=AF.Copy, scale=g[:, kc:kc + 1]), r=[s_, g], w=[b_])
                    else:
                        P.op("dve", lambda e, s_=s_, b_=b_, cw=cw, g=g, kc=kc: e.tensor_scalar(
                            out=b_[:, 0:cw], in0=s_[:, 0:cw], scalar1=g[:, kc:kc + 1], scalar2=None, op0=ALU.mult),
                            r=[s_, g], w=[b_])
                if dst is Wb_d:
                    P.dma("pool", lambda e, b_=b_, kc=kc, dst=dst: e.dma_start(
                        out=dst[:, :, kc, :], in_=b_[:, 0:D].rearrange("p (g c) -> p g c", c=512)), r=[b_], w=[dst], key=dst.name)
                else:
                    P.dma("pool", lambda e, b_=b_, kc=kc, c0=c0, cw=cw, dst=dst: e.dma_start(
                        out=dst[:, kc, c0:c0 + cw], in_=b_[:, 0:cw]), r=[b_], w=[dst], key=dst.name)
                yield

    def prep_rest():
        yield from prep(w_in, D, C_CU, PROJ, Wb_in2, gmix)
        yield from prep(w_ao, D, 0, D, Wb_ao, None)
        yield from prep(w_co, D, 0, D, Wb_co, None)
        yield from prep(w_o, D, 0, D, Wb_o, None)
        yield from prep(w_g, D, 0, DFF, Wb_g, gffn)
        yield from prep(w_u, D, 0, DFF, Wb_u, gffn)
        yield from prep(w_d, DFF, 0, D, Wb_d, None)

    for _ in prep(w_in, D, 0, C_CU, Wb_in, gmix):
        pass
    g_rest = prep_rest()

    def rms_transpose(es_bufs, x_src_ap, nrows, aT, col0, keep_x=None):
        rms_pre(es_bufs, x_src_ap, nrows)
        rms_tr(es_bufs, nrows, aT, col0)

    def rms_tr(es_bufs, nrows, aT, col0):
        xt, sq_junk, ss, rstd, a_bf, tp = es_bufs
        dfn = lambda hf: aT[:, hf * 4:(hf + 1) * 4, col0:col0 + nrows]
        dfn.buf = aT
        pe_transpose(a_bf, nrows, tp, dfn)

    def rms_pre(es_bufs, x_src_ap, nrows):
        xt, sq_junk, ss, rstd, a_bf, tp = es_bufs
        P.dma("sp", lambda e: e.dma_start(out=xt[0:nrows, :], in_=x_src_ap), w=[xt])
        P.op("act", lambda e: e.activation(out=sq_junk[0:nrows, :], in_=xt[0:nrows, :], func=AF.Square,
                                           accum_out=ss[0:nrows, 0:1]), r=[xt], w=[sq_junk, ss])
        P.op("act", lambda e: e.activation(out=rstd[0:nrows, 0:1], in_=ss[0:nrows, 0:1], func=AF.Sqrt,
                                           scale=1.0 / D, bias=epsb[0:nrows, 0:1]), r=[ss, epsb], w=[rstd])
        P.op("dve", lambda e: e.reciprocal(out=rstd[0:nrows, 0:1], in_=rstd[0:nrows, 0:1]), r=[rstd], w=[rstd])
        P.op("dve", lambda e: e.tensor_scalar(out=a_bf[0:nrows, :], in0=xt[0:nrows, :], scalar1=rstd[0:nrows, 0:1],
                                              scalar2=None, op0=ALU.mult), r=[xt, rstd], w=[a_bf])

    def pe_transpose(src, nrows, tp, dst_fn):
        for hf in range(2):
            t_ = tp[hf]
            for c4 in range(4):
                kc = hf * 4 + c4
                P.op("pe", lambda e, t_=t_, c4=c4, kc=kc: e.matmul(
                    t_[:, c4 * 128:c4 * 128 + nrows], lhsT=src[0:nrows, kc * 128:(kc + 1) * 128], rhs=ident[0:nrows, 0:nrows],
                    start=True, stop=True), r=[src, ident], w=[t_])
            tv = t_[:].rearrange("p (c t) -> p c t", t=128)
            if hf == 0:
                P.op("act", lambda e, tv=tv, hf=hf: e.activation(out=dst_fn(hf), in_=tv[:, :, 0:nrows], func=AF.Copy), r=[t_], w=[dst_fn.buf])
            else:
                P.op("dve", lambda e, tv=tv, hf=hf: e.tensor_copy(out=dst_fn(hf), in_=tv[:, :, 0:nrows]), r=[t_], w=[dst_fn.buf])


    def rms_bufs(es, tag, tp):
        return (sb(es, "xt" + tag, [128, D], F32), sb(es, "sqj" + tag, [128, D], BF16), sb(es, "ss" + tag, [128, 1], F32),
                sb(es, "rstd" + tag, [128, 1], F32), sb(es, "abf" + tag, [128, D], BF16), tp)


    with ExitStack() as es:
        Wkv = sb(es, "Wkv", [128, KC, 2176], BF16)
        P.dma("sp", multi(3, lambda e: [
            e.dma_start(out=Wkv[:, :, 0:2048], in_=Wb_in[:, :, C_K:C_K + 2048]),
            e.dma_start(out=Wkv[:, :, 2048:2112], in_=Wb_in[:, :, C_KI:C_KI + 64]),
            e.dma_start(out=Wkv[:, :, 2112:2176], in_=Wb_in[:, :, C_KI:C_KI + 64])]), r=[Wb_in], w=[Wkv])
        tp1 = [ps(es, "tp1_%d" % i, [128, 512]) for i in range(2)]
        xts1 = [sb(es, "xt1_%d" % i, [128, D], F32) for i in range(4)]
        sqj1 = [sb(es, "sqj1_%d" % i, [128, D], BF16) for i in range(2)]
        rb = [(xts1[i % 4], sqj1[i % 2], sb(es, "ss1_%d" % i, [128, 1], F32), sb(es, "rstd1_%d" % i, [128, 1], F32),
               sb(es, "abf1_%d" % i, [128, D], BF16), tp1) for i in range(8)]
        aTs = [sb(es, "aTs%d" % i, [128, KC, 512], BF16) for i in range(2)]
        KTst = [sb(es, "KTst%d" % i, [128, 4, NH, 128], BF16) for i in range(2)]
        Vst = [sb(es, "Vst%d" % i, [128, NH, 129], BF16) for i in range(2)]
        for v in Vst:
            P.op("dve", lambda e, v=v: e.memset(v[:, :, 128:129], 1.0), w=[v])
        pk = [ps(es, "pk%d" % i, [128, 512]) for i in range(2)]
        pv = [ps(es, "pv%d" % i, [128, 512]) for i in range(2)]
        pki = ps(es, "pki", [128, 512])
        pkc = 0

        def pre_tile(j):
            if j < NCH:
                rms_pre(rb[j % 8], x_all[j * 128:(j + 1) * 128, :], 128)
                next(g_rest, None)

        for j in range(min(4, NCH)):
            pre_tile(j)
        for g0 in range(0, NCH, 4):
            nt = min(4, NCH - g0)
            N = nt * 128
            aT = aTs[(g0 // 4) % 2]
            for jj in range(nt):
                rms_tr(rb[(g0 + jj) % 8], 128, aT, jj * 128)
            KS = KTst[(g0 // 4) % 2]
            for h in range(NH):
                if h % 2 == 1:
                    pre_tile(g0 + 4 + h // 2)
                pb = pk[pkc % 2]
                pkc += 1
                for kc in range(KC):
                    P.op("pe", lambda e, pb=pb, kc=kc, h=h, aT=aT, N=N: e.matmul(
                        pb[:, 0:N], lhsT=Wkv[:, kc, h * 128:(h + 1) * 128], rhs=aT[:, kc, 0:N],
                        start=(kc == 0), stop=(kc == KC - 1)), r=[Wkv, aT], w=[pb])
                pbv = pb[:].rearrange("p (j s) -> p j s", s=128)
                if h % 2 == 0:
                    P.op("act", lambda e, pbv=pbv, KS=KS, h=h, nt=nt: e.activation(out=KS[:, 0:nt, h, :], in_=pbv[:, 0:nt, :],
                                                                                func=AF.Copy), r=[pb], w=[KS])
                else:
                    P.op("dve", lambda e, pbv=pbv, KS=KS, h=h, nt=nt: e.tensor_copy(out=KS[:, 0:nt, h, :], in_=pbv[:, 0:nt, :]),
                         r=[pb], w=[KS])
            P.dma("pool", lambda e, KS=KS, g0=g0, nt=nt: e.dma_start(
                out=KT_d[g0:g0 + nt].rearrange("j d f -> d j f"),
                in_=KS[:, 0:nt].rearrange("p j h s -> p j (h s)")), r=[KS], w=[KT_d], key="dr_KT_d")
            for kc in range(KC):
                P.op("pe", lambda e, kc=kc, aT=aT, N=N: e.matmul(pki[:, 0:N], lhsT=Wkv[:, kc, 2048:2176], rhs=aT[:, kc, 0:N],
                                                              start=(kc == 0), stop=(kc == KC - 1)), r=[Wkv, aT], w=[pki])
            P.op("act", lambda e, g0=g0, N=N: e.activation(out=kiT[:, g0 * 128:g0 * 128 + N], in_=pki[:, 0:N], func=AF.Copy),
                 r=[pki], w=[kiT])
            for jj in range(nt):
                VS = Vst[(g0 + jj) % 2]
                for half in range(2):
                    pb = pv[half]
                    for kc in range(KC):
                        P.op("pe", lambda e, pb=pb, kc=kc, half=half, aT=aT, jj=jj: e.matmul(
                            pb[:, :], lhsT=aT[:, kc, jj * 128:(jj + 1) * 128],
                            rhs=Wkv[:, kc, 1024 + half * 512:1024 + (half + 1) * 512],
                            start=(kc == 0), stop=(kc == KC - 1)), r=[Wkv, aT], w=[pb])
                    pbv = pb[:].rearrange("p (h d) -> p h d", d=128)
                    if half == 0:
                        P.op("act", lambda e, pbv=pbv, VS=VS: e.activation(out=VS[:, 0:4, 0:128], in_=pbv, func=AF.Copy),
                             r=[pb], w=[VS])
                    else:
                        P.op("dve", lambda e, pbv=pbv, VS=VS: e.tensor_copy(out=VS[:, 4:8, 0:128], in_=pbv), r=[pb], w=[VS])
                P.dma("pool", lambda e, VS=VS, j=g0 + jj: e.dma_start(out=V_d[j], in_=VS[:].rearrange("p h d -> p (h d)")),
                      r=[VS], w=[V_d], key="dr_V_d")
        for _ in g_rest:
            pass
        P.barrier()
    es_p0.close()

    q_d = dscr("q_d", [NSLOT, 128, NH * 128], kind="Internal")
    qi_d = dscr("qi_d", [NSLOT, 128, NH * 128], kind="Internal")
    dg_d = dscr("dg_d", [NSLOT, 128, NH * 128], kind="Internal")
    with ExitStack() as es:
        Wq = sb(es, "Wq", [128, KC, 1024 + 512], BF16)
        P.dma("sp", multi(2, lambda e: [
            e.dma_start(out=Wq[:, :, 0:1024], in_=Wb_in[:, :, C_Q:C_Q + 1024]),
            e.dma_start(out=Wq[:, :, 1024:1536], in_=Wb_in[:, :, C_QI:C_QI + 512])]), r=[Wb_in], w=[Wq])
        Ww = sb(es, "Ww", [128, KC, 8], BF16)
        P.dma("sp", lambda e: e.dma_start(out=Ww[:], in_=Wb_in[:, :, C_WI:C_WI + 8]), r=[Wb_in], w=[Ww])
        tpa = [ps(es, "tpa%d" % i, [128, 512]) for i in range(2)]
        rba = [rms_bufs(es, "qa", tpa), rms_bufs(es, "qb", tpa)]
        G4 = 4
        NG = NSLOT // G4
        aT4 = [sb(es, "aT4_%d" % i, [128, KC, G4 * 128], BF16) for i in range(2)]
        qT4 = [sb(es, "qT4_%d" % i, [128, G4, NH, 128], BF16) for i in range(2)]
        qiT4 = [sb(es, "qiT4_%d" % i, [128, G4, 4, 2, 128], BF16) for i in range(2)]
        for q_ in qiT4:
            P.op("dve", lambda e, q_=q_: e.memset(q_[:], 0.0), w=[q_])
        Dgs = [sb(es, "Dga%d" % i, [128, NH, 128], BF16) for i in range(2)]
        wsb4 = [sb(es, "wsb4_%d" % i, [128, G4 * 8], F32) for i in range(2)]
        pqa = [ps(es, "pqa%d" % i, [128, 512]) for i in range(2)]
        pw = ps(es, "pw", [128, 512])
        rms_pre(rba[0], x_own[0], 128)
        pcnt = 0
        for g in range(NG):
            aT, qTg, qiTg, wsbg = aT4[g % 2], qT4[g % 2], qiT4[g % 2], wsb4[g % 2]
            for j in range(G4):
                m = g * G4 + j
                rms_tr(rba[m % 2], 128, aT, j * 128)
                if m + 1 < NSLOT:
                    rms_pre(rba[(m + 1) % 2], x_own[m + 1], 128)
            for h in range(NH):
                pb = pqa[pcnt % 2]
                pcnt += 1
                for kc in range(KC):
                    P.op("pe", lambda e, pb=pb, h=h, kc=kc, aT=aT: e.matmul(
                        pb[:, :], lhsT=Wq[:, kc, h * 128:(h + 1) * 128], rhs=aT[:, kc, :],
                        start=(kc == 0), stop=(kc == KC - 1)), r=[Wq, aT], w=[pb])
                pbv = pb[:].rearrange("p (b t) -> p b t", t=128)
                if h % 2 == 0:
                    P.op("act", lambda e, pbv=pbv, h=h, qTg=qTg: e.activation(out=qTg[:, :, h, :], in_=pbv, func=AF.Copy),
                         r=[pb], w=[qTg])
                else:
                    P.op("dve", lambda e, pbv=pbv, h=h, qTg=qTg: e.tensor_copy(out=qTg[:, :, h, :], in_=pbv), r=[pb], w=[qTg])
            for c in range(4):
                pb = pqa[pcnt % 2]
                pcnt += 1
                for kc in range(KC):
                    P.op("pe", lambda e, pb=pb, c=c, kc=kc, aT=aT: e.matmul(
                        pb[:, :], lhsT=Wq[:, kc, 1024 + c * 128:1024 + (c + 1) * 128], rhs=aT[:, kc, :],
                        start=(kc == 0), stop=(kc == KC - 1)), r=[Wq, aT], w=[pb])
                pbv = pb[:].rearrange("p (b t) -> p b t", t=128)
                P.op("act", lambda e, pbv=pbv, c=c, qiTg=qiTg: e.activation(out=qiTg[0:64, :, c, 0, :], in_=pbv[0:64, :, :],
                                                                          func=AF.Copy), r=[pb], w=[qiTg])
                P.op("dve", lambda e, pbv=pbv, c=c, qiTg=qiTg: e.tensor_copy(out=qiTg[64:128, :, c, 1, :], in_=pbv[64:128, :, :]),
                     r=[pb], w=[qiTg])
            for j in range(G4):
                for kc in range(KC):
                    P.op("pe", lambda e, j=j, kc=kc, aT=aT: e.matmul(pw[:, j * 8:(j + 1) * 8], lhsT=aT[:, kc, j * 128:(j + 1) * 128],
                                                                    rhs=Ww[:, kc, :], start=(kc == 0), stop=(kc == KC - 1)),
                         r=[Ww, aT], w=[pw])
            P.op("dve", lambda e, wsbg=wsbg: e.tensor_scalar(out=wsbg[:], in0=pw[:, 0:G4 * 8], scalar1=IDX_SCALE, scalar2=None,
                                                            op0=ALU.mult), r=[pw], w=[wsbg])
            for j in range(G4):
                m = g * G4 + j
                Dg = Dgs[m % 2]
                for h in range(NH):
                    col = j * 8 + h
                    if h % 2 == 0:
                        P.op("dve", lambda e, h=h, Dg=Dg, wsbg=wsbg, col=col: e.tensor_scalar(
                            out=Dg[:, h, :], in0=ident_f[:], scalar1=wsbg[:, col:col + 1], scalar2=None, op0=ALU.mult),
                            r=[ident_f, wsbg], w=[Dg])
                    else:
                        P.op("act", lambda e, h=h, Dg=Dg, wsbg=wsbg, col=col: e.activation(
                            out=Dg[:, h, :], in_=ident_f[:], func=AF.Copy, scale=wsbg[:, col:col + 1]), r=[ident_f, wsbg], w=[Dg])
                P.dma("pool", lambda e, m=m, j=j, qTg=qTg: e.dma_start(out=q_d[m], in_=qTg[:, j].rearrange("p h t -> p (h t)")),
                      r=[qTg], w=[q_d], key="dr_q_d")
                P.dma("pool", lambda e, m=m, j=j, qiTg=qiTg: e.dma_start(out=qi_d[m], in_=qiTg[:, j].rearrange("p c two t -> p (c two t)")),
                      r=[qiTg], w=[qi_d], key="dr_qi_d")
                P.dma("pool", lambda e, m=m, Dg=Dg: e.dma_start(out=dg_d[m], in_=Dg[:].rearrange("p h t -> p (h t)")),
                      r=[Dg], w=[dg_d], key="dr_dg_d")
        P.barrier()

    with ExitStack() as es:
        cbx = sb(es, "cbx", [128, 512], F32)
        cbm = sb(es, "cbm", [128, 128], F32)
        P.dma("sp", lambda e: e.dma_start(out=cbx[:], in_=cbx_in[:, :]), w=[cbx])
        P.dma("sp", lambda e: e.dma_start(out=cbm[:], in_=cbm_in[:, :]), w=[cbm])
        qTs = [sb(es, "qT%d" % i, [128, NH, 128], BF16) for i in range(2)]
        qiTs = [sb(es, "qiT%d" % i, [128, NH, 128], BF16) for i in range(2)]
        Dgs = [sb(es, "Dg%d" % i, [128, NH, 128], BF16) for i in range(2)]
        score = sb(es, "score", [128, SMAX], F32)
        MBs = [sb(es, "MB%d" % i, [128, SMAX], BF16) for i in range(2)]
        MBas = [Buf("MBact%d" % i, MBs[i].t) for i in range(2)]
        sacc = sb(es, "sacc", [128, 1], F32)
        cnte = sb(es, "cnte", [128, 1], F32)
        NR = 3
        Rsb = [sb(es, "Rsb%d" % i, [128, 512], BF16) for i in range(NR)]
        stile = [sb(es, "stile%d" % i, [128, 512], F32) for i in range(2)]
        score_d = [dscr("score_d%d" % i, [128, SMAX], F32, kind="Internal") for i in range(2)]
        PT = [sb(es, "PT%d" % i, [128, 512], BF16) for i in range(4)]
        NKV = 3
        KTs = [sb(es, "KTs%d" % i, [128, NH * 128], BF16) for i in range(NKV)]
        Vs = [sb(es, "Vs%d" % i, [128, NH * 129], BF16) for i in range(NKV)]
        amax = sb(es, "amax", [128, 1], F32)
        hk = sb(es, "hk", [128, NBIS + 2], F32)
        p2 = sb(es, "p2", [128, NBIS + 2], F32)
        mid = sb(es, "mid", [128, 1], F32)
        cntb = sb(es, "cntb", [128, 1], F32)
        ub = sb(es, "ub", [128, 1], F32)
        tau = sb(es, "tau", [128, 1], F32)
        rden = sb(es, "rden", [128, NH], F32)
        yat = sb(es, "yat", [128, D], BF16)
        yT = sb(es, "yT", [128, D], BF16)
        for k in range(NBIS + 2):
            P.op("dve", lambda e, k=k: e.memset(p2[:, k:k + 1], 2.0 ** (-k)), w=[p2])
        Rps = [ps(es, "Rps%d" % i, [128, 512]) for i in range(2)]
        SCps = ps(es, "SCps", [128, 512])
        Lps = [ps(es, "Lps%d" % i, [128, 512]) for i in range(2)]
        Ops = [ps(es, "Ops%d" % i, [128, 512]) for i in range(3)]
        cnts = {"r": 0, "kv": 0, "l": 0, "st": 0}

        def chunks_of(m):
            return list(range(4 * m + 4)) + [NXC]

        def load_qT(m):
            P.dma("sp", lambda e: e.dma_start(out=qTs[m % 2][:].rearrange("p h t -> p (h t)"), in_=q_d[m]), r=[q_d], w=[qTs[m % 2]])

        def load_qi(m):
            P.dma("sp", lambda e: e.dma_start(out=qiTs[m % 2][:].rearrange("p h t -> p (h t)"), in_=qi_d[m]), r=[qi_d], w=[qiTs[m % 2]])
            P.dma("sp", lambda e: e.dma_start(out=Dgs[m % 2][:].rearrange("p h t -> p (h t)"), in_=dg_d[m]), r=[dg_d], w=[Dgs[m % 2]])

        def stage_I(m):
            qiT, Dg = qiTs[m % 2], Dgs[m % 2]
            par = m % 2
            nxg = m + 1
            groups = [(g * 512, 512, g * 512) for g in range(nxg)] + [(NXC * 128, 128, nxg * 512)]
            for gi, (k0, N, s0) in enumerate(groups):
                pend = None
                for h in range(NH):
                    rp = Rps[cnts["r"] % 2]
                    rs = Rsb[cnts["r"] % NR]
                    cnts["r"] += 1
                    P.op("pe", lambda e, rp=rp, h=h, k0=k0, N=N: e.matmul(
                        rp[:, 0:N], lhsT=qiT[:, h, :], rhs=kiT[:, k0:k0 + N], start=True, stop=True),
                        r=[qiT, kiT], w=[rp])
                    P.op("act", lambda e, rp=rp, rs=rs, N=N: e.activation(out=rs[:, 0:N], in_=rp[:, 0:N], func=AF.Relu),
                         r=[rp], w=[rs])
                    if pend is not None:
                        ph_, prs = pend
                        P.op("pe", lambda e, prs=prs, ph_=ph_, N=N: e.matmul(SCps[:, 0:N], lhsT=Dg[:, ph_, :], rhs=prs[:, 0:N],
                                                                           start=(ph_ == 0), stop=False), r=[Dg, prs], w=[SCps])
                    pend = (h, rs)
                    yield
                ph_, prs = pend
                P.op("pe", lambda e, prs=prs, ph_=ph_, N=N: e.matmul(SCps[:, 0:N], lhsT=Dg[:, ph_, :], rhs=prs[:, 0:N],
                                                                   start=False, stop=True), r=[Dg, prs], w=[SCps])
                st = stile[cnts["st"] % 2]
                cnts["st"] += 1
                P.op("act", lambda e, st=st, N=N: e.activation(out=st[:, 0:N], in_=SCps[:, 0:N], func=AF.Copy), r=[SCps], w=[st])
                P.dma("pool", lambda e, st=st, s0=s0, N=N, par=par: e.dma_start(out=score_d[par][:, s0:s0 + N], in_=st[:, 0:N]),
                      r=[st], w=[score_d[par]], key=score_d[par].name)
                yield

        scq = [Buf("scq%d" % i, score.t) for i in range(4)]
        amax4 = sb(es, "amax4", [128, 4], F32)
        hk2 = sb(es, "hk2", [128, NBIS + 2], F32)
        thr0 = sb(es, "thr0", [128, 1], F32)

        def quarters(m):
            S = len(chunks_of(m)) * 128
            return [(i * S // 4 // 128) * 128 for i in range(4)] + [S]

        def reload(m):
            par = m % 2
            q4 = quarters(m)
            for i in range(4):
                P.dma("sp", lambda e, i=i: e.dma_start(out=score[:, q4[i]:q4[i + 1]], in_=score_d[par][:, q4[i]:q4[i + 1]]),
                      r=[score_d[par]], w=[scq[i]], key="scq%d" % i)

        def stage_B(m):
            S = len(chunks_of(m)) * 128
            MB, MBa = MBs[m % 2], MBas[m % 2]
            n_act = min(ACT_COLS, (S // 3 // 128) * 128)
            c0 = S - n_act
            nxg = m + 1
            q4 = quarters(m)
            for i in range(4):
                P.op("dve", lambda e, i=i: e.tensor_reduce(out=amax4[:, i:i + 1], in_=score[:, q4[i]:q4[i + 1]], axis=AX.X, op=ALU.max,
                                                           apply_absolute_value=True), r=[scq[i]], w=[amax4])
            P.op("dve", lambda e: e.tensor_reduce(out=amax[:], in_=amax4[:], axis=AX.X, op=ALU.max), r=[amax4], w=[amax])
            sx = (nxg - 1) * 512
            P.op("dve", lambda e, sx=sx: e.tensor_tensor(out=score[:, sx:sx + 512], in0=score[:, sx:sx + 512], in1=cbx[:],
                                                         op=ALU.add), r=scq + [cbx], w=scq)
            sm = nxg * 512
            P.op("dve", lambda e, sm=sm: e.tensor_tensor(out=score[:, sm:sm + 128], in0=score[:, sm:sm + 128], in1=cbm[:],
                                                         op=ALU.add), r=scq + [cbm], w=scq)
            P.op("dve", lambda e: e.tensor_scalar(out=amax[:], in0=amax[:], scalar1=1.001, scalar2=1e-3, op0=ALU.mult,
                                                  op1=ALU.add), r=[amax], w=[amax])
            P.op("dve", lambda e: e.tensor_scalar(out=hk[:], in0=p2[:], scalar1=amax[:, 0:1], scalar2=None, op0=ALU.mult),
                 r=[p2, amax], w=[hk])
            P.op("dve", lambda e: e.tensor_copy(out=mid[:], in_=hk[:, NBIS + 1:NBIS + 2]), r=[hk], w=[mid])
            P.op("dve", lambda e: e.tensor_scalar(out=hk2[:], in0=hk[:], scalar1=2.0, scalar2=None, op0=ALU.mult), r=[hk], w=[hk2])
            P.op("dve", lambda e, n_act=n_act: e.memset(thr0[:], TOPK - 0.5 - 0.5 * n_act), w=[thr0])
            if n_act == 0:
                P.op("dve", lambda e: e.memset(cnte[:], TOPK - 0.5), w=[cnte])
            yield
            for k in range(NBIS):
                if n_act > 0:
                    P.op("act", lambda e, S=S, c0=c0: e.activation(out=MBa[:, c0:S], in_=score[:, c0:S], func=AF.Sign, scale=-1.0,
                                                                   bias=mid[:, 0:1], accum_out=sacc[:, 0:1]),
                         r=scq + [mid], w=[MBa, sacc])
                    P.op("act", lambda e, n_act=n_act: e.activation(out=cnte[:], in_=sacc[:], func=AF.Identity, scale=0.5,
                                                                    bias=thr0[:, 0:1]), r=[sacc, thr0], w=[cnte])
                P.op("dve", lambda e, c0=c0: e.tensor_scalar(out=MB[:, 0:c0], in0=score[:, 0:c0], scalar1=mid[:, 0:1], scalar2=None,
                                                             op0=ALU.is_ge, op1=ALU.add, accum_out=cntb[:, 0:1]),
                     r=scq + [mid], w=[MB, cntb])
                if k < NBIS - 1:
                    P.op("dve", lambda e, k=k: e.tensor_scalar(out=ub[:], in0=cntb[:], scalar1=cnte[:, 0:1], scalar2=hk2[:, k + 1:k + 2],
                                                               op0=ALU.is_ge, op1=ALU.mult), r=[cntb, cnte, hk2], w=[ub])
                    P.op("dve", lambda e, k=k: e.scalar_tensor_tensor(out=mid[:], in0=ub[:], scalar=hk[:, k + 1:k + 2], in1=mid[:],
                                                                      op0=ALU.subtract, op1=ALU.add), r=[ub, hk, mid], w=[mid])
                else:
                    P.op("dve", lambda e, k=k: e.tensor_scalar(out=ub[:], in0=cntb[:], scalar1=cnte[:, 0:1], scalar2=hk[:, k:k + 1],
                                                               op0=ALU.is_ge, op1=ALU.mult), r=[cntb, cnte, hk], w=[ub])
                    P.op("dve", lambda e, k=k: e.scalar_tensor_tensor(out=tau[:], in0=ub[:], scalar=hk[:, k:k + 1], in1=mid[:],
                                                                      op0=ALU.subtract, op1=ALU.add), r=[ub, hk, mid], w=[tau])
                    P.op("dve", lambda e, S=S: e.tensor_scalar(out=MB[:, 0:S], in0=score[:, 0:S], scalar1=tau[:, 0:1], scalar2=NEG,
                                                               op0=ALU.is_lt, op1=ALU.mult), r=scq + [tau], w=[MB, MBa])
                yield

        def stage_A(m):
            qT, MB, MBa = qTs[m % 2], MBs[m % 2], MBas[m % 2]
            chunks = chunks_of(m)
            nchunks = len(chunks)
            prev = None

            def emit_pv(ci, pts, vv, half):
                vv3 = vv[:].rearrange("p (h d) -> p h d", d=129)
                pt = pts[half]
                for hh in range(4):
                    h = half * 4 + hh
                    ob = Ops[h // 3]
                    o0 = (h % 3) * 129
                    P.op("pe", lambda e, ob=ob, o0=o0, pt=pt, hh=hh, h=h, vv3=vv3, ci=ci: e.matmul(
                        ob[:, o0:o0 + 129], lhsT=pt[:, hh * 128:(hh + 1) * 128], rhs=vv3[:, h, :],
                        start=(ci == 0 and h % 3 == 0), stop=(ci == nchunks - 1), skip_group_check=True),
                        r=[pt, vv], w=[ob])

            for ci, j in enumerate(chunks):
                kt = KTs[cnts["kv"] % NKV]
                vv = Vs[cnts["kv"] % NKV]
                cnts["kv"] += 1
                P.dma("sp", lambda e, kt=kt, j=j: e.dma_start(out=kt[:], in_=KT_d[j]), r=[KT_d], w=[kt])
                P.dma("sp", lambda e, vv=vv, j=j: e.dma_start(out=vv[:], in_=V_d[j]), r=[V_d], w=[vv])
                pts = []
                for half in range(2):
                    lp = Lps[half]
                    pt = PT[(cnts["l"] % 2) * 2 + half]
                    pts.append(pt)
                    P.op("pe", lambda e, lp=lp, ci=ci: e.matmul(
                        lp[:, :], lhsT=MB[:, ci * 128:(ci + 1) * 128], rhs=ident4[:].rearrange("p a t -> p (a t)"),
                        start=True, stop=False, skip_group_check=True), r=[MB, MBa, ident4], w=[lp])
                    for hh in range(4):
                        h = half * 4 + hh
                        P.op("pe", lambda e, lp=lp, hh=hh, h=h, kt=kt: e.matmul(
                            lp[:, hh * 128:(hh + 1) * 128], lhsT=kt[:, h * 128:(h + 1) * 128], rhs=qT[:, h, :],
                            start=False, stop=(hh == 3), skip_group_check=True), r=[kt, qT], w=[lp])
                    P.op("act", lambda e, lp=lp, pt=pt: e.activation(out=pt[:], in_=lp[:], func=AF.Exp, scale=SM_SCALE),
                         r=[lp], w=[pt])
                    yield
                    if prev is not None:
                        emit_pv(prev[0], prev[1], prev[2], half)
                        yield
                cnts["l"] += 1
                prev = (ci, pts, vv)
            emit_pv(prev[0], prev[1], prev[2], 0)
            emit_pv(prev[0], prev[1], prev[2], 1)
            yield

        def stage_N(m):
            for b3 in range(3):
                nh3 = 3 if b3 < 2 else 2
                ov = Ops[b3][:, 0:nh3 * 129].rearrange("p (h d) -> p h d", d=129)
                P.op("dve", lambda e, ov=ov, b3=b3, nh3=nh3: e.reciprocal(out=rden[:, b3 * 3:b3 * 3 + nh3], in_=ov[:, :, 128]),
                     r=[Ops[b3]], w=[rden])
            for h in range(NH):
                ob = Ops[h // 3]
                o0 = (h % 3) * 129
                P.op("act", lambda e, ob=ob, o0=o0, h=h: e.activation(out=yat[:, h * 128:(h + 1) * 128], in_=ob[:, o0:o0 + 128],
                                                                    func=AF.Copy, scale=rden[:, h:h + 1]), r=[ob, rden], w=[yat])
            yT3 = yT[:].rearrange("p (c t) -> p c t", t=128)
            dfn = lambda hf: yT3[:, hf * 4:(hf + 1) * 4, :]
            dfn.buf = yT
            pe_transpose(yat, 128, Lps, dfn)
            P.dma("pool", lambda e, m=m: e.dma_start(out=ya_d[m], in_=yT[:]), r=[yT], w=[ya_d], key="dr_ya_d")

        def n_I(m):
            return (m + 2) * (NH + 1)

        def n_A(m):
            return 4 * len(chunks_of(m)) - 1

        def drive(gI, nI, gA, nA, gB, nB):
            accA = accB = 0.0
            hold = min(B_HOLD, nI // 3)
            if gB is not None:
                next(gB, None)
            holdA = min(A_HOLD, nI // 3)
            for it, _ in enumerate(gI):
                if it >= holdA:
                    accA += nA / float(nI - holdA)
                if it >= hold:
                    accB += nB / float(nI - hold)
                while accA >= 1.0:
                    next(gA, None)
                    accA -= 1.0
                while gB is not None and accB >= 1.0:
                    next(gB, None)
                    accB -= 1.0
            if gB is not None:
                for _ in gB:
                    pass
            for _ in gA:
                pass

        load_qi(0)
        if NSLOT > 1:
            load_qi(1)
        load_qT(0)
        for _ in stage_I(0):
            pass
        if NSLOT > 1:
            for _ in stage_I(1):
                pass
        reload(0)
        for _ in stage_B(0):
            pass
        for m in range(NSLOT):
            gA = stage_A(m)
            for _ in range(5):
                next(gA, None)
            if m + 2 < NSLOT:
                load_qi(m + 2)
            if m + 1 < NSLOT:
                load_qT(m + 1)
            if m + 1 < NSLOT:
                reload(m + 1)
            gB = stage_B(m + 1) if m + 1 < NSLOT else None
            if m + 2 < NSLOT:
                drive(stage_I(m + 2), n_I(m + 2), gA, n_A(m), gB, NBIS)
            elif gB is not None:
                per = -(-n_A(m) // NBIS)
                for _ in gB:
                    for _ in range(per):
                        next(gA, None)
                for _ in gA:
                    pass
            else:
                for _ in gA:
                    pass
            stage_N(m)
        P.barrier()
    es_ki.close()

    with ExitStack() as es:
        NT = TT * 128
        WSLOT = 6
        wslot = [sb(es, "wsl%d" % i, [128, 4096], BF16) for i in range(WSLOT)]
        wcnt = [0]

        def wload(Wb, K_c, c0, ncols):
            s = wslot[wcnt[0] % WSLOT]
            wcnt[0] += 1
            view = s[:, 0:K_c * ncols].rearrange("p (k n) -> p k n", n=ncols)
            P.dma("sp", lambda e: e.dma_start(out=view, in_=Wb[:, :, c0:c0 + ncols]), r=[Wb], w=[s])
            return s, view

        gfin = sb(es, "gfin", [128, D], F32)
        cw = sb(es, "cw", [128, KC * 3], F32)
        P.dma("sp", lambda e: e.dma_start(out=gfin[:], in_=gfin_in[:, :]), w=[gfin])
        P.dma("sp", lambda e: e.dma_start(out=cw[:], in_=cw_in[:, :]), w=[cw])
        pg = [ps(es, "pg%d" % i, [128, 512]) for i in range(2)]
        rbc = rms_bufs(es, "c", pg)
        hx = [sb(es, "hx%d" % i, [128, D], F32) for i in range(TT)]
        ss_c = [sb(es, "ssc%d" % i, [128, 1], F32) for i in range(TT)]
        rstd_c = [sb(es, "rstdc%d" % i, [128, 1], F32) for i in range(TT)]
        abf_c = [rbc[4], sb(es, "abfc2", [128, D], BF16)]
        aT = sb(es, "aTc", [128, KC, NT], BF16)
        aTh = sb(es, "aTh", [128, KC, 2 * TT], BF16)
        yaT = sb(es, "yaT", [128, KC, NT], BF16)
        uT = sb(es, "uT", [128, KC, TT, 130], F32)
        cuT = sb(es, "cuT", [128, KC, TT, 130], BF16)
        acc = sb(es, "acc", [128, KC, TT, 128], F32)
        ycv = sb(es, "ycv", [128, KC, NT], BF16)
        sg = sb(es, "sg", [128, KC, NT], BF16)
        mixed = acc
        mixb = sb(es, "mixb", [128, KC, NT], BF16)
        fT = sb(es, "fT", [128, KC, NT], BF16)
        actT = sb(es, "actT", [128, FC, NT], BF16)
        sil = [sb(es, "sil%d" % i, [128, NT], F32) for i in range(2)]
        ss2 = sb(es, "ss2", [128, 1], F32)
        rs2 = sb(es, "rs2", [128, 1], F32)
        fbf = sb(es, "fbf", [128, D], BF16)
        sqj = sb(es, "sqj2", [128, D], BF16)
        ot = [sb(es, "ot%d" % i, [128, D], F32) for i in range(2)]
        NPA = 6
        pa = [ps(es, "pa%d" % i, [128, 512]) for i in range(NPA)]
        pac = [0]

        def nxt():
            b = pa[pac[0] % NPA]
            pac[0] += 1
            return b

        def proj_fm(Wb, c_base, consume):
            for hb in range(2):
                s, wv = wload(Wb, KC, c_base + hb * 512, 512)
                for c4 in range(4):
                    c = hb * 4 + c4
                    pm = nxt()
                    for kc in range(KC):
                        P.op("pe", lambda e, pm=pm, wv=wv, c4=c4, kc=kc: e.matmul(
                            pm[:, 0:NT], lhsT=wv[:, kc, c4 * 128:(c4 + 1) * 128], rhs=aT[:, kc, :], start=(kc == 0), stop=(kc == KC - 1)),
                            r=[s, aT], w=[pm])
                    consume(c, pm, s, wv, c4)

        for sp_i in range(NSUP):
            m0 = sp_i * TT
            for tt in range(TT):
                bufs_tt = (hx[tt], rbc[1], ss_c[tt], rstd_c[tt], abf_c[tt % 2], pg)
                rms_pre(bufs_tt, x_own[m0 + tt], 128)
                if tt >= 1:
                    bufs_p = (hx[tt - 1], rbc[1], ss_c[tt - 1], rstd_c[tt - 1], abf_c[(tt - 1) % 2], pg)
                    rms_tr(bufs_p, 128, aT, (tt - 1) * 128)
            bufs_p = (hx[TT - 1], rbc[1], ss_c[TT - 1], rstd_c[TT - 1], abf_c[(TT - 1) % 2], pg)
            rms_tr(bufs_p, 128, aT, (TT - 1) * 128)
            rms_transpose(rbc, x_halo[2 * m0:2 * m0 + 2 * TT, :], 2 * TT, aTh, 0)
            P.dma("sp", multi(TT, lambda e, m0=m0: [
                e.dma_start(out=yaT[:, :, tt * 128:(tt + 1) * 128], in_=ya_d[m0 + tt].rearrange("p (c t) -> p c t", t=128))
                for tt in range(TT)]), r=[ya_d], w=[yaT])

            def halo_mm(pm, s, wv, c4):
                for kc in range(KC):
                    P.op("pe", lambda e, pm=pm, wv=wv, c4=c4, kc=kc: e.matmul(
                        pm[:, 0:2 * TT], lhsT=wv[:, kc, c4 * 128:(c4 + 1) * 128], rhs=aTh[:, kc, :], start=(kc == 0),
                        stop=(kc == KC - 1)), r=[s, aTh], w=[pm])

            def c_cu(c, pm, s, wv, c4):
                P.op("act", lambda e, c=c, pm=pm: e.activation(out=cuT[:, c, :, 2:130], in_=pm[:, 0:NT].rearrange("p (b t) -> p b t", t=128),
                                                            func=AF.Copy), r=[pm], w=[cuT])
                p2_ = nxt()
                halo_mm(p2_, s, wv, c4)
                P.op("act", lambda e, c=c, p2_=p2_: e.activation(out=cuT[:, c, :, 0:2], in_=p2_[:, 0:2 * TT].rearrange("p (b t) -> p b t", t=2),
                                                             func=AF.Copy), r=[p2_], w=[cuT])

            proj_fm(Wb_in2, C_CU, c_cu)

            def c_cc(c, pm, s, wv, c4):
                P.op("dve", lambda e, c=c, pm=pm: e.tensor_tensor(out=uT[:, c, :, 2:130], in0=pm[:, 0:NT].rearrange("p (b t) -> p b t", t=128),
                                                               in1=cuT[:, c, :, 2:130], op=ALU.mult), r=[pm, cuT], w=[uT])
                p2_ = nxt()
                halo_mm(p2_, s, wv, c4)
                P.op("dve", lambda e, c=c, p2_=p2_: e.tensor_tensor(out=uT[:, c, :, 0:2], in0=p2_[:, 0:2 * TT].rearrange("p (b t) -> p b t", t=2),
                                                                in1=cuT[:, c, :, 0:2], op=ALU.mult), r=[p2_, cuT], w=[uT])
                P.op("act", lambda e, c=c: e.activation(out=acc[:, c], in_=uT[:, c, :, 0:128], func=AF.Copy,
                                                       scale=cw[:, c * 3:c * 3 + 1]), r=[uT, cw], w=[acc])
                P.op("dve", lambda e, c=c: e.scalar_tensor_tensor(out=acc[:, c], in0=uT[:, c, :, 1:129], scalar=cw[:, c * 3 + 1:c * 3 + 2],
                                                                 in1=acc[:, c], op0=ALU.mult, op1=ALU.add), r=[uT, cw, acc], w=[acc])
                P.op("dve", lambda e, c=c: e.scalar_tensor_tensor(out=acc[:, c], in0=uT[:, c, :, 2:130], scalar=cw[:, c * 3 + 2:c * 3 + 3],
                                                                 in1=acc[:, c], op0=ALU.mult, op1=ALU.add), r=[uT, cw, acc], w=[acc])

            proj_fm(Wb_in2, C_CC, c_cc)

            def c_cb(c, pm, s, wv, c4):
                P.op("dve", lambda e, c=c, pm=pm: e.tensor_tensor(out=ycv[:, c, :].rearrange("p (b t) -> p b t", t=128),
                                                               in0=pm[:, 0:NT].rearrange("p (b t) -> p b t", t=128), in1=acc[:, c],
                                                               op=ALU.mult), r=[pm, acc], w=[ycv])

            proj_fm(Wb_in2, C_CB, c_cb)

            def c_gate(c, pm, s, wv, c4):
                P.op("act", lambda e, c=c, pm=pm: e.activation(out=sg[:, c, :], in_=pm[:, 0:NT], func=AF.Sigmoid), r=[pm], w=[sg])

            def branch(Wb, src, first):
                for c in range(KC):
                    if c % 4 == 0:
                        s, wv = wload(Wb, KC, c * 128, 512)
                    c4 = c % 4
                    pm = nxt()
                    for kc in range(KC):
                        P.op("pe", lambda e, pm=pm, wv=wv, c4=c4, kc=kc, s=s: e.matmul(
                            pm[:, 0:NT], lhsT=wv[:, kc, c4 * 128:(c4 + 1) * 128], rhs=src[:, kc, :], start=(kc == 0), stop=(kc == KC - 1)),
                            r=[s, src], w=[pm])
                    if first:
                        P.op("dve", lambda e, c=c, pm=pm: e.tensor_tensor(out=mixed[:, c].rearrange("p b t -> p (b t)"), in0=pm[:, 0:NT], in1=sg[:, c, :], op=ALU.mult),
                             r=[pm, sg], w=[mixed])
                    else:
                        P.op("dve", lambda e, c=c, pm=pm: e.tensor_tensor(out=sil[0][:], in0=pm[:, 0:NT], in1=sg[:, c, :], op=ALU.mult),
                             r=[pm, sg], w=[sil[0]])
                        P.op("dve", lambda e, c=c: e.tensor_tensor(out=mixb[:, c, :], in0=sil[0][:], in1=mixed[:, c].rearrange("p b t -> p (b t)"), op=ALU.add),
                             r=[sil[0], mixed], w=[mixb])

            proj_fm(Wb_in2, C_GA, c_gate)
            branch(Wb_ao, yaT, True)
            proj_fm(Wb_in2, C_GB, c_gate)
            branch(Wb_co, ycv, False)

            for half in range(2):
                s_, wv = wload(Wb_o, KC, half * 512, 512)
                for tt in range(TT):
                    ph = nxt()
                    for kc in range(KC):
                        P.op("pe", lambda e, ph=ph, wv=wv, tt=tt, kc=kc: e.matmul(
                            ph[:, :], lhsT=mixb[:, kc, tt * 128:(tt + 1) * 128], rhs=wv[:, kc, :],
                            start=(kc == 0), stop=(kc == KC - 1)), r=[s_, mixb], w=[ph])
                    P.op("dve", lambda e, ph=ph, tt=tt, half=half: e.tensor_tensor(out=hx[tt][:, half * 512:(half + 1) * 512], in0=ph[:, :],
                                                                                  in1=hx[tt][:, half * 512:(half + 1) * 512], op=ALU.add),
                         r=[ph, hx[tt]], w=[hx[tt]])
            for tt in range(TT):
                h_ = hx[tt]
                P.op("act", lambda e, h_=h_: e.activation(out=sqj[:], in_=h_[:], func=AF.Square, accum_out=ss2[:, 0:1]), r=[h_], w=[sqj, ss2])
                P.op("act", lambda e: e.activation(out=rs2[:], in_=ss2[:], func=AF.Sqrt, scale=1.0 / D, bias=epsb[:, 0:1]), r=[ss2, epsb], w=[rs2])
                P.op("dve", lambda e: e.reciprocal(out=rs2[:], in_=rs2[:]), r=[rs2], w=[rs2])
                P.op("dve", lambda e, h_=h_: e.tensor_scalar(out=fbf[:], in0=h_[:], scalar1=rs2[:, 0:1], scalar2=None, op0=ALU.mult),
                     r=[h_, rs2], w=[fbf])
                dfn = lambda hf, tt=tt: fT[:, hf * 4:(hf + 1) * 4, tt * 128:(tt + 1) * 128]
                dfn.buf = fT
                pe_transpose(fbf, 128, pg, dfn)
            for f0 in range(0, FC, 4):
                nf = min(4, FC - f0)
                sg_, wg = wload(Wb_g, KC, f0 * 128, nf * 128)
                su_, wu = wload(Wb_u, KC, f0 * 128, nf * 128)
                for fc in range(nf):
                    for kc in range(KC):
                        P.op("pe", lambda e, wg=wg, fc=fc, kc=kc: e.matmul(pg[0][:, 0:NT], lhsT=wg[:, kc, fc * 128:(fc + 1) * 128],
                                                                         rhs=fT[:, kc, :], start=(kc == 0), stop=(kc == KC - 1)),
                             r=[sg_, fT], w=[pg[0]])
                    for kc in range(KC):
                        P.op("pe", lambda e, wu=wu, fc=fc, kc=kc: e.matmul(pg[1][:, 0:NT], lhsT=wu[:, kc, fc * 128:(fc + 1) * 128],
                                                                         rhs=fT[:, kc, :], start=(kc == 0), stop=(kc == KC - 1)),
                             r=[su_, fT], w=[pg[1]])
                    sl = sil[(f0 + fc) % 2]
                    P.op("act", lambda e, sl=sl: e.activation(out=sl[:], in_=pg[0][:, 0:NT], func=AF.Silu), r=[pg[0]], w=[sl])
                    P.op("dve", lambda e, sl=sl, f=f0 + fc: e.tensor_tensor(out=actT[:, f, :], in0=pg[1][:, 0:NT], in1=sl[:], op=ALU.mult),
                         r=[pg[1], sl], w=[actT])
            FG = [(0, 8), (8, 16), (16, FC)]
            for half in range(2):
                slots = []
                for (f0, f1) in FG:
                    s_ = wslot[wcnt[0] % WSLOT]
                    wcnt[0] += 1
                    nfc = f1 - f0
                    P.dma("sp", lambda e, s_=s_, half=half, f0=f0, f1=f1, nfc=nfc: e.dma_start(
                        out=s_[:, 0:nfc * 512], in_=Wb_d[:, half, f0:f1, :].rearrange("p f c -> p (f c)")), r=[Wb_d], w=[s_])
                    slots.append((s_, s_[:, 0:nfc * 512].rearrange("p (k n) -> p k n", n=512), f0, f1))
                for tt in range(TT):
                    pq = nxt()
                    for (s_, wv, f0, f1) in slots:
                        for fc in range(f0, f1):
                            P.op("pe", lambda e, pq=pq, wv=wv, tt=tt, fc=fc, f0=f0: e.matmul(
                                pq[:, :], lhsT=actT[:, fc, tt * 128:(tt + 1) * 128], rhs=wv[:, fc - f0, :],
                                start=(fc == 0), stop=(fc == FC - 1)), r=[s_, actT], w=[pq])
                    P.op("dve", lambda e, pq=pq, tt=tt, half=half: e.tensor_tensor(out=hx[tt][:, half * 512:(half + 1) * 512], in0=pq[:, :],
                                                                                  in1=hx[tt][:, half * 512:(half + 1) * 512], op=ALU.add),
                         r=[pq, hx[tt]], w=[hx[tt]])
            for tt in range(TT):
                h_ = hx[tt]
                o_ = ot[tt % 2]
                P.op("act", lambda e, h_=h_: e.activation(out=sqj[:], in_=h_[:], func=AF.Square, accum_out=ss2[:, 0:1]), r=[h_], w=[sqj, ss2])
                P.op("act", lambda e: e.activation(out=rs2[:], in_=ss2[:], func=AF.Sqrt, scale=1.0 / D, bias=epsb[:, 0:1]), r=[ss2, epsb], w=[rs2])
                P.op("dve", lambda e: e.reciprocal(out=rs2[:], in_=rs2[:]), r=[rs2], w=[rs2])
                P.op("dve", lambda e, h_=h_, o_=o_: e.scalar_tensor_tensor(out=o_[:], in0=h_[:], scalar=rs2[:, 0:1], in1=gfin[:],
                                                                        op0=ALU.mult, op1=ALU.mult), r=[h_, rs2, gfin], w=[o_])
                P.dma("pool", lambda e, o_=o_, mm=m0 + tt: e.dma_start(out=out[mm], in_=o_[:]), r=[o_], w=[], key="out")
        P.barrier()

    P.lower(top)
    top.close()
    return nc


_CACHE = {}


def _prep_inputs(x, meta_tokens, norm_mix_g, w_in, w_attn_out, conv_w, w_conv_out, w_out, norm_ffn_g, w_gate, w_up,
                 w_down, norm_final_g):
    f = np.float32
    x = np.asarray(x, f)
    B, SEQ, _ = x.shape
    meta = np.asarray(meta_tokens, f)
    NXC = SEQ // 128
    NSLOT = NXC // 4
    common = {
        "w_in": np.ascontiguousarray(np.asarray(w_in, f)[0]),
        "w_attn_out": np.ascontiguousarray(np.asarray(w_attn_out, f)[0]),
        "w_conv_out": np.ascontiguousarray(np.asarray(w_conv_out, f)[0]),
        "w_out": np.ascontiguousarray(np.asarray(w_out, f)[0]),
        "w_gate": np.ascontiguousarray(np.asarray(w_gate, f)[0]),
        "w_up": np.ascontiguousarray(np.asarray(w_up, f)[0]),
        "w_down": np.ascontiguousarray(np.asarray(w_down, f)[0]),
        "g_mix": np.ascontiguousarray(np.asarray(norm_mix_g, f)[0].reshape(KC, 128).T),
        "g_ffn": np.ascontiguousarray(np.asarray(norm_ffn_g, f)[0].reshape(KC, 128).T),
        "g_fin": np.ascontiguousarray(np.broadcast_to(np.asarray(norm_final_g, f)[None, :], (128, D))),
        "conv_w": np.ascontiguousarray(np.asarray(conv_w, f)[0].reshape(3, KC, 128).transpose(2, 1, 0).reshape(128, KC * 3)),
        "ident": np.eye(128, dtype=f),
    }
    cbm = np.zeros((128, 128), f)
    cbm[:, 16:] = -1e30
    in_maps = []
    for core in range(8):
        b, r = core // 4, core % 4
        x_all = np.zeros(((NXC + 1) * 128, D), f)
        x_all[:SEQ] = x[b]
        x_all[SEQ:SEQ + 16] = meta
        qbs = [4 * m + r for m in range(NSLOT)]
        x_own = np.stack([x[b, qb * 128:(qb + 1) * 128] for qb in qbs])
        halo = np.zeros((NSLOT * 2, D), f)
        for m, qb in enumerate(qbs):
            if qb == 0:
                halo[2 * m:2 * m + 2] = meta[14:16]
            else:
                halo[2 * m:2 * m + 2] = x[b, qb * 128 - 2:qb * 128]
        t = np.arange(128)[:, None]
        c = np.arange(512)[None, :]
        cbx = np.where(c <= r * 128 + t, 0.0, -1e30).astype(f)
        d = dict(common)
        d.update({"x_all": x_all, "x_own": np.ascontiguousarray(x_own), "x_halo": halo, "cbx": cbx, "cbm": cbm})
        in_maps.append(d)
    return in_maps, B, SEQ, NSLOT


def kernel(x, meta_tokens, norm_mix_g, w_in, w_attn_out, conv_w, w_conv_out, w_out, norm_ffn_g, w_gate, w_up, w_down,
           norm_final_g, _dbg=False):
    in_maps, B, SEQ, NSLOT = _prep_inputs(x, meta_tokens, norm_mix_g, w_in, w_attn_out, conv_w, w_conv_out, w_out,
                                          norm_ffn_g, w_gate, w_up, w_down, norm_final_g)
    key = (SEQ, _dbg)
    if key not in _CACHE:
        _CACHE[key] = build(SEQ, _dbg)
    nc = _CACHE[key]
    res = run_bass_kernel_spmd(nc, in_maps, core_ids=list(range(8)))
    outp = np.zeros((B, SEQ, D), np.float32)
    for core in range(8):
        b, r = core // 4, core % 4
        o = np.asarray(res.results[core]["out"], np.float32)
        for m in range(NSLOT):
            qb = 4 * m + r
            outp[b, qb * 128:(qb + 1) * 128] = o[m]
    if _dbg:
        return outp, res.results
    return outp
```
